# Optimizing a Trainium2 kernel written in Bass

```python
import jax, jax.numpy as jnp
from jax import lax
import numpy as np

D_MODEL = 1024
BATCH = 8
SEQ = 4096
DEPTH = 2

F32 = jnp.float32
EPS = 1e-6
MEM_LEN = 256
N_BRANCHES = 3
D_FF = 2816

RW_HEADS = 8
RW_HEAD_DIM = 64
RW_WIDTH = RW_HEADS * RW_HEAD_DIM
RW_DECAY_LORA = 64
RW_AAA_LORA = 64
RW_GATE_LORA = 160
RW_GN_EPS = 64e-5
RW_COLS = (RW_WIDTH, RW_WIDTH, RW_WIDTH, RW_DECAY_LORA, RW_AAA_LORA, RW_GATE_LORA)
RW_IN_COLS = sum(RW_COLS)

DIL_PATTERNS = ((128, 1), (512, 4), (2048, 16))
N_DIL = len(DIL_PATTERNS)
DIL_HEADS = 8
HEAD_DIM = 64
DIL_WIDTH = DIL_HEADS * HEAD_DIM
DIL_IN_COLS = 3 * N_DIL * DIL_WIDTH
ROPE_THETA = 500000.0
ROPE_DIM = HEAD_DIM // 4

RET_HEADS = 4
RET_QK_DIM = 64
RET_V_DIM = 128
RET_CHUNK = 128
RET_ROPE_BASE = 10000.0
RET_COLS = (RET_HEADS * RET_QK_DIM, RET_HEADS * RET_QK_DIM, RET_HEADS * RET_V_DIM, RET_HEADS * RET_V_DIM)
RET_IN_COLS = sum(RET_COLS)
RET_WIDTH = RET_HEADS * RET_V_DIM

GATE_IN_COLS = N_BRANCHES * D_MODEL
IN_COLS = RW_IN_COLS + DIL_IN_COLS + RET_IN_COLS + GATE_IN_COLS

XA_HEADS = 4
XA_HEAD_DIM = D_MODEL // XA_HEADS

kernel_name = 'hybrid_rwkv7_dilated_retention_block'


def split_cols(x, sizes):
    idx = [int(i) for i in np.cumsum(sizes)[:-1]]
    return jnp.split(x, idx, axis=-1)


def rms_norm(x, g, eps=EPS):
    xf = x.astype(F32)
    y = xf * lax.rsqrt(jnp.mean(xf * xf, axis=-1, keepdims=True) + eps)
    return (y * g.astype(F32)).astype(x.dtype)


def apply_rope(x, pos, rot_dim, base):
    half = rot_dim // 2
    inv_freq = base ** (-jnp.arange(half, dtype=F32) / half)
    ang = pos.astype(F32)[:, None] * inv_freq[None, :]
    cos = jnp.cos(ang)[None, :, None, :]
    sin = jnp.sin(ang)[None, :, None, :]
    xf = x.astype(F32)
    x1, x2, rest = xf[..., :half], xf[..., half:rot_dim], xf[..., rot_dim:]
    out = jnp.concatenate([x1 * cos - x2 * sin, x2 * cos + x1 * sin, rest], axis=-1)
    return out.astype(x.dtype)


def swiglu(x, w13, w2):
    a, b = jnp.split(x @ w13, 2, axis=-1)
    return (jax.nn.silu(a) * b) @ w2


def rwkv7_recurrence(r, decay, k, v, a, b):
    B, T, H, N = r.shape

    def step(S, inp):
        r_t, w_t, k_t, v_t, a_t, b_t = inp
        sa = jnp.einsum('bhvk,bhk->bhv', S, a_t)
        S = S * w_t[:, :, None, :] + sa[..., None] * b_t[:, :, None, :] + v_t[..., None] * k_t[:, :, None, :]
        return S, jnp.einsum('bhvk,bhk->bhv', S, r_t)

    xs = tuple(jnp.moveaxis(t, 1, 0) for t in (r, decay, k, v, a, b))
    _, y = lax.scan(step, jnp.zeros((B, H, N, N), F32), xs)
    return jnp.moveaxis(y, 0, 1)


def rwkv7_time_mix(p, mu, w0, w2, a0, a2, g2, k_k, k_a, r_k, ln_w, ln_b):
    B, T, _ = p.shape
    shifted = jnp.pad(p, ((0, 0), (1, 0), (0, 0)))[:, :-1]
    p = p + (shifted - p) * mu
    r, k, v, wd, ad, gd = split_cols(p, RW_COLS)
    w_log = -jax.nn.softplus(-(w0 + jnp.tanh(wd) @ w2).astype(F32)) - 0.5
    decay = jnp.exp(-jnp.exp(w_log))
    a = jax.nn.sigmoid((a0 + ad @ a2).astype(F32))
    g = jax.nn.sigmoid(gd) @ g2
    heads = lambda t: t.astype(F32).reshape(B, T, RW_HEADS, RW_HEAD_DIM)
    kk = heads(k * k_k)
    kk = kk / jnp.maximum(jnp.sqrt(jnp.sum(kk * kk, axis=-1, keepdims=True)), 1e-12)
    k = k.astype(F32) * (1.0 + (a - 1.0) * k_a.astype(F32))
    r_h, k_h, v_h, a_h, w_h = heads(r), heads(k), heads(v), heads(a), heads(decay)
    y = rwkv7_recurrence(r_h, w_h, k_h, v_h, -kk, kk * a_h)
    mean = jnp.mean(y, axis=-1, keepdims=True)
    var = jnp.mean(jnp.square(y - mean), axis=-1, keepdims=True)
    y = ((y - mean) * lax.rsqrt(var + RW_GN_EPS)).reshape(B, T, RW_WIDTH)
    y = y * ln_w.astype(F32) + ln_b.astype(F32)
    bonus = jnp.sum(r_h * k_h * r_k.astype(F32).reshape(RW_HEADS, RW_HEAD_DIM), axis=-1, keepdims=True) * v_h
    y = (y + bonus.reshape(B, T, RW_WIDTH)) * g.astype(F32)
    return y.astype(p.dtype)


def dilated_window_attention(q, k, v, window, dilation):
    B, T, H, Dh = q.shape
    L = window // dilation
    unit = L * dilation
    Tp = -(-T // unit) * unit
    M = Tp // dilation
    nb = M // L

    def to_blocks(t):
        t = jnp.pad(t, ((0, 0), (0, Tp - T), (0, 0), (0, 0)))
        t = t.reshape(B, M, dilation, H, Dh).transpose(0, 2, 1, 3, 4)
        return t.reshape(B, dilation, nb, L, H, Dh)

    def with_prev(t):
        prev = jnp.pad(t, ((0, 0), (0, 0), (1, 0), (0, 0), (0, 0), (0, 0)))[:, :, :-1]
        return jnp.concatenate([prev, t], axis=3)

    qb = to_blocks(q)
    kb = with_prev(to_blocks(k))
    vb = with_prev(to_blocks(v))
    s = jnp.einsum('brnqhd,brnkhd->brnhqk', qb, kb, preferred_element_type=F32) * (Dh ** -0.5)
    qi = jnp.arange(L)[:, None] + L
    ki = jnp.arange(2 * L)[None, :]
    dist = qi - ki
    band = (dist >= 0) & (dist <= L)
    valid = band[None] & ((jnp.arange(nb)[:, None, None] > 0) | (ki[None] >= L))
    s = jnp.where(valid[None, None, :, None], s, -jnp.inf)
    m = jnp.max(s, axis=-1, keepdims=True)
    e = jnp.exp(s - m)
    den = jnp.sum(e, axis=-1, keepdims=True)
    o = jnp.einsum('brnhqk,brnkhd->brnqhd', e / den, vb.astype(F32))
    lse = (m + jnp.log(den))[..., 0]
    o = o.reshape(B, dilation, M, H, Dh).transpose(0, 2, 1, 3, 4).reshape(B, Tp, H, Dh)[:, :T]
    lse = lse.transpose(0, 1, 2, 4, 3).reshape(B, dilation, M, H).transpose(0, 2, 1, 3).reshape(B, Tp, H)[:, :T]
    return o, lse


def dilated_attention_mixture(p, q_norm, k_norm, pos):
    B, T, _ = p.shape
    cols = split_cols(p, (DIL_WIDTH,) * (3 * N_DIL))
    outs, lses = [], []
    for g, (window, dilation) in enumerate(DIL_PATTERNS):
        q, k, v = (c.reshape(B, T, DIL_HEADS, HEAD_DIM) for c in cols[3 * g:3 * g + 3])
        q = apply_rope(rms_norm(q, q_norm[g]), pos, ROPE_DIM, ROPE_THETA)
        k = apply_rope(rms_norm(k, k_norm[g]), pos, ROPE_DIM, ROPE_THETA)
        o, lse = dilated_window_attention(q, k, v, window, dilation)
        outs.append(o)
        lses.append(lse)
    weights = jax.nn.softmax(jnp.stack(lses), axis=0)
    o = jnp.einsum('gbth,gbthd->bthd', weights, jnp.stack(outs))
    return o.reshape(B, T, DIL_WIDTH).astype(p.dtype)


def retention_chunkwise(q, k, v, log_gamma):
    B, T, H, dk = q.shape
    dv = v.shape[-1]
    C = RET_CHUNK
    Tp = -(-T // C) * C
    nc = Tp // C
    chunk = lambda t: jnp.pad(t, ((0, 0), (0, Tp - T), (0, 0), (0, 0))).reshape(B, nc, C, H, t.shape[-1])
    qc, kc, vc = chunk(q), chunk(k), chunk(v)
    j = jnp.arange(C, dtype=F32)
    diff = j[:, None] - j[None, :]
    decay_in = jnp.where(diff >= 0, jnp.exp(log_gamma[:, None, None] * jnp.maximum(diff, 0.0)), 0.0)
    s = jnp.einsum('bcihd,bcjhd->bchij', qc, kc) * decay_in
    inner = jnp.einsum('bchij,bcjhe->bcihe', s, vc)
    k_dec = jnp.exp(log_gamma[None, :] * (C - 1 - j)[:, None])
    kv = jnp.einsum('bcjhd,jh,bcjhe->bchde', kc, k_dec, vc)
    gamma_c = jnp.exp(log_gamma * C)[None, :, None, None]

    def step(S, kv_c):
        return gamma_c * S + kv_c, S

    _, S_prev = lax.scan(step, jnp.zeros((B, H, dk, dv), F32), jnp.moveaxis(kv, 1, 0))
    S_prev = jnp.moveaxis(S_prev, 0, 1)
    q_dec = jnp.exp(log_gamma[None, :] * (j + 1.0)[:, None])
    cross = jnp.einsum('bcihd,ih,bchde->bcihe', qc, q_dec, S_prev)
    return (inner + cross).reshape(B, Tp, H, dv)[:, :T]


def multiscale_retention(p, gain, pos):
    B, T, _ = p.shape
    q, k, v, g = split_cols(p, RET_COLS)
    q = apply_rope(q.reshape(B, T, RET_HEADS, RET_QK_DIM), pos, RET_QK_DIM, RET_ROPE_BASE)
    k = apply_rope(k.reshape(B, T, RET_HEADS, RET_QK_DIM), pos, RET_QK_DIM, RET_ROPE_BASE)
    v = v.reshape(B, T, RET_HEADS, RET_V_DIM)
    log_gamma = jnp.log(1.0 - 2.0 ** (-5.0 - jnp.arange(RET_HEADS, dtype=F32)))
    y = retention_chunkwise(q.astype(F32), k.astype(F32) * (RET_QK_DIM ** -0.5), v.astype(F32), log_gamma)
    y = rms_norm(y, gain.reshape(RET_HEADS, RET_V_DIM))
    return jax.nn.silu(g) * y.reshape(B, T, RET_WIDTH).astype(g.dtype)


def hybrid_mixer(u, w_in, rw_mu, rw_w0, rw_w2, rw_a0, rw_a2, rw_g2, rw_k_k, rw_k_a, rw_r_k,
                 rw_ln_w, rw_ln_b, dil_q_norm, dil_k_norm, ret_norm,
                 w_branch_rwkv, w_branch_dil, w_branch_ret, w_out):
    B, T, _ = u.shape
    pos = jnp.arange(T)
    proj = u @ w_in
    p_rw, p_dil, p_ret, p_gate = split_cols(proj, (RW_IN_COLS, DIL_IN_COLS, RET_IN_COLS, GATE_IN_COLS))
    y_a = rwkv7_time_mix(p_rw, rw_mu, rw_w0, rw_w2, rw_a0, rw_a2, rw_g2, rw_k_k, rw_k_a, rw_r_k, rw_ln_w, rw_ln_b)
    y_b = dilated_attention_mixture(p_dil, dil_q_norm, dil_k_norm, pos)
    y_c = multiscale_retention(p_ret, ret_norm, pos)
    gates = jax.nn.sigmoid(p_gate.reshape(B, T, N_BRANCHES, D_MODEL))
    merged = (gates[:, :, 0] * (y_a @ w_branch_rwkv)
              + gates[:, :, 1] * (y_b @ w_branch_dil)
              + gates[:, :, 2] * (y_c @ w_branch_ret))
    return merged @ w_out


def memory_cross_attention(hn, mn, wq, wkv, q_norm, k_norm, wo):
    B, T, _ = hn.shape
    Ml = mn.shape[1]
    q = (hn @ wq).reshape(B, T, XA_HEADS, XA_HEAD_DIM)
    k, v = jnp.split(mn @ wkv, 2, axis=-1)
    k = k.reshape(B, Ml, XA_HEADS, XA_HEAD_DIM)
    v = v.reshape(B, Ml, XA_HEADS, XA_HEAD_DIM)
    q = rms_norm(q, q_norm)
    k = rms_norm(k, k_norm)
    s = jnp.einsum('bthd,bmhd->bhtm', q, k, preferred_element_type=F32) * (XA_HEAD_DIM ** -0.5)
    pr = jax.nn.softmax(s, axis=-1)
    o = jnp.einsum('bhtm,bmhd->bthd', pr.astype(v.dtype), v)
    return o.reshape(B, T, D_MODEL) @ wo


def setup_inputs(seed: int = 0) -> dict:
    key = jax.random.key(seed)
    ks = iter(list(jax.random.split(key, 64)))
    L, D = DEPTH, D_MODEL

    def nrm(shape, scale):
        return jax.random.normal(next(ks), shape, F32) * scale

    def gain(shape):
        return 1.0 + 0.02 * jax.random.normal(next(ks), shape, F32)

    def unif(shape, lo, hi):
        return jax.random.uniform(next(ks), shape, F32, lo, hi)

    return {
        'x': nrm((BATCH, SEQ, D), 1.0),
        'mem': nrm((BATCH, MEM_LEN, D), 1.0),
        'norm_ffn1': gain((L, D)),
        'ffn1_w13': nrm((L, D, 2 * D_FF), D ** -0.5),
        'ffn1_w2': nrm((L, D_FF, D), D_FF ** -0.5),
        'norm_mix': gain((L, D)),
        'w_in': nrm((L, D, IN_COLS), D ** -0.5),
        'rw_mu': unif((L, RW_IN_COLS), 0.0, 1.0),
        'rw_w0': unif((L, RW_WIDTH), -6.0, 0.0),
        'rw_w2': nrm((L, RW_DECAY_LORA, RW_WIDTH), 0.5 * RW_DECAY_LORA ** -0.5),
        'rw_a0': nrm((L, RW_WIDTH), 0.1),
        'rw_a2': nrm((L, RW_AAA_LORA, RW_WIDTH), 0.5 * RW_AAA_LORA ** -0.5),
        'rw_g2': nrm((L, RW_GATE_LORA, RW_WIDTH), RW_GATE_LORA ** -0.5),
        'rw_k_k': 0.85 + 0.02 * jax.random.normal(next(ks), (L, RW_WIDTH), F32),
        'rw_k_a': gain((L, RW_WIDTH)),
        'rw_r_k': nrm((L, RW_WIDTH), 0.1),
        'rw_ln_w': gain((L, RW_WIDTH)),
        'rw_ln_b': nrm((L, RW_WIDTH), 0.02),
        'dil_q_norm': gain((L, N_DIL, HEAD_DIM)),
        'dil_k_norm': gain((L, N_DIL, HEAD_DIM)),
        'ret_norm': gain((L, RET_WIDTH)),
        'w_branch_rwkv': nrm((L, RW_WIDTH, D), RW_WIDTH ** -0.5),
        'w_branch_dil': nrm((L, DIL_WIDTH, D), DIL_WIDTH ** -0.5),
        'w_branch_ret': nrm((L, RET_WIDTH, D), RET_WIDTH ** -0.5),
        'w_out': nrm((L, D, D), D ** -0.5),
        'norm_xattn': gain((L, D)),
        'norm_mem': gain((L, D)),
        'xa_wq': nrm((L, D, D), D ** -0.5),
        'xa_wkv': nrm((L, D, 2 * D), D ** -0.5),
        'xa_q_norm': gain((L, XA_HEAD_DIM)),
        'xa_k_norm': gain((L, XA_HEAD_DIM)),
        'xa_wo': nrm((L, D, D), D ** -0.5),
        'norm_ffn2': gain((L, D)),
        'ffn2_w13': nrm((L, D, 2 * D_FF), D ** -0.5),
        'ffn2_w2': nrm((L, D_FF, D), D_FF ** -0.5),
    }


def reference(x, mem, norm_ffn1, ffn1_w13, ffn1_w2, norm_mix, w_in, rw_mu, rw_w0, rw_w2, rw_a0, rw_a2,
              rw_g2, rw_k_k, rw_k_a, rw_r_k, rw_ln_w, rw_ln_b, dil_q_norm, dil_k_norm, ret_norm,
              w_branch_rwkv, w_branch_dil, w_branch_ret, w_out, norm_xattn, norm_mem, xa_wq, xa_wkv,
              xa_q_norm, xa_k_norm, xa_wo, norm_ffn2, ffn2_w13, ffn2_w2):
    h = x
    for l in range(DEPTH):
        h = h + 0.5 * swiglu(rms_norm(h, norm_ffn1[l]), ffn1_w13[l], ffn1_w2[l])
        h = h + hybrid_mixer(rms_norm(h, norm_mix[l]), w_in[l], rw_mu[l], rw_w0[l], rw_w2[l], rw_a0[l],
                             rw_a2[l], rw_g2[l], rw_k_k[l], rw_k_a[l], rw_r_k[l], rw_ln_w[l], rw_ln_b[l],
                             dil_q_norm[l], dil_k_norm[l], ret_norm[l],
                             w_branch_rwkv[l], w_branch_dil[l], w_branch_ret[l], w_out[l])
        h = h + memory_cross_attention(rms_norm(h, norm_xattn[l]), rms_norm(mem, norm_mem[l]),
                                       xa_wq[l], xa_wkv[l], xa_q_norm[l], xa_k_norm[l], xa_wo[l])
        h = h + 0.5 * swiglu(rms_norm(h, norm_ffn2[l]), ffn2_w13[l], ffn2_w2[l])
    return h
```

```python
import contextlib
import numpy as np
import concourse.bass as bass
import concourse.mybir as mybir
from concourse.bass_utils import run_bass_kernel_spmd

F32 = mybir.dt.float32
BF16 = mybir.dt.bfloat16
AF = mybir.ActivationFunctionType
ALU = mybir.AluOpType
AX = mybir.AxisListType


class Buf:
    def __init__(self, name, t=None, unordered=False):
        self.name = name
        self.t = t
        self.writers = {}
        self.readers = {}
        self.unordered = unordered


class _Eng:
    def __init__(self, name, eng, sem, step):
        self.name, self.eng, self.sem, self.step = name, eng, sem, step
        self.count = 0
        self.seen = {}


class Sched:
    N_DMA_SEMS = 12
    same_engine_sync = True

    def __init__(self, nc):
        self.nc = nc
        self.engs = {}
        self.queues = {}
        self._stack = contextlib.ExitStack()
        for name, eng in (("pe", nc.tensor), ("act", nc.scalar), ("dve", nc.vector), ("pool", nc.gpsimd),
                          ("sync", nc.sync)):
            sem = nc.alloc_semaphore("s_" + name)
            self.engs[name] = _Eng(name, eng, sem, 1)
        self.dma_pool = {}
        self.dma_rr = {}
        for q in ("sync", "pool"):
            lst = []
            for i in range(self.N_DMA_SEMS):
                nm = "d_%s%d" % (q, i)
                sem = nc.alloc_semaphore(nm)
                self.engs[nm] = _Eng(nm, None, sem, 16)
                lst.append(nm)
            self.dma_pool[q] = lst
            self.dma_rr[q] = 0
        self.n_ops = 0

    def sb(self, name, shape, dtype, unordered=False):
        t = self.nc.alloc_sbuf_tensor(name, list(shape), dtype)
        return Buf(name, t, unordered)

    def ps(self, name, shape, dtype):
        t = self.nc.alloc_psum_tensor(name, list(shape), dtype)
        return Buf(name, t)

    def block(self):
        return contextlib.nullcontext()

    def _waits(self, E, reads, writes):
        waits = {}
        for b in reads:
            for en, c in b.writers.items():
                if waits.get(en, 0) < c:
                    waits[en] = c
        for b in writes:
            if not b.unordered:
                for en, c in b.writers.items():
                    if waits.get(en, 0) < c:
                        waits[en] = c
            for en, c in b.readers.items():
                if waits.get(en, 0) < c:
                    waits[en] = c
        for en, c in waits.items():
            if E.seen.get(en, 0) >= c:
                continue
            X = self.engs[en]
            E.eng.wait_ge(X.sem, c * X.step)
            E.seen[en] = c

    def op(self, ename, fn, reads=(), writes=()):
        E = self.engs[ename]
        self._waits(E, reads, writes)
        ins = fn(E.eng)
        E.count += 1
        ins.then_inc(E.sem, 1)
        if not (self.same_engine_sync and ename != 'pe'):
            E.seen[ename] = E.count
        for b in reads:
            b.readers[ename] = E.count
        for b in writes:
            if b.unordered:
                b.writers[ename] = E.count
            else:
                b.writers = {ename: E.count}
                b.readers = {}
        self.n_ops += 1
        return ins

    def dma(self, q, out, in_, reads=(), writes=()):
        Q = self.engs[q]
        pool = self.dma_pool[q]
        vn = pool[self.dma_rr[q] % len(pool)]
        self.dma_rr[q] += 1
        V = self.engs[vn]
        if V.count > 0 and Q.seen.get(vn, 0) < V.count:
            Q.eng.wait_ge(V.sem, V.count * 16)
            Q.seen[vn] = V.count
        self._waits(Q, reads, writes)
        ins = Q.eng.dma_start(out=out, in_=in_)
        V.count += 1
        ins.then_inc(V.sem, 16)
        for b in reads:
            b.readers[vn] = V.count
        for b in writes:
            if b.unordered:
                b.writers[vn] = V.count
            else:
                b.writers = {vn: V.count}
                b.readers = {}
        self.n_ops += 1
        return ins

    def finish(self):
        Q = self.engs["sync"]
        for en, X in self.engs.items():
            if en == "sync" or X.count == 0:
                continue
            if Q.seen.get(en, 0) < X.count:
                Q.eng.wait_ge(X.sem, X.count * X.step)
                Q.seen[en] = X.count


    def barrier(self):
        names = ["pe", "act", "dve", "pool", "sync"]
        for en in names:
            E = self.engs[en]
            for xn, X in self.engs.items():
                if xn == en or X.count == 0:
                    continue
                if E.seen.get(xn, 0) < X.count:
                    E.eng.wait_ge(X.sem, X.count * X.step)
                    E.seen[xn] = X.count

    def sbt(self, es, name, shape, dtype, unordered=False):
        self._uid = getattr(self, "_uid", 0) + 1
        name = "sb%d_%s" % (self._uid, name)
        t = es.enter_context(self.nc.sbuf_tensor(name, list(shape), dtype))
        return Buf(name, t, unordered)

    def mm(self, out, lhsT, rhs, start, stop, reads, writes):
        return self.op("pe", lambda e: e.matmul(out, lhsT=lhsT, rhs=rhs, start=start, stop=stop), reads, writes)

    def tr(self, out, in_, ident, reads, writes):
        return self.op("pe", lambda e: e.transpose(out, in_, ident.t[:, 0:128]), list(reads) + [ident], writes)

    def act(self, out, in_, func, reads, writes, **kw):
        return self.op("act", lambda e: e.activation(out=out, in_=in_, func=func, **kw), reads, writes)

    def copy(self, eng, out, in_, reads, writes):
        if eng == "act":
            return self.op("act", lambda e: e.activation(out=out, in_=in_, func=AF.Copy), reads, writes)
        return self.op(eng, lambda e: e.tensor_copy(out=out, in_=in_), reads, writes)

    def tt(self, eng, out, in0, in1, op, reads, writes):
        return self.op(eng, lambda e: e.tensor_tensor(out=out, in0=in0, in1=in1, op=op), reads, writes)

    def ts(self, eng, out, in0, s1, op0, reads, writes, s2=None, op1=None):
        if op1 is None:
            return self.op(eng, lambda e: e.tensor_scalar(out=out, in0=in0, scalar1=s1, scalar2=None, op0=op0), reads, writes)
        return self.op(eng, lambda e: e.tensor_scalar(out=out, in0=in0, scalar1=s1, scalar2=s2, op0=op0, op1=op1), reads, writes)

    def stt(self, eng, out, in0, scalar, in1, op0, op1, reads, writes):
        return self.op(eng, lambda e: e.scalar_tensor_tensor(out=out, in0=in0, scalar=scalar, in1=in1, op0=op0, op1=op1), reads, writes)


class Rot:
    def __init__(self, items):
        self.items = list(items)
        self.i = 0

    def next(self):
        b = self.items[self.i % len(self.items)]
        self.i += 1
        return b


T = 4096
D = 1024
DFF = 2816
MEM = 256
NT = T // 128
EPS = 1e-6
GN_EPS = 64e-5
C_RW, C_DIL, C_RET, C_GATE = 0, 1824, 1824 + 4608, 1824 + 4608 + 1536
DILS = (1, 4, 16)
CP_MU, CP_W0, CP_A0, CP_KK, CP_KA, CP_XQ, CP_XK, CP_RK, CP_N = 0, 15, 19, 23, 27, 31, 33, 35, 67
RP_FFN1, RP_MIX, RP_XA, RP_MEM, RP_FFN2 = 0, 1024, 2048, 3072, 4096
RP_LNW, RP_LNB, RP_DQ, RP_DK, RP_RET, RP_N = 5120, 5632, 6144, 7680, 9216, 9728
CB_ID, CB_ONES, CB_BO2, CB_M3, CB_MPC, CB_N = 0, 128, 256, 384, 768, 1280
CF_DP, CF_QD, CF_KD, CF_GC, CF_DCOS, CF_DSIN, CF_RCOS, CF_RSIN, CF_N = 0, 512, 516, 520, 522, 1290, 2058, 3082, 4106


def host_constants():
    p = np.arange(128)
    cb = np.zeros((128, CB_N), np.float32)
    cb[:, CB_ID:CB_ID + 128] = np.eye(128)
    cb[:, CB_ONES:CB_ONES + 128] = 1.0
    cb[:, CB_BO2:CB_BO2 + 128] = (p[:, None] // 64 == p[None, :] // 64)
    cb[:, CB_M3:CB_M3 + 128] = (p[:, None] < p[None, :])
    cb[:, CB_M3 + 128:CB_M3 + 256] = (p[:, None] <= p[None, :])
    cb[:, CB_M3 + 256:CB_M3 + 384] = (p[None, :] < p[:, None])
    for hh in range(2):
        cb[:, CB_MPC + hh * 256:CB_MPC + hh * 256 + 128] = (p[:, None] >= p[None, :])
        cb[:, CB_MPC + hh * 256 + 128:CB_MPC + hh * 256 + 256] = (p[:, None] <= p[None, :])
    cf = np.zeros((128, CF_N), np.float64)
    gam = 1.0 - 2.0 ** (-5.0 - np.arange(4))
    for h in range(4):
        cf[:, CF_DP + h * 128:CF_DP + (h + 1) * 128] = (gam[h] ** (-(p[:, None] + 1.0))) * (p[:, None] <= p[None, :]) / 8.0
        cf[:, CF_QD + h] = gam[h] ** (p + 1.0)
        cf[:, CF_KD + h] = gam[h] ** (127.0 - p) / 8.0
    for j in range(2):
        cf[:, CF_GC + j] = gam[2 * j + p // 64] ** 128.0
    invf = 500000.0 ** (-np.arange(8, dtype=np.float64) / 8.0)
    for g, d in enumerate(DILS):
        nblk = 32 // d
        for c in range(d):
            for n in range(nblk):
                combo = c * nblk + n
                pos = ((n * 128 + p) * d + c).astype(np.float32)
                ang = (pos[:, None] * invf[None, :].astype(np.float32)).astype(np.float32)
                o = (g * 32 + combo) * 8
                cf[:, CF_DCOS + o:CF_DCOS + o + 8] = np.cos(ang)
                cf[:, CF_DSIN + o:CF_DSIN + o + 8] = np.sin(ang)
    invr = (10000.0 ** (-np.arange(32, dtype=np.float64) / 32.0)).astype(np.float32)
    for c in range(32):
        pos = (c * 128 + p).astype(np.float32)
        ang = (pos[:, None] * invr[None, :]).astype(np.float32)
        cf[:, CF_RCOS + c * 32:CF_RCOS + (c + 1) * 32] = np.cos(ang)
        cf[:, CF_RSIN + c * 32:CF_RSIN + (c + 1) * 32] = np.sin(ang)
    return cb.astype(np.float32), cf.astype(np.float32)


def host_layout_params(inp):
    L = 2
    colp = np.zeros((L, 128, CP_N), np.float32)
    rowp = np.zeros((L, RP_N), np.float32)
    p = np.arange(128)
    for l in range(L):
        mu = np.zeros(15 * 128, np.float32)
        mu[:1824] = inp["rw_mu"][l]
        colp[l, :, CP_MU:CP_MU + 15] = mu.reshape(15, 128).T
        for off, key in ((CP_W0, "rw_w0"), (CP_A0, "rw_a0"), (CP_KK, "rw_k_k"), (CP_KA, "rw_k_a")):
            colp[l, :, off:off + 4] = inp[key][l].reshape(4, 128).T
        colp[l, :, CP_XQ:CP_XQ + 2] = inp["xa_q_norm"][l].reshape(2, 128).T
        colp[l, :, CP_XK:CP_XK + 2] = inp["xa_k_norm"][l].reshape(2, 128).T
        rk = inp["rw_r_k"][l]
        for j in range(4):
            for h in range(8):
                if h // 2 == j:
                    rows = p[(p // 64) == (h % 2)]
                    colp[l, rows, CP_RK + j * 8 + h] = rk[h * 64 + (rows % 64)]
        for off, key in ((RP_FFN1, "norm_ffn1"), (RP_MIX, "norm_mix"), (RP_XA, "norm_xattn"), (RP_MEM, "norm_mem"),
                         (RP_FFN2, "norm_ffn2")):
            rowp[l, off:off + 1024] = inp[key][l]
        rowp[l, RP_LNW:RP_LNW + 512] = inp["rw_ln_w"][l]
        rowp[l, RP_LNB:RP_LNB + 512] = inp["rw_ln_b"][l]
        for g in range(3):
            rowp[l, RP_DQ + g * 512:RP_DQ + (g + 1) * 512] = np.tile(inp["dil_q_norm"][l, g], 8)
            rowp[l, RP_DK + g * 512:RP_DK + (g + 1) * 512] = np.tile(inp["dil_k_norm"][l, g], 8)
        rowp[l, RP_RET:RP_RET + 512] = inp["ret_norm"][l]
    return colp, rowp


WEIGHT_SPECS = [
    ("ffn1_w13", [2, D, 2 * DFF]), ("ffn1_w2", [2, DFF, D]), ("w_in", [2, D, 11040]),
    ("rw_w2", [2, 64, 512]), ("rw_a2", [2, 64, 512]), ("rw_g2", [2, 160, 512]),
    ("w_branch_rwkv", [2, 512, D]), ("w_branch_dil", [2, 512, D]), ("w_branch_ret", [2, 512, D]),
    ("w_out", [2, D, D]), ("xa_wq", [2, D, D]), ("xa_wkv", [2, D, 2 * D]), ("xa_wo", [2, D, D]),
    ("ffn2_w13", [2, D, 2 * DFF]), ("ffn2_w2", [2, DFF, D]),
]


class Prog:
    def __init__(self, stages=None, debug=False):
        self.stages = stages
        nc = self.nc = bass.Bass("TRN2", target_bir_lowering=False)
        self.S = S = Sched(nc)
        dt = lambda n, sh, ty=F32, kind="ExternalInput": nc.dram_tensor(n, sh, ty, kind=kind).ap()
        self.x = dt("x", [T, D])
        self.mem = dt("mem", [MEM, D])
        self.W = {n: dt(n, sh) for n, sh in WEIGHT_SPECS}
        self.colp_d = dt("colp", [2, 128, CP_N])
        self.rowp_d = dt("rowp", [2, RP_N])
        self.cb_d = dt("cb", [128, CB_N])
        self.cf_d = dt("cf", [128, CF_N])
        self.out = dt("out", [T, D], F32, "ExternalOutput")
        okind = "ExternalOutput" if debug else "Internal"
        self.ya_d = dt("ya_s", [T, 512], BF16, okind)
        self.yc_d = dt("yc_s", [T, 512], BF16, okind)
        self.oa_d = dt("oa_s", [3, T, 520], F32, okind)
        self.hbuf = [Buf("h%d" % i) for i in range(NT)]
        self.ya_b = [Buf("ya%d" % i) for i in range(NT)]
        self.yc_b = [Buf("yc%d" % i) for i in range(NT)]
        self.oa_b = [Buf("oa%d" % g, unordered=True) for g in range(3)]
        self.banks = Rot([S.ps("pb%d" % i, [128, 512], F32) for i in range(6)])
        self.tbanks = Rot([S.ps("pt%d" % i, [128, 1024], BF16) for i in range(2)])
        with contextlib.ExitStack() as es:
            self.cb = S.sbt(es, "cb", [128, CB_N], BF16)
            S.dma("pool", self.cb.t[:], self.cb_d, writes=[self.cb])
            self.ident = self.cb
            self.colp = S.sbt(es, "colp", [128, CP_N], F32)
            self.smalls = Rot([S.sbt(es, "sm%d" % i, [128, 16], F32) for i in range(8)])
            self.junk = S.sbt(es, "junk", [128, 1024], BF16)
            self.xn = Rot([S.sbt(es, "xn%d" % i, [128, 1024], BF16) for i in range(2)])
            for l in range(2):
                self.l = l
                S.barrier()
                S.dma("sync", self.colp.t[:], self.colp_d[l], writes=[self.colp])
                self.run("cp", self.copy_x)
                self.run("ffn1", lambda: self.ffn(es, l, RP_FFN1, self.W["ffn1_w13"][l], self.W["ffn1_w2"][l], first=(l == 0)))
                self.run("mix", lambda: self.mixer(es, l))
                self.run("xa", lambda: self.xattn(es, l))
                self.run("ffn2", lambda: self.ffn(es, l, RP_FFN2, self.W["ffn2_w13"][l], self.W["ffn2_w2"][l], first=False))
            S.barrier()
            S.finish()

    def run(self, name, fn):
        key = "%s%d" % (name, self.l)
        if self.stages is not None and key not in self.stages:
            return
        if self.stages is None and name == "cp":
            return
        fn()
        self.S.barrier()

    def copy_x(self):
        with contextlib.ExitStack() as es:
            hts = Rot([self.S.sbt(es, "cpx%d" % i, [128, D], F32) for i in range(2)])
            for ti in range(NT):
                ht = hts.next()
                self.S.dma("sync", ht.t[:], self.x[ti * 128:(ti + 1) * 128, :], writes=[ht])
                self.S.dma("sync", self.out[ti * 128:(ti + 1) * 128, :], ht.t[:], reads=[ht], writes=[self.hbuf[ti]])

    def idn(self):
        return self.cb.t[:, CB_ID:CB_ID + 128]

    def norm_T(self, src_ap, src_bufs, gain, dstT, col0, ht, nparts=128):
        S = self.S
        S.dma("sync", ht.t[:], src_ap, reads=src_bufs, writes=[ht])
        sm = self.smalls.next()
        S.act(self.junk.t[:], ht.t[:], AF.Square, [ht], [self.junk, sm], accum_out=sm.t[:, 0:1])
        S.act(sm.t[:, 1:2], sm.t[:, 0:1], AF.Sqrt, [sm], [sm], scale=1.0 / D, bias=EPS)
        S.op("dve", lambda e: e.reciprocal(out=sm.t[:, 2:3], in_=sm.t[:, 1:2]), [sm], [sm])
        xn = self.xn.next()
        S.stt("dve", xn.t[:], ht.t[:], sm.t[:, 2:3], gain.t[:], ALU.mult, ALU.mult, [ht, sm, gain], [xn])
        tb = self.tbanks.next()
        for kc in range(8):
            S.tr(tb.t[:, kc * 128:(kc + 1) * 128], xn.t[:, kc * 128:(kc + 1) * 128], self.cb, [xn], [tb])
        S.copy("act", dstT.t[:, 0:8, col0:col0 + 128], tb.t[:].rearrange("p (k t) -> p k t", k=8), [tb], [dstT])

    def load_w(self, dst, src_ap, nk, rows=128):
        for kc in range(nk):
            self.S.dma("pool", dst.t[0:rows, kc, :], src_ap[kc * 128:kc * 128 + rows, :], writes=[dst])

    def load_row(self, dst, l, off, n):
        self.S.dma("sync", dst.t[:, 0:n], self.rowp_d[l, off:off + n].partition_broadcast(128), writes=[dst])

    def ffn(self, es0, l, rp_off, w13_ap, w2_ap, first):
        S = self.S
        TT = 256
        with contextlib.ExitStack() as es:
            w13 = S.sbt(es, "w13", [128, 8, 2 * DFF], BF16)
            w2 = S.sbt(es, "w2", [128, 22, D], BF16)
            gain = S.sbt(es, "gain", [128, D], F32)
            self.load_row(gain, l, rp_off, D)
            self.load_w(w13, w13_ap, 8)
            self.load_w(w2, w2_ap, 22)
            nTs = Rot([S.sbt(es, "nT%d" % i, [128, 8, TT], BF16) for i in range(2)])
            gTs = Rot([S.sbt(es, "gT%d" % i, [128, 22, TT], BF16) for i in range(1)])
            hts = Rot([S.sbt(es, "ht%d" % i, [128, D], F32) for i in range(4)])
            sas = Rot([S.sbt(es, "sa%d" % i, [128, TT], F32) for i in range(3)])
            src = self.x if first else self.out
            for tt in range(T // TT):
                n = nTs.next()
                hs = []
                for s in range(2):
                    ti = tt * 2 + s
                    ht = hts.next()
                    self.norm_T(src[ti * 128:(ti + 1) * 128, :], [] if first else [self.hbuf[ti]], gain, n, s * 128, ht)
                    hs.append(ht)
                g = gTs.next()
                for fc in range(22):
                    pa = self.banks.next()
                    pb = self.banks.next()
                    for kc in range(8):
                        S.mm(pa.t[:, 0:TT], w13.t[:, kc, fc * 128:(fc + 1) * 128], n.t[:, kc, :], kc == 0, kc == 7, [w13, n], [pa])
                    for kc in range(8):
                        S.mm(pb.t[:, 0:TT], w13.t[:, kc, DFF + fc * 128:DFF + (fc + 1) * 128], n.t[:, kc, :], kc == 0, kc == 7, [w13, n], [pb])
                    sa = sas.next()
                    S.act(sa.t[:], pa.t[:, 0:TT], AF.Silu, [pa], [sa])
                    S.tt("dve", g.t[:, fc, :], sa.t[:], pb.t[:, 0:TT], ALU.mult, [sa, pb], [g])
                for s in range(2):
                    ti = tt * 2 + s
                    for half in range(2):
                        po = self.banks.next()
                        for fc in range(22):
                            S.mm(po.t[:], g.t[:, fc, s * 128:(s + 1) * 128], w2.t[:, fc, half * 512:(half + 1) * 512], fc == 0, fc == 21, [g, w2], [po])
                        hsl = hs[s].t[:, half * 512:(half + 1) * 512]
                        S.stt("dve", hsl, po.t[:], 0.5, hsl, ALU.mult, ALU.add, [po, hs[s]], [hs[s]])
                    S.dma("sync", self.out[ti * 128:(ti + 1) * 128, :], hs[s].t[:], reads=[hs[s]], writes=[self.hbuf[ti]])

    def xattn(self, es0, l):
        S = self.S
        TT = 256
        ones = self.cb.t[:, CB_ONES:CB_ONES + 128]
        with contextlib.ExitStack() as es:
            wq = S.sbt(es, "wq", [128, 8, D], BF16)
            wkv = S.sbt(es, "wkv", [128, 8, 2 * D], BF16)
            wo = S.sbt(es, "wo", [128, 8, D], BF16)
            gx = S.sbt(es, "gx", [128, D], F32)
            gm = S.sbt(es, "gm", [128, D], F32)
            self.load_row(gx, l, RP_XA, D)
            self.load_row(gm, l, RP_MEM, D)
            self.load_w(wkv, self.W["xa_wkv"][l], 8)
            self.load_w(wq, self.W["xa_wq"][l], 8)
            self.load_w(wo, self.W["xa_wo"][l], 8)
            mnT = S.sbt(es, "mnT", [128, 8, MEM], BF16)
            KnT = S.sbt(es, "KnT", [128, 8, MEM], BF16)
            V = S.sbt(es, "V", [128, 2, D], BF16)
            qf = S.sbt(es, "qf", [128, 8, TT], F32)
            sq = S.sbt(es, "sq", [128, 8, TT], BF16)
            rrs = Rot([S.sbt(es, "rr%d" % i, [128, 2, TT], F32) for i in range(2)])
            hts = Rot([S.sbt(es, "ht%d" % i, [128, D], F32) for i in range(4)])
            nTs = Rot([S.sbt(es, "nT%d" % i, [128, 8, TT], BF16) for i in range(2)])
            qnT = S.sbt(es, "qnT", [128, 8, TT], BF16)
            oT = S.sbt(es, "oT", [128, 8, TT], BF16)
            Es = Rot([S.sbt(es, "E%d" % i, [128, 2, TT], BF16) for i in range(2)])

            def qknorm(src_w, c0, rhsT, dst, cpoff):
                for fc in range(8):
                    pq = self.banks.next()
                    for kc in range(8):
                        S.mm(pq.t[:, 0:TT], src_w.t[:, kc, c0 + fc * 128:c0 + (fc + 1) * 128], rhsT.t[:, kc, :], kc == 0, kc == 7, [src_w, rhsT], [pq])
                    S.copy("dve", qf.t[:, fc, :], pq.t[:, 0:TT], [pq], [qf])
                    S.act(sq.t[:, fc, :], qf.t[:, fc, :], AF.Square, [qf], [sq])
                import os
                qc = int(os.environ.get("QK_CUT", "99"))
                if qc < 1:
                    return
                for hh in range(4):
                    pss = self.banks.next()
                    for c2 in range(2):
                        S.mm(pss.t[:, 0:TT], ones, sq.t[:, hh * 2 + c2, :], c2 == 0, c2 == 1, [self.cb, sq], [pss])
                    rr = rrs.next()
                    S.act(rr.t[:, 0, :], pss.t[:, 0:TT], AF.Sqrt, [pss], [rr], scale=1.0 / 256, bias=EPS)
                    if qc < 2:
                        continue
                    S.op("dve", lambda e: e.reciprocal(out=rr.t[:, 1, :], in_=rr.t[:, 0, :]), [rr], [rr])
                    if qc < 3:
                        continue
                    for c2 in range(2):
                        fc = hh * 2 + c2
                        S.stt("dve", dst.t[:, fc, :], qf.t[:, fc, :], self.colp.t[:, cpoff + c2:cpoff + c2 + 1], rr.t[:, 1, :], ALU.mult, ALU.mult, [qf, rr, self.colp], [dst])

            for mt in range(2):
                self.norm_T(self.mem[mt * 128:(mt + 1) * 128, :], [], gm, mnT, mt * 128, hts.next())
            import os
            cut = int(os.environ.get("XA_CUT", "99"))
            if cut < 1:
                return
            qknorm(wkv, 0, mnT, KnT, CP_XK)
            if cut < 2:
                return
            for mt in range(2):
                for half in range(2):
                    pv = self.banks.next()
                    for kc in range(8):
                        S.mm(pv.t[:], mnT.t[:, kc, mt * 128:(mt + 1) * 128], wkv.t[:, kc, D + half * 512:D + (half + 1) * 512], kc == 0, kc == 7, [mnT, wkv], [pv])
                    S.copy("act", V.t[:, mt, half * 512:(half + 1) * 512], pv.t[:], [pv], [V])
            if cut < 3:
                return
            for tt in range(T // TT):
                n = nTs.next()
                hs = []
                for s in range(2):
                    ti = tt * 2 + s
                    ht = hts.next()
                    self.norm_T(self.out[ti * 128:(ti + 1) * 128, :], [self.hbuf[ti]], gx, n, s * 128, ht)
                    hs.append(ht)
                qknorm(wq, 0, n, qnT, CP_XQ)
                if cut < 4:
                    continue
                for hh in range(4):
                    E = Es.next()
                    for mc in range(2):
                        ps_ = self.banks.next()
                        for c2 in range(2):
                            S.mm(ps_.t[:, 0:TT], KnT.t[:, hh * 2 + c2, mc * 128:(mc + 1) * 128], qnT.t[:, hh * 2 + c2, :], c2 == 0, c2 == 1, [KnT, qnT], [ps_])
                        S.act(E.t[:, mc, :], ps_.t[:, 0:TT], AF.Exp, [ps_], [E], scale=1.0 / 16)
                    pden = self.banks.next()
                    for mc in range(2):
                        S.mm(pden.t[:, 0:TT], ones, E.t[:, mc, :], mc == 0, mc == 1, [self.cb, E], [pden])
                    rr = rrs.next()
                    S.op("dve", lambda e: e.reciprocal(out=rr.t[:, 0, :], in_=pden.t[:, 0:TT]), [pden], [rr])
                    for dc in range(2):
                        po = self.banks.next()
                        for mc in range(2):
                            S.mm(po.t[:, 0:TT], V.t[:, mc, hh * 256 + dc * 128:hh * 256 + (dc + 1) * 128], E.t[:, mc, :], mc == 0, mc == 1, [V, E], [po])
                        S.tt("dve", oT.t[:, hh * 2 + dc, :], po.t[:, 0:TT], rr.t[:, 0, :], ALU.mult, [po, rr], [oT])
                for s in range(2):
                    ti = tt * 2 + s
                    for half in range(2):
                        po = self.banks.next()
                        for kc in range(8):
                            S.mm(po.t[:], oT.t[:, kc, s * 128:(s + 1) * 128], wo.t[:, kc, half * 512:(half + 1) * 512], kc == 0, kc == 7, [oT, wo], [po])
                        hsl = hs[s].t[:, half * 512:(half + 1) * 512]
                        S.tt("dve", hsl, po.t[:], hsl, ALU.add, [po, hs[s]], [hs[s]])
                    S.dma("sync", self.out[ti * 128:(ti + 1) * 128, :], hs[s].t[:], reads=[hs[s]], writes=[self.hbuf[ti]])

    def mixer(self, es0, l):
        S = self.S
        with contextlib.ExitStack() as es:
            uT = S.sbt(es, "uT", [128, 8, T], BF16)
            with contextlib.ExitStack() as es2:
                gain = S.sbt(es2, "gmix", [128, D], F32)
                self.load_row(gain, l, RP_MIX, D)
                hts = Rot([S.sbt(es2, "ht%d" % i, [128, D], F32) for i in range(3)])
                for ti in range(NT):
                    self.norm_T(self.out[ti * 128:(ti + 1) * 128, :], [self.hbuf[ti]], gain, uT, ti * 128, hts.next())
            S.barrier()
            sub = self.stages_sub()
            if "rw" in sub:
                self.rwkv(l, uT, None)
                S.barrier()
            with contextlib.ExitStack() as es3:
                cf = S.sbt(es3, "cf", [128, CF_N], F32)
                S.dma("sync", cf.t[:], self.cf_d, writes=[cf])
                if "dil" in sub:
                    self.dilated(l, uT, cf)
                    S.barrier()
                if "ret" in sub:
                    self.retention(l, uT, cf)
                    S.barrier()
            S.barrier()
            if "merge" in sub:
                self.merge(l, uT)

    def stages_sub(self):
        if self.stages is None:
            return ("rw", "dil", "ret", "merge")
        return [k for k in ("rw", "dil", "ret", "merge") if k in self.stages]

    def dilated(self, l, uT, cf):
        S = self.S
        with contextlib.ExitStack() as es:
            Wgs = Rot([S.sbt(es, "Wd%d" % i, [128, 8, 1536], BF16) for i in range(2)])
            gq = S.sbt(es, "gq", [128, 1536], F32)
            gk = S.sbt(es, "gk", [128, 1536], F32)
            self.load_row(gq, l, RP_DQ, 1536)
            self.load_row(gk, l, RP_DK, 1536)
            qks = [S.sbt(es, "dqk%d" % i, [128, 8, 128], BF16) for i in range(2)]
            Vas = [S.sbt(es, "dVa%d" % i, [128, 8, 65], BF16) for i in range(2)]
            for v in Vas:
                S.op("dve", lambda e: e.memset(v.t[:], 1.0), [], [v])
            sqs = Rot([S.sbt(es, "dsq%d" % i, [128, 512], F32) for i in range(2)])
            xns = Rot([S.sbt(es, "dxn%d" % i, [128, 8, 64], F32) for i in range(2)])
            tms = Rot([S.sbt(es, "dtm%d" % i, [128, 4, 8, 8], F32) for i in range(2)])
            qrs = Rot([S.sbt(es, "dqr%d" % i, [128, 2, 512], BF16) for i in range(2)])
            Ees = Rot([S.sbt(es, "dEe%d" % i, [128, 512], BF16) for i in range(2)])
            Ems = Rot([S.sbt(es, "dEm%d" % i, [128, 4, 512], BF16) for i in range(2)])
            Osb = Rot([S.sbt(es, "dO%d" % i, [128, 520], F32) for i in range(2)])
            mpc = self.cb.t[:, CB_MPC:CB_MPC + 512]
            h8 = lambda ap: ap.rearrange("p (h e) -> p h e", h=8)
            for g, d in enumerate(DILS):
                nblk = 32 // d
                W = Wgs.next()
                self.load_w(W, self.W["w_in"][l][:, C_DIL + g * 1536:C_DIL + (g + 1) * 1536], 8)
                for c in range(d):
                    for n in range(nblk):
                        combo = c * nblk + n
                        tok0 = n * 128 * d + c
                        cur, prv = n % 2, (n + 1) % 2
                        pq, pk, pv = self.banks.next(), self.banks.next(), self.banks.next()
                        for pp, jq in ((pq, 0), (pk, 1), (pv, 2)):
                            for kc in range(8):
                                S.mm(pp.t[:], uT.t[:, kc, tok0:tok0 + 127 * d + 1:d], W.t[:, kc, jq * 512:(jq + 1) * 512], kc == 0, kc == 7, [uT, W], [pp])
                        S.copy("act", Vas[cur].t[:, :, 0:64], h8(pv.t[:]), [pv], [Vas[cur]])
                        qr = qrs.next()
                        o = (g * 32 + combo) * 8
                        cosb = cf.t[:, CF_DCOS + o:CF_DCOS + o + 8].rearrange("p (a e) -> p a e", a=1).broadcast_to([128, 8, 8])
                        sinb = cf.t[:, CF_DSIN + o:CF_DSIN + o + 8].rearrange("p (a e) -> p a e", a=1).broadcast_to([128, 8, 8])
                        for w, (pp, gain) in enumerate(((pq, gq), (pk, gk))):
                            sq = sqs.next()
                            xn = xns.next()
                            S.copy("dve", xn.t[:].rearrange("p h e -> p (h e)"), pp.t[:], [pp], [xn])
                            S.act(sq.t[:], xn.t[:].rearrange("p h e -> p (h e)"), AF.Square, [xn], [sq])
                            sm = self.smalls.next()
                            S.op("dve", lambda e: e.tensor_reduce(out=sm.t[:, 0:8], in_=h8(sq.t[:]), axis=AX.X, op=ALU.add), [sq], [sm])
                            S.act(sm.t[:, 8:16], sm.t[:, 0:8], AF.Sqrt, [sm], [sm], scale=1.0 / 64, bias=EPS)
                            S.op("dve", lambda e: e.reciprocal(out=sm.t[:, 0:8], in_=sm.t[:, 8:16]), [sm], [sm])
                            S.tt("dve", xn.t[:], xn.t[:], sm.t[:, 0:8].rearrange("p (h e) -> p h e", e=1).broadcast_to([128, 8, 64]), ALU.mult, [xn, sm], [xn])
                            S.tt("dve", xn.t[:], xn.t[:], h8(gain.t[:, g * 512:(g + 1) * 512]), ALU.mult, [xn, gain], [xn])
                            tm = tms.next()
                            x1, x2 = xn.t[:, :, 0:8], xn.t[:, :, 8:16]
                            S.tt("dve", tm.t[:, 0], x1, cosb, ALU.mult, [xn, cf], [tm])
                            S.tt("dve", tm.t[:, 1], x2, sinb, ALU.mult, [xn, cf], [tm])
                            S.tt("dve", tm.t[:, 2], x2, cosb, ALU.mult, [xn, cf], [tm])
                            S.tt("dve", tm.t[:, 3], x1, sinb, ALU.mult, [xn, cf], [tm])
                            S.tt("dve", x1, tm.t[:, 0], tm.t[:, 1], ALU.subtract, [tm, xn], [xn])
                            S.tt("dve", x2, tm.t[:, 2], tm.t[:, 3], ALU.add, [tm, xn], [xn])
                            S.copy("act", qr.t[:, w, :], xn.t[:].rearrange("p h e -> p (h e)"), [xn], [qr])
                        tb = self.tbanks.next()
                        for w in range(2):
                            for j in range(4):
                                S.tr(tb.t[:, (w * 4 + j) * 128:(w * 4 + j + 1) * 128], qr.t[:, w, j * 128:(j + 1) * 128], self.cb, [qr], [tb])
                        qk = qks[cur]
                        S.copy("act", qk.t[:, 0:8, :], tb.t[:].rearrange("p (k t) -> p k t", k=8), [tb], [qk])
                        Em = Ems.next()
                        for j in range(4):
                            psd = (self.banks.next(), self.banks.next())
                            for hh in range(2):
                                pb = 64 * hh
                                ps_ = psd[hh]
                                if n > 0:
                                    S.mm(ps_.t[:, hh * 256:hh * 256 + 128], qks[prv].t[pb:pb + 64, 4 + j, :], qk.t[pb:pb + 64, j, :], True, True, [qks[prv], qk], [ps_])
                                S.mm(ps_.t[:, hh * 256 + 128:hh * 256 + 256], qk.t[pb:pb + 64, 4 + j, :], qk.t[pb:pb + 64, j, :], True, True, [qk], [ps_])
                            Ee = Ees.next()
                            for hh in range(2):
                                ps_ = psd[hh]
                                lo = hh * 256 if n > 0 else hh * 256 + 128
                                hi = hh * 256 + 256
                                S.act(Ee.t[:, lo:hi], ps_.t[:, lo:hi], AF.Exp, [ps_], [Ee], scale=0.125)
                                S.tt("dve", Em.t[:, j, lo:hi], Ee.t[:, lo:hi], mpc[:, lo:hi], ALU.mult, [Ee, self.cb], [Em])
                        O = Osb.next()
                        for half in range(2):
                            po = self.banks.next()
                            for hq in range(4):
                                h = half * 4 + hq
                                j, hh = h // 2, h % 2
                                if n > 0:
                                    S.mm(po.t[:, hq * 65:(hq + 1) * 65], Em.t[:, j, hh * 256:hh * 256 + 128], Vas[prv].t[:, h, :], True, False, [Em, Vas[prv]], [po])
                                S.mm(po.t[:, hq * 65:(hq + 1) * 65], Em.t[:, j, hh * 256 + 128:hh * 256 + 256], Vas[cur].t[:, h, :], n == 0, True, [Em, Vas[cur]], [po])
                            S.copy("act", O.t[:, half * 260:(half + 1) * 260], po.t[:, 0:260], [po], [O])
                        S.dma("sync", self.oa_d[g, tok0:tok0 + 127 * d + 1:d, :], O.t[:], reads=[O], writes=[self.oa_b[g]])

    def rwkv(self, l, uT, cf):
        S = self.S
        ST = 256
        NJ = 15
        with contextlib.ExitStack() as es:
            W = S.sbt(es, "Wrw", [128, 8, 1824], BF16)
            self.load_w(W, self.W["w_in"][l][:, 0:1824], 8)
            w2s = S.sbt(es, "w2s", [128, 512], BF16)
            a2s = S.sbt(es, "a2s", [128, 512], BF16)
            g2a = S.sbt(es, "g2a", [128, 512], BF16)
            g2b = S.sbt(es, "g2b", [128, 512], BF16)
            S.dma("pool", w2s.t[0:64, :], self.W["rw_w2"][l], writes=[w2s])
            S.dma("pool", a2s.t[64:128, :], self.W["rw_a2"][l], writes=[a2s])
            S.dma("pool", g2a.t[:, :], self.W["rw_g2"][l][0:128, :], writes=[g2a])
            S.dma("pool", g2b.t[0:32, :], self.W["rw_g2"][l][128:160, :], writes=[g2b])
            lnw = S.sbt(es, "lnw", [128, 512], F32)
            lnb = S.sbt(es, "lnb", [128, 512], F32)
            self.load_row(lnw, l, RP_LNW, 512)
            self.load_row(lnb, l, RP_LNB, 512)
            cp = self.colp
            omk = S.sbt(es, "omk", [128, 4], F32)
            S.ts("dve", omk.t[:], cp.t[:, CP_KA:CP_KA + 4], -1.0, ALU.mult, [cp], [omk], s2=1.0, op1=ALU.add)
            rkb = S.sbt(es, "rkb", [128, 32], BF16)
            S.copy("dve", rkb.t[:], cp.t[:, CP_RK:CP_RK + 32], [cp], [rkb])
            onesf = S.sbt(es, "onesf", [128, 128], F32)
            S.op("dve", lambda e: e.memset(onesf.t[:], 1.0), [], [onesf])
            St = S.sbt(es, "St", [128, 4, 128], F32)
            Sb = S.sbt(es, "Sb", [128, 4, 128], BF16)
            Stmp = S.sbt(es, "Stmp", [128, 4, 128], F32)
            prevc = S.sbt(es, "prevc", [128, NJ], F32)
            for t_ in (St, Sb, prevc):
                S.op("dve", lambda e: e.memset(t_.t[:], 0.0), [], [t_])
            PM = S.sbt(es, "PM", [128, NJ, ST], F32)
            Pjs = Rot([S.sbt(es, "Pj%d" % i, [128, ST + 1], F32) for i in range(2)])
            dts = Rot([S.sbt(es, "dt%d" % i, [128, ST], F32) for i in range(2)])
            f32t = lambda nm: S.sbt(es, nm, [128, ST], F32)
            sgT, aT, lwT, LT, LpT, GT, GiT, GpT, kkkT, nrT, kkT, tkT, kmT, beT = [f32t("rw%d" % i) for i in range(14)]
            thw = S.sbt(es, "thw", [128, ST], BF16)
            adb = S.sbt(es, "adb", [128, ST], BF16)
            sgd = S.sbt(es, "sgd", [128, 2, ST], BF16)
            sqk = S.sbt(es, "sqk", [128, ST], BF16)
            AR = S.sbt(es, "AR", [128, 4, 2, 2, 128], BF16)
            BT = S.sbt(es, "BT", [128, 4, ST], BF16)
            KT = S.sbt(es, "KT", [128, 4, ST], BF16)
            VT = S.sbt(es, "VT", [128, 4, ST], BF16)
            RK = S.sbt(es, "RK", [128, 4, ST], BF16)
            GC = S.sbt(es, "GC", [128, 4, 2], F32)
            BKtok = S.sbt(es, "BKtok", [128, 2, 512], BF16)
            Vtok = S.sbt(es, "Vtok", [128, 512], BF16)
            AXt = S.sbt(es, "AXt", [128, 8, 384], BF16)
            AYt = S.sbt(es, "AYt", [128, 8, 256], BF16)
            Bts = [S.sbt(es, "Bt%d" % i, [128, 8, 128], BF16) for i in range(2)]
            Ans = [S.sbt(es, "An%d" % i, [128, 8, 128], BF16) for i in range(2)]
            TTs = [S.sbt(es, "TT%d" % i, [128, 8, 128], BF16) for i in range(2)]
            Xb = S.sbt(es, "Xb", [128, 512], BF16)
            Ub = S.sbt(es, "Ub", [128, 512], BF16)
            ysq = S.sbt(es, "ysq", [128, 512], F32)
            y1 = S.sbt(es, "y1", [128, 512], F32)
            y2 = S.sbt(es, "y2", [128, 512], F32)
            yo = Rot([S.sbt(es, "yo%d" % i, [128, 512], BF16) for i in range(2)])
            m3 = self.cb.t[:, CB_M3:CB_M3 + 384]
            bo2 = self.cb.t[:, CB_BO2:CB_BO2 + 128]
            h8 = lambda ap: ap.rearrange("p (h e) -> p h e", h=8)
            c2 = lambda ap: ap.rearrange("p (c t) -> p c t", c=2)
            col = lambda off, j: cp.t[:, off + j:off + j + 1]
            for st in range(T // ST):
                tok0 = st * ST
                for jc in range(NJ):
                    M = 128 if jc < 14 else 32
                    pp = self.banks.next()
                    for kc in range(8):
                        S.mm(pp.t[0:M, 0:ST], W.t[:, kc, jc * 128:jc * 128 + M], uT.t[:, kc, tok0:tok0 + ST], kc == 0, kc == 7, [W, uT], [pp])
                    Pj = Pjs.next()
                    S.copy("act", Pj.t[0:M, 1:ST + 1], pp.t[0:M, 0:ST], [pp], [Pj])
                    S.copy("dve", Pj.t[0:M, 0:1], prevc.t[0:M, jc:jc + 1], [prevc], [Pj])
                    dtm = dts.next()
                    S.tt("dve", dtm.t[0:M, :], Pj.t[0:M, 0:ST], Pj.t[0:M, 1:ST + 1], ALU.subtract, [Pj], [dtm])
                    S.stt("dve", PM.t[0:M, jc, :], dtm.t[0:M, :], col(CP_MU, jc)[0:M], Pj.t[0:M, 1:ST + 1], ALU.mult, ALU.add, [dtm, Pj, cp], [PM])
                    S.copy("act", prevc.t[0:M, jc:jc + 1], Pj.t[0:M, ST:ST + 1], [Pj], [prevc])
                S.act(thw.t[0:64, :], PM.t[0:64, 12, :], AF.Tanh, [PM], [thw])
                S.copy("act", adb.t[64:128, :], PM.t[64:128, 12, :], [PM], [adb])
                S.act(sgd.t[:, 0, :], PM.t[:, 13, :], AF.Sigmoid, [PM], [sgd])
                S.act(sgd.t[0:32, 1, :], PM.t[0:32, 14, :], AF.Sigmoid, [PM], [sgd])
                for j in range(4):
                    rj, kj, vj = PM.t[:, j, :], PM.t[:, 4 + j, :], PM.t[:, 8 + j, :]
                    px, px2 = self.banks.next(), self.banks.next()
                    S.mm(px.t[:, 0:ST], w2s.t[0:64, j * 128:(j + 1) * 128], thw.t[0:64, :], True, True, [w2s, thw], [px])
                    S.mm(px2.t[:, ST:2 * ST], a2s.t[64:128, j * 128:(j + 1) * 128], adb.t[64:128, :], True, True, [a2s, adb], [px2])
                    S.act(sgT.t[:], px.t[:, 0:ST], AF.Sigmoid, [px, cp], [sgT], bias=col(CP_W0, j))
                    S.act(aT.t[:], px2.t[:, ST:2 * ST], AF.Sigmoid, [px2, cp], [aT], bias=col(CP_A0, j))
                    S.ts("dve", lwT.t[:], sgT.t[:], -0.6065306597126334, ALU.mult, [sgT], [lwT])
                    for ch in range(2):
                        S.op("dve", lambda e: e.tensor_tensor_scan(out=LT.t[:, ch * 128:(ch + 1) * 128], data0=onesf.t[:], data1=lwT.t[:, ch * 128:(ch + 1) * 128], initial=0.0, op0=ALU.mult, op1=ALU.add), [onesf, lwT], [LT])
                    S.tt("dve", LpT.t[:], LT.t[:], lwT.t[:], ALU.subtract, [LT, lwT], [LpT])
                    S.act(GT.t[:], LT.t[:], AF.Exp, [LT], [GT])
                    S.act(GiT.t[:], LT.t[:], AF.Exp, [LT], [GiT], scale=-1.0)
                    S.act(GpT.t[:], LpT.t[:], AF.Exp, [LpT], [GpT])
                    S.copy("dve", GC.t[:, j, :].rearrange("p (c e) -> p c e", e=1), c2(GT.t[:])[:, :, 127:128], [GT], [GC])
                    S.ts("dve", kkkT.t[:], kj, col(CP_KK, j), ALU.mult, [PM, cp], [kkkT])
                    S.act(sqk.t[:], kkkT.t[:], AF.Square, [kkkT], [sqk])
                    pss = self.banks.next()
                    S.mm(pss.t[:, 0:ST], bo2, sqk.t[:], True, True, [self.cb, sqk], [pss])
                    S.act(nrT.t[:], pss.t[:, 0:ST], AF.Sqrt, [pss], [nrT])
                    S.ts("dve", nrT.t[:], nrT.t[:], 1e-12, ALU.max, [nrT], [nrT])
                    S.op("dve", lambda e: e.reciprocal(out=nrT.t[:], in_=nrT.t[:]), [nrT], [nrT])
                    S.tt("dve", kkT.t[:], kkkT.t[:], nrT.t[:], ALU.mult, [kkkT, nrT], [kkT])
                    S.ts("dve", tkT.t[:], aT.t[:], col(CP_KA, j), ALU.mult, [aT, cp, omk], [tkT], s2=omk.t[:, j:j + 1], op1=ALU.add)
                    S.tt("dve", kmT.t[:], kj, tkT.t[:], ALU.mult, [PM, tkT], [kmT])
                    S.tt("dve", beT.t[:], kkT.t[:], aT.t[:], ALU.mult, [kkT, aT], [beT])
                    S.stt("dve", AR.t[:, j, :, 0, :], c2(kkT.t[:]), -1.0, c2(GpT.t[:]), ALU.mult, ALU.mult, [kkT, GpT], [AR])
                    S.tt("dve", AR.t[:, j, :, 1, :], c2(rj), c2(GT.t[:]), ALU.mult, [PM, GT], [AR])
                    S.tt("dve", BT.t[:, j, :], beT.t[:], GiT.t[:], ALU.mult, [beT, GiT], [BT])
                    S.tt("dve", KT.t[:, j, :], kmT.t[:], GiT.t[:], ALU.mult, [kmT, GiT], [KT])
                    S.copy("act", VT.t[:, j, :], vj, [PM], [VT])
                    S.tt("dve", RK.t[:, j, :], rj, kmT.t[:], ALU.mult, [PM, kmT], [RK])
                for ch in range(2):
                    ci = st * 2 + ch
                    tsl = slice(ch * 128, (ch + 1) * 128)
                    tb1, tb2 = self.tbanks.next(), self.tbanks.next()
                    for j in range(4):
                        S.tr(tb1.t[:, j * 128:(j + 1) * 128], BT.t[:, j, tsl], self.cb, [BT], [tb1])
                        S.tr(tb1.t[:, 512 + j * 128:512 + (j + 1) * 128], KT.t[:, j, tsl], self.cb, [KT], [tb1])
                        S.tr(tb2.t[:, j * 128:(j + 1) * 128], VT.t[:, j, tsl], self.cb, [VT], [tb2])
                    S.copy("act", BKtok.t[:].rearrange("p a (k t) -> p (a k) t", k=4), tb1.t[:].rearrange("p (k t) -> p k t", k=8), [tb1], [BKtok])
                    S.copy("act", Vtok.t[:].rearrange("p (k t) -> p k t", k=4), tb2.t[:, 0:512].rearrange("p (k t) -> p k t", k=4), [tb2], [Vtok])
                    pgb = self.banks.next()
                    S.mm(pgb.t[:], sgd.t[:, 0, tsl], g2a.t[:], True, False, [sgd, g2a], [pgb])
                    S.mm(pgb.t[:], sgd.t[0:32, 1, tsl], g2b.t[0:32, :], False, True, [sgd, g2b], [pgb])
                    S.copy("act", y2.t[:], pgb.t[:], [pgb], [y2])
                    pbn = self.banks.next()
                    for j in range(4):
                        S.mm(pbn.t[:, 0:8], RK.t[:, j, tsl], rkb.t[:, j * 8:(j + 1) * 8], j == 0, j == 3, [RK, rkb], [pbn])
                    smb = self.smalls.next()
                    S.copy("act", smb.t[:, 0:8], pbn.t[:, 0:8], [pbn], [smb])
                    for h in range(8):
                        j, pb = h // 2, 64 * (h % 2)
                        bx, by = self.banks.next(), self.banks.next()
                        arf = AR.t[pb:pb + 64, j, ch, :, :].rearrange("p a t -> p (a t)")
                        S.mm(bx.t[:, 0:256], BT.t[pb:pb + 64, j, tsl], arf, True, True, [BT, AR], [bx])
                        S.mm(bx.t[:, 256:384], AR.t[pb:pb + 64, j, ch, 0, :], BT.t[pb:pb + 64, j, tsl], True, True, [BT, AR], [bx])
                        S.mm(by.t[:, 0:256], KT.t[pb:pb + 64, j, tsl], arf, True, True, [KT, AR], [by])
                        S.tt("dve", AXt.t[:, h, :], bx.t[:, 0:384], m3, ALU.mult, [bx, self.cb], [AXt])
                        S.tt("dve", AYt.t[:, h, :], by.t[:, 0:256], m3[:, 0:256], ALU.mult, [by, self.cb], [AYt])
                    idb = self.cb.t[:, CB_ID:CB_ID + 128].rearrange("p (a t) -> p a t", a=1).broadcast_to([128, 8, 128])
                    S.tt("dve", TTs[0].t[:], AXt.t[:, :, 0:128], idb, ALU.add, [AXt, self.cb], [TTs[0]])
                    Btp = lambda h: AXt.t[:, h, 0:128]
                    Ap = lambda h: AXt.t[:, h, 256:384]
                    Btb, Ab = AXt, AXt
                    cur = 0
                    for n in range(1, 7):
                        Bn, An, Tn, Tp = Bts[n % 2], Ans[n % 2], TTs[(cur + 1) % 2], TTs[cur]
                        for grp in range(2):
                            gs = slice(grp * 4, grp * 4 + 4)
                            pA = self.banks.next()
                            pB = self.banks.next() if n < 6 else None
                            for hq in range(4):
                                h = grp * 4 + hq
                                if n < 6:
                                    S.mm(pB.t[:, hq * 128:(hq + 1) * 128], Ap(h), Btp(h), True, True, [Ab, Btb], [pB])
                                S.mm(pA.t[:, hq * 128:(hq + 1) * 128], Btp(h), Ap(h), True, True, [Ab, Btb], [pA])
                            S.copy("act", An.t[:, gs, :].rearrange("p a t -> p (a t)"), pA.t[:], [pA], [An])
                            if n < 6:
                                S.copy("dve", Bn.t[:, gs, :].rearrange("p a t -> p (a t)"), pB.t[:], [pB], [Bn])
                            pT = self.banks.next()
                            for hq in range(4):
                                h = grp * 4 + hq
                                S.mm(pT.t[:, hq * 128:(hq + 1) * 128], An.t[:, h, :], Tp.t[:, h, :], True, True, [An, Tp], [pT])
                            S.tt("dve", Tn.t[:, gs, :].rearrange("p a t -> p (a t)"), pT.t[:], Tp.t[:, gs, :].rearrange("p a t -> p (a t)"), ALU.add, [pT, Tp], [Tn])
                        Btp = (lambda B: (lambda h: B.t[:, h, :]))(Bn)
                        Ap = (lambda A: (lambda h: A.t[:, h, :]))(An)
                        Btb, Ab = Bn, An
                        cur = (cur + 1) % 2
                    TTf = TTs[cur]
                    for j in range(4):
                        S.ts("dve", Stmp.t[:, j, :], St.t[:, j, :], GC.t[:, j, ch:ch + 1], ALU.mult, [St, GC], [Stmp])
                    pX = self.banks.next()
                    for h in range(8):
                        j, pb = h // 2, 64 * (h % 2)
                        S.mm(pX.t[:, h * 64:(h + 1) * 64], AR.t[pb:pb + 64, j, ch, 0, :], Sb.t[pb:pb + 64, j, (h % 2) * 64:(h % 2 + 1) * 64], True, False, [AR, Sb], [pX])
                        S.mm(pX.t[:, h * 64:(h + 1) * 64], AYt.t[:, h, 0:128], Vtok.t[:, h * 64:(h + 1) * 64], False, True, [AYt, Vtok], [pX])
                    S.copy("act", Xb.t[:], pX.t[:], [pX], [Xb])
                    pU = self.banks.next()
                    for h in range(8):
                        S.mm(pU.t[:, h * 64:(h + 1) * 64], TTf.t[:, h, :], Xb.t[:, h * 64:(h + 1) * 64], True, True, [TTf, Xb], [pU])
                    S.copy("dve", Ub.t[:], pU.t[:], [pU], [Ub])
                    pY = self.banks.next()
                    for h in range(8):
                        j, pb = h // 2, 64 * (h % 2)
                        hs_ = slice(h * 64, (h + 1) * 64)
                        S.mm(pY.t[:, hs_], AR.t[pb:pb + 64, j, ch, 1, :], Sb.t[pb:pb + 64, j, (h % 2) * 64:(h % 2 + 1) * 64], True, False, [AR, Sb], [pY])
                        S.mm(pY.t[:, hs_], AXt.t[:, h, 128:256], Ub.t[:, hs_], False, False, [AXt, Ub], [pY])
                        S.mm(pY.t[:, hs_], AYt.t[:, h, 128:256], Vtok.t[:, hs_], False, True, [AYt, Vtok], [pY])
                    pS = self.banks.next()
                    for j in range(4):
                        js = slice(j * 128, (j + 1) * 128)
                        S.mm(pS.t[:, js], BKtok.t[:, 0, js], Ub.t[:, js], True, False, [BKtok, Ub], [pS])
                        S.mm(pS.t[:, js], BKtok.t[:, 1, js], Vtok.t[:, js], False, True, [BKtok, Vtok], [pS])
                    for j in range(4):
                        S.stt("dve", St.t[:, j, :], pS.t[:, j * 128:(j + 1) * 128], GC.t[:, j, ch:ch + 1], Stmp.t[:, j, :], ALU.mult, ALU.add, [pS, GC, Stmp], [St])
                    S.copy("act", Sb.t[:], St.t[:], [St], [Sb])
                    S.copy("dve", y1.t[:], pY.t[:], [pY], [y1])
                    S.act(ysq.t[:], y1.t[:], AF.Square, [y1], [ysq])
                    sm = self.smalls.next()
                    S.op("dve", lambda e: e.tensor_reduce(out=sm.t[:, 0:8], in_=h8(y1.t[:]), axis=AX.X, op=ALU.add), [y1], [sm])
                    sm2 = self.smalls.next()
                    S.op("dve", lambda e: e.tensor_reduce(out=sm2.t[:, 0:8], in_=h8(ysq.t[:]), axis=AX.X, op=ALU.add), [ysq], [sm2])
                    S.ts("dve", sm.t[:, 0:8], sm.t[:, 0:8], 1.0 / 64, ALU.mult, [sm], [sm])
                    S.tt("dve", sm.t[:, 8:16], sm.t[:, 0:8], sm.t[:, 0:8], ALU.mult, [sm], [sm])
                    S.stt("dve", sm2.t[:, 8:16], sm2.t[:, 0:8], 1.0 / 64, sm.t[:, 8:16], ALU.mult, ALU.subtract, [sm, sm2], [sm2])
                    S.act(sm2.t[:, 0:8], sm2.t[:, 8:16], AF.Sqrt, [sm2], [sm2], bias=GN_EPS)
                    S.op("dve", lambda e: e.reciprocal(out=sm2.t[:, 8:16], in_=sm2.t[:, 0:8]), [sm2], [sm2])
                    bc = lambda ap: ap.rearrange("p (h e) -> p h e", e=1).broadcast_to([128, 8, 64])
                    S.tt("dve", h8(y1.t[:]), h8(y1.t[:]), bc(sm.t[:, 0:8]), ALU.subtract, [y1, sm], [y1])
                    S.tt("dve", h8(y1.t[:]), h8(y1.t[:]), bc(sm2.t[:, 8:16]), ALU.mult, [y1, sm2], [y1])
                    S.tt("dve", y1.t[:], y1.t[:], lnw.t[:], ALU.mult, [y1, lnw], [y1])
                    S.tt("dve", y1.t[:], y1.t[:], lnb.t[:], ALU.add, [y1, lnb], [y1])
                    S.tt("dve", h8(ysq.t[:]), h8(Vtok.t[:]), bc(smb.t[:, 0:8]), ALU.mult, [Vtok, smb], [ysq])
                    S.tt("dve", y1.t[:], y1.t[:], ysq.t[:], ALU.add, [y1, ysq], [y1])
                    yb_ = yo.next()
                    S.tt("dve", yb_.t[:], y1.t[:], y2.t[:], ALU.mult, [y1, y2], [yb_])
                    S.dma("sync", self.ya_d[ci * 128:(ci + 1) * 128, :], yb_.t[:], reads=[yb_], writes=[self.ya_b[ci]])

    def retention(self, l, uT, cf):
        S = self.S
        with contextlib.ExitStack() as es:
            W = S.sbt(es, "Wret", [128, 8, 1536], BF16)
            self.load_w(W, self.W["w_in"][l][:, C_RET:C_RET + 1536], 8)
            gret = S.sbt(es, "gret", [128, 512], F32)
            self.load_row(gret, l, RP_RET, 512)
            Sr = S.sbt(es, "Sr", [128, 2, 256], F32)
            Srb = S.sbt(es, "Srb", [128, 2, 256], BF16)
            S.op("dve", lambda e: e.memset(Sr.t[:], 0.0), [], [Sr])
            S.op("dve", lambda e: e.memset(Srb.t[:], 0.0), [], [Srb])
            vbs = Rot([S.sbt(es, "vb%d" % i, [128, 512], BF16) for i in range(2)])
            sgs = Rot([S.sbt(es, "sg%d" % i, [128, 512], F32) for i in range(2)])
            qks = Rot([S.sbt(es, "qk%d" % i, [128, 2, 4, 64], F32) for i in range(2)])
            qkrs = Rot([S.sbt(es, "qkr%d" % i, [128, 2, 4, 64], F32) for i in range(2)])
            tmps = Rot([S.sbt(es, "rt%d" % i, [128, 4, 4, 32], F32) for i in range(2)])
            qts = Rot([S.sbt(es, "qt%d" % i, [128, 3, 256], BF16) for i in range(2)])
            qkTs = Rot([S.sbt(es, "qkT%d" % i, [128, 4, 128], BF16) for i in range(2)])
            sTms = Rot([S.sbt(es, "sTm%d" % i, [128, 4, 128], BF16) for i in range(2)])
            ysqs = Rot([S.sbt(es, "ysq%d" % i, [128, 512], F32) for i in range(2)])
            yns = Rot([S.sbt(es, "yn%d" % i, [128, 512], F32) for i in range(2)])
            ycs = Rot([S.sbt(es, "yc%d" % i, [128, 512], BF16) for i in range(2)])
            Dp = cf.t[:, CF_DP:CF_DP + 512]
            import os
            rc = int(os.environ.get('RET_CUT', '99'))
            for c in range(int(os.environ.get('RET_NCH', NT))):
                tsl = slice(c * 128, (c + 1) * 128)
                if os.environ.get("RET_VAR", "") == "only_tr":
                    tb = self.tbanks.next()
                    for a in range(4):
                        S.tr(tb.t[:, a * 128:(a + 1) * 128], self.junk.t[:, a * 128:(a + 1) * 128], self.cb, [self.junk], [tb])
                    continue
                if os.environ.get("RET_VAR", "") == "mm_tr":
                    pqk = self.banks.next()
                    for kc in range(8):
                        S.mm(pqk.t[:], uT.t[:, kc, tsl], W.t[:, kc, 0:512], kc == 0, kc == 7, [uT, W], [pqk])
                    tb = self.tbanks.next()
                    for a in range(4):
                        S.tr(tb.t[:, a * 128:(a + 1) * 128], self.junk.t[:, a * 128:(a + 1) * 128], self.cb, [self.junk], [tb])
                    continue
                pqk, pv, pg = self.banks.next(), self.banks.next(), self.banks.next()
                for pp, c0 in ((pqk, 0), (pv, 512), (pg, 1024)):
                    for kc in range(8):
                        S.mm(pp.t[:], uT.t[:, kc, tsl], W.t[:, kc, c0:c0 + 512], kc == 0, kc == 7, [uT, W], [pp])
                vb = vbs.next()
                S.copy("act", vb.t[:], pv.t[:], [pv], [vb])
                sg = sgs.next()
                S.act(sg.t[:], pg.t[:], AF.Silu, [pg], [sg])
                qk = qks.next()
                S.copy("dve", qk.t[:].rearrange("p a h e -> p (a h e)"), pqk.t[:], [pqk], [qk])
                if rc < 1:
                    continue
                qkr = qkrs.next()
                tm = tmps.next()
                cosb = cf.t[:, CF_RCOS + c * 32:CF_RCOS + (c + 1) * 32].rearrange("p (a e) -> p a e", a=1).broadcast_to([128, 4, 32])
                sinb = cf.t[:, CF_RSIN + c * 32:CF_RSIN + (c + 1) * 32].rearrange("p (a e) -> p a e", a=1).broadcast_to([128, 4, 32])
                for w in range(2):
                    x1 = qk.t[:, w, :, 0:32]
                    x2 = qk.t[:, w, :, 32:64]
                    eng = "dve"
                    S.tt(eng, tm.t[:, 0], x1, cosb, ALU.mult, [qk, cf], [tm])
                    S.tt(eng, tm.t[:, 1], x2, sinb, ALU.mult, [qk, cf], [tm])
                    S.tt(eng, tm.t[:, 2], x2, cosb, ALU.mult, [qk, cf], [tm])
                    S.tt(eng, tm.t[:, 3], x1, sinb, ALU.mult, [qk, cf], [tm])
                    S.tt(eng, qkr.t[:, w, :, 0:32], tm.t[:, 0], tm.t[:, 1], ALU.subtract, [tm], [qkr])
                    S.tt(eng, qkr.t[:, w, :, 32:64], tm.t[:, 2], tm.t[:, 3], ALU.add, [tm], [qkr])
                if rc < 2:
                    continue
                qt = qts.next()
                qdb = cf.t[:, CF_QD:CF_QD + 4].rearrange("p (h e) -> p h e", e=1).broadcast_to([128, 4, 64])
                kdb = cf.t[:, CF_KD:CF_KD + 4].rearrange("p (h e) -> p h e", e=1).broadcast_to([128, 4, 64])
                v3 = lambda ap: ap.rearrange("p (h e) -> p h e", h=4)
                S.tt("dve", v3(qt.t[:, 0, :]), qkr.t[:, 0], qdb, ALU.mult, [qkr, cf], [qt])
                S.copy("act", v3(qt.t[:, 1, :]), qkr.t[:, 1], [qkr], [qt])
                S.tt("dve", v3(qt.t[:, 2, :]), qkr.t[:, 1], kdb, ALU.mult, [qkr, cf], [qt])
                if rc < 3:
                    continue
                tb = self.tbanks.next()
                rv = os.environ.get("RET_VAR", "")
                for a in range(2):
                    for jj in range(2):
                        if rv == "notr" or (rv == "tr1" and (a or jj)):
                            continue
                        src_ = qt.t[:, a, jj * 128:(jj + 1) * 128]
                        if rv == "xnsrc":
                            src_ = self.junk.t[:, (a * 2 + jj) * 128:(a * 2 + jj + 1) * 128]
                        S.tr(tb.t[:, (a * 2 + jj) * 128:(a * 2 + jj + 1) * 128], src_, self.cb, [qt], [tb])
                qkT = qkTs.next()
                if rv != "nocopy":
                    S.copy("act", qkT.t[:, 0:4, :], tb.t[:, 0:512].rearrange("p (k t) -> p k t", k=4), [tb], [qkT])
                if rc < 4:
                    continue
                psr = (self.banks.next(), self.banks.next())
                for h in range(4):
                    jj, pb = h // 2, 64 * (h % 2)
                    S.mm(psr[h % 2].t[:, h * 128:(h + 1) * 128], qkT.t[pb:pb + 64, 2 + jj, :], qkT.t[pb:pb + 64, jj, :], True, True, [qkT], [psr[h % 2]])
                sTm = sTms.next()
                for h in range(4):
                    S.tt("dve", sTm.t[:, h, :], psr[h % 2].t[:, h * 128:(h + 1) * 128], Dp[:, h * 128:(h + 1) * 128], ALU.mult, [psr[h % 2], cf], [sTm])
                if rc < 5:
                    continue
                po = self.banks.next()
                for h in range(4):
                    jj, pb = h // 2, 64 * (h % 2)
                    S.mm(po.t[:, h * 128:(h + 1) * 128], sTm.t[:, h, :], vb.t[:, h * 128:(h + 1) * 128], True, False, [sTm, vb], [po])
                    S.mm(po.t[:, h * 128:(h + 1) * 128], qkT.t[pb:pb + 64, jj, :], Srb.t[pb:pb + 64, jj, (h % 2) * 128:(h % 2 + 1) * 128], False, True, [qkT, Srb], [po])
                if rc < 6:
                    continue
                pkv = self.banks.next()
                for jj in range(2):
                    S.mm(pkv.t[:, jj * 256:(jj + 1) * 256], qt.t[:, 2, jj * 128:(jj + 1) * 128], vb.t[:, jj * 256:(jj + 1) * 256], True, True, [qt, vb], [pkv])
                for jj in range(2):
                    S.stt("dve", Sr.t[:, jj, :], Sr.t[:, jj, :], cf.t[:, CF_GC + jj:CF_GC + jj + 1], pkv.t[:, jj * 256:(jj + 1) * 256], ALU.mult, ALU.add, [Sr, cf, pkv], [Sr])
                S.copy("act", Srb.t[:], Sr.t[:], [Sr], [Srb])
                if rc < 7:
                    continue
                ysq = ysqs.next()
                S.act(ysq.t[:], po.t[:], AF.Square, [po], [ysq])
                sm = self.smalls.next()
                S.op("dve", lambda e: e.tensor_reduce(out=sm.t[:, 0:4], in_=ysq.t[:].rearrange("p (h e) -> p h e", h=4), axis=AX.X, op=ALU.add), [ysq], [sm])
                S.act(sm.t[:, 4:8], sm.t[:, 0:4], AF.Sqrt, [sm], [sm], scale=1.0 / 128, bias=EPS)
                S.op("dve", lambda e: e.reciprocal(out=sm.t[:, 8:12], in_=sm.t[:, 4:8]), [sm], [sm])
                yn = yns.next()
                S.tt("dve", v3(yn.t[:]), v3(po.t[:]), sm.t[:, 8:12].rearrange("p (h e) -> p h e", e=1).broadcast_to([128, 4, 128]), ALU.mult, [po, sm], [yn])
                S.tt("dve", yn.t[:], yn.t[:], gret.t[:], ALU.mult, [yn, gret], [yn])
                yc = ycs.next()
                S.tt("dve", yc.t[:], yn.t[:], sg.t[:], ALU.mult, [yn, sg], [yc])
                S.dma("sync", self.yc_d[tsl, :], yc.t[:], reads=[yc], writes=[self.yc_b[c]])

    def merge(self, l, uT):
        S = self.S
        with contextlib.ExitStack() as es:
            Wg = S.sbt(es, "Wg", [128, 8, 3072], BF16)
            Wb = S.sbt(es, "Wb", [128, 12, D], BF16)
            Wo = S.sbt(es, "Wo", [128, 8, D], BF16)
            self.load_w(Wg, self.W["w_in"][l][:, C_GATE:C_GATE + 3072], 8)
            for b, nm in enumerate(("w_branch_rwkv", "w_branch_dil", "w_branch_ret")):
                for kc in range(4):
                    S.dma("pool", Wb.t[:, b * 4 + kc, :], self.W[nm][l][kc * 128:(kc + 1) * 128, :], writes=[Wb])
            self.load_w(Wo, self.W["w_out"][l], 8)
            yas = Rot([S.sbt(es, "ya%d" % i, [128, 512], BF16) for i in range(2)])
            ycs = Rot([S.sbt(es, "yc%d" % i, [128, 512], BF16) for i in range(2)])
            ybs = Rot([S.sbt(es, "yb%d" % i, [128, 512], BF16) for i in range(2)])
            oas = Rot([S.sbt(es, "oa%d" % i, [128, 3, 520], F32) for i in range(2)])
            hts = Rot([S.sbt(es, "ht%d" % i, [128, D], F32) for i in range(2)])
            yTs = Rot([S.sbt(es, "yT%d" % i, [128, 12, 128], BF16) for i in range(2)])
            mTs = Rot([S.sbt(es, "mT%d" % i, [128, 8, 128], BF16) for i in range(2)])
            sigs = Rot([S.sbt(es, "sig%d" % i, [128, 384], F32) for i in range(2)])
            prods = Rot([S.sbt(es, "prod%d" % i, [128, 384], F32) for i in range(2)])
            for ti in range(NT):
                tsl = slice(ti * 128, (ti + 1) * 128)
                ya, yc, yb, oa, ht = yas.next(), ycs.next(), ybs.next(), oas.next(), hts.next()
                S.dma("sync", ya.t[:], self.ya_d[tsl, :], reads=[self.ya_b[ti]], writes=[ya])
                S.dma("sync", yc.t[:], self.yc_d[tsl, :], reads=[self.yc_b[ti]], writes=[yc])
                for g in range(3):
                    S.dma("sync", oa.t[:, g, :], self.oa_d[g, tsl, :], reads=[self.oa_b[g]], writes=[oa])
                S.dma("sync", ht.t[:], self.out[tsl, :], reads=[self.hbuf[ti]], writes=[ht])
                S.tt("dve", oa.t[:, 0, :], oa.t[:, 0, :], oa.t[:, 1, :], ALU.add, [oa], [oa])
                S.tt("dve", oa.t[:, 0, :], oa.t[:, 0, :], oa.t[:, 2, :], ALU.add, [oa], [oa])
                o3 = oa.t[:, 0, :].rearrange("p (h e) -> p h e", h=8)
                sm = self.smalls.next()
                S.op("dve", lambda e: e.reciprocal(out=sm.t[:, 0:8].rearrange("p (h e) -> p h e", e=1), in_=o3[:, :, 64:65]), [oa], [sm])
                S.tt("dve", yb.t[:].rearrange("p (h e) -> p h e", h=8), o3[:, :, 0:64], sm.t[:, 0:8].rearrange("p (h e) -> p h e", e=1).broadcast_to([128, 8, 64]), ALU.mult, [oa, sm], [yb])
                yT = yTs.next()
                tb0, tb1 = self.tbanks.next(), self.tbanks.next()
                for kc in range(4):
                    S.tr(tb0.t[:, kc * 128:(kc + 1) * 128], ya.t[:, kc * 128:(kc + 1) * 128], self.cb, [ya], [tb0])
                for kc in range(4):
                    S.tr(tb0.t[:, 512 + kc * 128:512 + (kc + 1) * 128], yb.t[:, kc * 128:(kc + 1) * 128], self.cb, [yb], [tb0])
                for kc in range(4):
                    S.tr(tb1.t[:, kc * 128:(kc + 1) * 128], yc.t[:, kc * 128:(kc + 1) * 128], self.cb, [yc], [tb1])
                S.copy("act", yT.t[:, 0:8, :], tb0.t[:].rearrange("p (k t) -> p k t", k=8), [tb0], [yT])
                S.copy("act", yT.t[:, 8:12, :], tb1.t[:, 0:512].rearrange("p (k t) -> p k t", k=4), [tb1], [yT])
                mT = mTs.next()
                for oc in range(8):
                    pg_, pz = self.banks.next(), self.banks.next()
                    for b in range(3):
                        for kc in range(8):
                            S.mm(pg_.t[:, b * 128:(b + 1) * 128], Wg.t[:, kc, b * 1024 + oc * 128:b * 1024 + (oc + 1) * 128], uT.t[:, kc, tsl], kc == 0, kc == 7, [Wg, uT], [pg_])
                    for b in range(3):
                        for kc in range(4):
                            S.mm(pz.t[:, b * 128:(b + 1) * 128], Wb.t[:, b * 4 + kc, oc * 128:(oc + 1) * 128], yT.t[:, b * 4 + kc, :], kc == 0, kc == 3, [Wb, yT], [pz])
                    sig, prod = sigs.next(), prods.next()
                    S.act(sig.t[:], pg_.t[:, 0:384], AF.Sigmoid, [pg_], [sig])
                    S.tt("dve", prod.t[:], sig.t[:], pz.t[:, 0:384], ALU.mult, [sig, pz], [prod])
                    S.tt("dve", prod.t[:, 0:128], prod.t[:, 0:128], prod.t[:, 128:256], ALU.add, [prod], [prod])
                    S.tt("dve", mT.t[:, oc, :], prod.t[:, 0:128], prod.t[:, 256:384], ALU.add, [prod], [mT])
                for half in range(2):
                    po = self.banks.next()
                    for kc in range(8):
                        S.mm(po.t[:], mT.t[:, kc, :], Wo.t[:, kc, half * 512:(half + 1) * 512], kc == 0, kc == 7, [mT, Wo], [po])
                    hsl = ht.t[:, half * 512:(half + 1) * 512]
                    S.tt("dve", hsl, po.t[:], hsl, ALU.add, [po, ht], [ht])
                S.dma("sync", self.out[tsl, :], ht.t[:], reads=[ht], writes=[self.hbuf[ti]])


_CACHE = {}


def kernel(**inputs):
    stages = inputs.pop("_stages", None)
    debug = inputs.pop("_debug", False)
    ncores = inputs.pop("_ncores", 8)
    key = (None if stages is None else tuple(stages), debug)
    if key not in _CACHE:
        _CACHE[key] = Prog(stages, debug)
    prog = _CACHE[key]
    inp = {k: np.ascontiguousarray(np.asarray(v, dtype=np.float32)) for k, v in inputs.items()}
    colp, rowp = host_layout_params(inp)
    cb, cf = host_constants()
    in_maps = []
    for b in range(ncores):
        m = {"x": inp["x"][b], "mem": inp["mem"][b], "colp": colp, "rowp": rowp, "cb": cb, "cf": cf}
        for n, _ in WEIGHT_SPECS:
            m[n] = inp[n]
        in_maps.append(m)
    res = run_bass_kernel_spmd(prog.nc, in_maps, core_ids=list(range(ncores)))
    out = np.stack([np.asarray(r["out"], dtype=np.float32) for r in res.results], axis=0)
    if debug:
        return out, res.results
    return out
```

```python
import contextlib
import numpy as np
import concourse.bass as bass
import concourse.mybir as mybir
from concourse.bass_utils import run_bass_kernel_spmd

F32 = mybir.dt.float32
BF16 = mybir.dt.bfloat16
AF = mybir.ActivationFunctionType
ALU = mybir.AluOpType
AX = mybir.AxisListType


class Buf:
    def __init__(self, name, t=None, unordered=False):
        self.name = name
        self.t = t
        self.writers = {}
        self.readers = {}
        self.unordered = unordered


class _Eng:
    def __init__(self, name, eng, sem, step):
        self.name, self.eng, self.sem, self.step = name, eng, sem, step
        self.count = 0
        self.seen = {}


class Sched:
    N_DMA_SEMS = 12
    same_engine_sync = True

    def __init__(self, nc):
        self.nc = nc
        self.engs = {}
        self.queues = {}
        self._stack = contextlib.ExitStack()
        for name, eng in (("pe", nc.tensor), ("act", nc.scalar), ("dve", nc.vector), ("pool", nc.gpsimd),
                          ("sync", nc.sync)):
            sem = nc.alloc_semaphore("s_" + name)
            self.engs[name] = _Eng(name, eng, sem, 1)
        self.dma_pool = {}
        self.dma_rr = {}
        for q in ("sync", "pool"):
            lst = []
            for i in range(self.N_DMA_SEMS):
                nm = "d_%s%d" % (q, i)
                sem = nc.alloc_semaphore(nm)
                self.engs[nm] = _Eng(nm, None, sem, 16)
                lst.append(nm)
            self.dma_pool[q] = lst
            self.dma_rr[q] = 0
        self.n_ops = 0

    def sb(self, name, shape, dtype, unordered=False):
        t = self.nc.alloc_sbuf_tensor(name, list(shape), dtype)
        return Buf(name, t, unordered)

    def ps(self, name, shape, dtype):
        t = self.nc.alloc_psum_tensor(name, list(shape), dtype)
        return Buf(name, t)

    def block(self):
        return contextlib.nullcontext()

    def _waits(self, E, reads, writes):
        waits = {}
        for b in reads:
            for en, c in b.writers.items():
                if waits.get(en, 0) < c:
                    waits[en] = c
        for b in writes:
            if not b.unordered:
                for en, c in b.writers.items():
                    if waits.get(en, 0) < c:
                        waits[en] = c
            for en, c in b.readers.items():
                if waits.get(en, 0) < c:
                    waits[en] = c
        for en, c in waits.items():
            if E.seen.get(en, 0) >= c:
                continue
            X = self.engs[en]
            E.eng.wait_ge(X.sem, c * X.step)
            E.seen[en] = c

    def op(self, ename, fn, reads=(), writes=()):
        E = self.engs[ename]
        self._waits(E, reads, writes)
        ins = fn(E.eng)
        E.count += 1
        ins.then_inc(E.sem, 1)
        if not (self.same_engine_sync and ename != 'pe'):
            E.seen[ename] = E.count
        for b in reads:
            b.readers[ename] = E.count
        for b in writes:
            if b.unordered:
                b.writers[ename] = E.count
            else:
                b.writers = {ename: E.count}
                b.readers = {}
        self.n_ops += 1
        return ins

    def dma(self, q, out, in_, reads=(), writes=()):
        Q = self.engs[q]
        pool = self.dma_pool[q]
        vn = pool[self.dma_rr[q] % len(pool)]
        self.dma_rr[q] += 1
        V = self.engs[vn]
        if V.count > 0 and Q.seen.get(vn, 0) < V.count:
            Q.eng.wait_ge(V.sem, V.count * 16)
            Q.seen[vn] = V.count
        self._waits(Q, reads, writes)
        ins = Q.eng.dma_start(out=out, in_=in_)
        V.count += 1
        ins.then_inc(V.sem, 16)
        for b in reads:
            b.readers[vn] = V.count
        for b in writes:
            if b.unordered:
                b.writers[vn] = V.count
            else:
                b.writers = {vn: V.count}
                b.readers = {}
        self.n_ops += 1
        return ins

    def finish(self):
        Q = self.engs["sync"]
        for en, X in self.engs.items():
            if en == "sync" or X.count == 0:
                continue
            if Q.seen.get(en, 0) < X.count:
                Q.eng.wait_ge(X.sem, X.count * X.step)
                Q.seen[en] = X.count


    def barrier(self):
        names = ["pe", "act", "dve", "pool", "sync"]
        for en in names:
            E = self.engs[en]
            for xn, X in self.engs.items():
                if xn == en or X.count == 0:
                    continue
                if E.seen.get(xn, 0) < X.count:
                    E.eng.wait_ge(X.sem, X.count * X.step)
                    E.seen[xn] = X.count

    def sbt(self, es, name, shape, dtype, unordered=False):
        self._uid = getattr(self, "_uid", 0) + 1
        name = "sb%d_%s" % (self._uid, name)
        t = es.enter_context(self.nc.sbuf_tensor(name, list(shape), dtype))
        return Buf(name, t, unordered)

    def mm(self, out, lhsT, rhs, start, stop, reads, writes):
        return self.op("pe", lambda e: e.matmul(out, lhsT=lhsT, rhs=rhs, start=start, stop=stop), reads, writes)

    def tr(self, out, in_, ident, reads, writes):
        return self.op("pe", lambda e: e.transpose(out, in_, ident.t[:, 0:128]), list(reads) + [ident], writes)

    def act(self, out, in_, func, reads, writes, **kw):
        return self.op("act", lambda e: e.activation(out=out, in_=in_, func=func, **kw), reads, writes)

    def copy(self, eng, out, in_, reads, writes):
        if eng == "act":
            return self.op("act", lambda e: e.activation(out=out, in_=in_, func=AF.Copy), reads, writes)
        return self.op(eng, lambda e: e.tensor_copy(out=out, in_=in_), reads, writes)

    def tt(self, eng, out, in0, in1, op, reads, writes):
        return self.op(eng, lambda e: e.tensor_tensor(out=out, in0=in0, in1=in1, op=op), reads, writes)

    def ts(self, eng, out, in0, s1, op0, reads, writes, s2=None, op1=None):
        if op1 is None:
            return self.op(eng, lambda e: e.tensor_scalar(out=out, in0=in0, scalar1=s1, scalar2=None, op0=op0), reads, writes)
        return self.op(eng, lambda e: e.tensor_scalar(out=out, in0=in0, scalar1=s1, scalar2=s2, op0=op0, op1=op1), reads, writes)

    def stt(self, eng, out, in0, scalar, in1, op0, op1, reads, writes):
        return self.op(eng, lambda e: e.scalar_tensor_tensor(out=out, in0=in0, scalar=scalar, in1=in1, op0=op0, op1=op1), reads, writes)


def interleave(gens):
    gens = [g for g in gens if g is not None]
    while gens:
        for g in list(gens):
            try:
                next(g)
            except StopIteration:
                gens.remove(g)


class Rot:
    def __init__(self, items):
        self.items = list(items)
        self.i = 0

    def next(self):
        b = self.items[self.i % len(self.items)]
        self.i += 1
        return b


T = 4096
D = 1024
DFF = 2816
MEM = 256
NT = T // 128
EPS = 1e-6
GN_EPS = 64e-5
C_RW, C_DIL, C_RET, C_GATE = 0, 1824, 1824 + 4608, 1824 + 4608 + 1536
DILS = (1, 4, 16)
CP_MU, CP_W0, CP_A0, CP_KK, CP_KA, CP_XQ, CP_XK, CP_RK, CP_N = 0, 15, 19, 23, 27, 31, 33, 35, 67
RP_FFN1, RP_MIX, RP_XA, RP_MEM, RP_FFN2 = 0, 1024, 2048, 3072, 4096
RP_LNW, RP_LNB, RP_DQ, RP_DK, RP_RET, RP_N = 5120, 5632, 6144, 7680, 9216, 9728
CB_ID, CB_ONES, CB_BO2, CB_M3, CB_MPC, CB_N = 0, 128, 256, 384, 768, 1280
CF_DP, CF_QD, CF_KD, CF_GC, CF_DCOS, CF_DSIN, CF_RCOS, CF_RSIN, CF_N = 0, 512, 516, 520, 522, 1290, 2058, 3082, 4106


def host_constants():
    p = np.arange(128)
    cb = np.zeros((128, CB_N), np.float32)
    cb[:, CB_ID:CB_ID + 128] = np.eye(128)
    cb[:, CB_ONES:CB_ONES + 128] = 1.0
    cb[:, CB_BO2:CB_BO2 + 128] = (p[:, None] // 64 == p[None, :] // 64)
    cb[:, CB_M3:CB_M3 + 128] = (p[:, None] < p[None, :])
    cb[:, CB_M3 + 128:CB_M3 + 256] = (p[:, None] <= p[None, :])
    cb[:, CB_M3 + 256:CB_M3 + 384] = (p[None, :] < p[:, None])
    for hh in range(2):
        cb[:, CB_MPC + hh * 256:CB_MPC + hh * 256 + 128] = (p[:, None] >= p[None, :])
        cb[:, CB_MPC + hh * 256 + 128:CB_MPC + hh * 256 + 256] = (p[:, None] <= p[None, :])
    cf = np.zeros((128, CF_N), np.float64)
    gam = 1.0 - 2.0 ** (-5.0 - np.arange(4))
    for h in range(4):
        cf[:, CF_DP + h * 128:CF_DP + (h + 1) * 128] = (gam[h] ** (-(p[:, None] + 1.0))) * (p[:, None] <= p[None, :]) / 8.0
        cf[:, CF_QD + h] = gam[h] ** (p + 1.0)
        cf[:, CF_KD + h] = gam[h] ** (127.0 - p) / 8.0
    for j in range(2):
        cf[:, CF_GC + j] = gam[2 * j + p // 64] ** 128.0
    invf = 500000.0 ** (-np.arange(8, dtype=np.float64) / 8.0)
    for g, d in enumerate(DILS):
        nblk = 32 // d
        for c in range(d):
            for n in range(nblk):
                combo = c * nblk + n
                pos = ((n * 128 + p) * d + c).astype(np.float32)
                ang = (pos[:, None] * invf[None, :].astype(np.float32)).astype(np.float32)
                o = (g * 32 + combo) * 8
                cf[:, CF_DCOS + o:CF_DCOS + o + 8] = np.cos(ang)
                cf[:, CF_DSIN + o:CF_DSIN + o + 8] = np.sin(ang)
    invr = (10000.0 ** (-np.arange(32, dtype=np.float64) / 32.0)).astype(np.float32)
    for c in range(32):
        pos = (c * 128 + p).astype(np.float32)
        ang = (pos[:, None] * invr[None, :]).astype(np.float32)
        cf[:, CF_RCOS + c * 32:CF_RCOS + (c + 1) * 32] = np.cos(ang)
        cf[:, CF_RSIN + c * 32:CF_RSIN + (c + 1) * 32] = np.sin(ang)
    return cb.astype(np.float32), cf.astype(np.float32)


def host_layout_params(inp):
    L = 2
    colp = np.zeros((L, 128, CP_N), np.float32)
    rowp = np.zeros((L, RP_N), np.float32)
    p = np.arange(128)
    for l in range(L):
        mu = np.zeros(15 * 128, np.float32)
        mu[:1824] = inp["rw_mu"][l]
        colp[l, :, CP_MU:CP_MU + 15] = mu.reshape(15, 128).T
        for off, key in ((CP_W0, "rw_w0"), (CP_A0, "rw_a0"), (CP_KK, "rw_k_k"), (CP_KA, "rw_k_a")):
            colp[l, :, off:off + 4] = inp[key][l].reshape(4, 128).T
        colp[l, :, CP_XQ:CP_XQ + 2] = inp["xa_q_norm"][l].reshape(2, 128).T
        colp[l, :, CP_XK:CP_XK + 2] = inp["xa_k_norm"][l].reshape(2, 128).T
        rk = inp["rw_r_k"][l]
        for j in range(4):
            for h in range(8):
                if h // 2 == j:
                    rows = p[(p // 64) == (h % 2)]
                    colp[l, rows, CP_RK + j * 8 + h] = rk[h * 64 + (rows % 64)]
        for off, key in ((RP_FFN1, "norm_ffn1"), (RP_MIX, "norm_mix"), (RP_XA, "norm_xattn"), (RP_MEM, "norm_mem"),
                         (RP_FFN2, "norm_ffn2")):
            rowp[l, off:off + 1024] = inp[key][l]
        rowp[l, RP_LNW:RP_LNW + 512] = inp["rw_ln_w"][l]
        rowp[l, RP_LNB:RP_LNB + 512] = inp["rw_ln_b"][l]
        for g in range(3):
            rowp[l, RP_DQ + g * 512:RP_DQ + (g + 1) * 512] = np.tile(inp["dil_q_norm"][l, g], 8)
            rowp[l, RP_DK + g * 512:RP_DK + (g + 1) * 512] = np.tile(inp["dil_k_norm"][l, g], 8)
        rowp[l, RP_RET:RP_RET + 512] = inp["ret_norm"][l]
    return colp, rowp


WEIGHT_SPECS = [
    ("ffn1_w13", [2, D, 2 * DFF]), ("ffn1_w2", [2, DFF, D]), ("w_in", [2, D, 11040]),
    ("rw_w2", [2, 64, 512]), ("rw_a2", [2, 64, 512]), ("rw_g2", [2, 160, 512]),
    ("w_branch_rwkv", [2, 512, D]), ("w_branch_dil", [2, 512, D]), ("w_branch_ret", [2, 512, D]),
    ("w_out", [2, D, D]), ("xa_wq", [2, D, D]), ("xa_wkv", [2, D, 2 * D]), ("xa_wo", [2, D, D]),
    ("ffn2_w13", [2, D, 2 * DFF]), ("ffn2_w2", [2, DFF, D]),
]


class Prog:
    def __init__(self, stages=None, debug=False):
        self.stages = stages
        nc = self.nc = bass.Bass("TRN2", target_bir_lowering=False)
        self.S = S = Sched(nc)
        dt = lambda n, sh, ty=F32, kind="ExternalInput": nc.dram_tensor(n, sh, ty, kind=kind).ap()
        self.x = dt("x", [T, D])
        self.mem = dt("mem", [MEM, D])
        self.W = {n: dt(n, sh) for n, sh in WEIGHT_SPECS}
        self.colp_d = dt("colp", [2, 128, CP_N])
        self.rowp_d = dt("rowp", [2, RP_N])
        self.cb_d = dt("cb", [128, CB_N])
        self.cf_d = dt("cf", [128, CF_N])
        self.out = dt("out", [T, D], F32, "ExternalOutput")
        okind = "ExternalOutput" if debug else "Internal"
        self.ya_d = dt("ya_s", [T, 512], BF16, okind)
        self.yc_d = dt("yc_s", [T, 512], BF16, okind)
        self.oa_d = dt("oa_s", [3, T, 520], F32, okind)
        self.hbuf = [Buf("h%d" % i) for i in range(NT)]
        self.ya_b = [Buf("ya%d" % i) for i in range(NT)]
        self.yc_b = [Buf("yc%d" % i) for i in range(NT)]
        self.oa_b = [Buf("oa%d" % g, unordered=True) for g in range(3)]
        self.banks = Rot([S.ps("pb%d" % i, [128, 512], F32) for i in range(6)])
        self.tbanks = Rot([S.ps("pt%d" % i, [128, 1024], BF16) for i in range(2)])
        with contextlib.ExitStack() as es:
            self.cb = S.sbt(es, "cb", [128, CB_N], BF16)
            S.dma("pool", self.cb.t[:], self.cb_d, writes=[self.cb])
            self.ident = self.cb
            self.colp = S.sbt(es, "colp", [128, CP_N], F32)
            self.smalls = Rot([S.sbt(es, "sm%d" % i, [128, 16], F32) for i in range(8)])
            self.junk = S.sbt(es, "junk", [128, 1024], BF16)
            self.xn = Rot([S.sbt(es, "xn%d" % i, [128, 1024], BF16) for i in range(2)])
            for l in range(2):
                self.l = l
                S.barrier()
                S.dma("sync", self.colp.t[:], self.colp_d[l], writes=[self.colp])
                self.run("cp", self.copy_x)
                self.run("ffn1", lambda: self.ffn(es, l, RP_FFN1, self.W["ffn1_w13"][l], self.W["ffn1_w2"][l], first=(l == 0)))
                self.run("mix", lambda: self.mixer(es, l))
                self.run("xa", lambda: self.xattn(es, l))
                self.run("ffn2", lambda: self.ffn(es, l, RP_FFN2, self.W["ffn2_w13"][l], self.W["ffn2_w2"][l], first=False))
            S.barrier()
            S.finish()

    def run(self, name, fn):
        key = "%s%d" % (name, self.l)
        if self.stages is not None and key not in self.stages:
            return
        if self.stages is None and name == "cp":
            return
        fn()
        self.S.barrier()

    def copy_x(self):
        with contextlib.ExitStack() as es:
            hts = Rot([self.S.sbt(es, "cpx%d" % i, [128, D], F32) for i in range(2)])
            for ti in range(NT):
                ht = hts.next()
                self.S.dma("sync", ht.t[:], self.x[ti * 128:(ti + 1) * 128, :], writes=[ht])
                self.S.dma("sync", self.out[ti * 128:(ti + 1) * 128, :], ht.t[:], reads=[ht], writes=[self.hbuf[ti]])

    def idn(self):
        return self.cb.t[:, CB_ID:CB_ID + 128]

    def norm_T(self, src_ap, src_bufs, gain, dstT, col0, ht, nparts=128):
        S = self.S
        S.dma("sync", ht.t[:], src_ap, reads=src_bufs, writes=[ht])
        sm = self.smalls.next()
        S.act(self.junk.t[:], ht.t[:], AF.Square, [ht], [self.junk, sm], accum_out=sm.t[:, 0:1])
        S.act(sm.t[:, 1:2], sm.t[:, 0:1], AF.Sqrt, [sm], [sm], scale=1.0 / D, bias=EPS)
        S.op("dve", lambda e: e.reciprocal(out=sm.t[:, 2:3], in_=sm.t[:, 1:2]), [sm], [sm])
        xn = self.xn.next()
        S.stt("dve", xn.t[:], ht.t[:], sm.t[:, 2:3], gain.t[:], ALU.mult, ALU.mult, [ht, sm, gain], [xn])
        tb = self.tbanks.next()
        for kc in range(8):
            S.tr(tb.t[:, kc * 128:(kc + 1) * 128], xn.t[:, kc * 128:(kc + 1) * 128], self.cb, [xn], [tb])
        S.copy("act", dstT.t[:, 0:8, col0:col0 + 128], tb.t[:].rearrange("p (k t) -> p k t", k=8), [tb], [dstT])

    def load_w(self, dst, src_ap, nk, rows=128):
        for kc in range(nk):
            self.S.dma("pool", dst.t[0:rows, kc, :], src_ap[kc * 128:kc * 128 + rows, :], writes=[dst])

    def load_row(self, dst, l, off, n):
        self.S.dma("sync", dst.t[:, 0:n], self.rowp_d[l, off:off + n].partition_broadcast(128), writes=[dst])

    def ffn(self, es0, l, rp_off, w13_ap, w2_ap, first):
        S = self.S
        TT = 256
        with contextlib.ExitStack() as es:
            w13 = S.sbt(es, "w13", [128, 8, 2 * DFF], BF16)
            w2 = S.sbt(es, "w2", [128, 22, D], BF16)
            gain = S.sbt(es, "gain", [128, D], F32)
            self.load_row(gain, l, rp_off, D)
            self.load_w(w13, w13_ap, 8)
            self.load_w(w2, w2_ap, 22)
            nTs = Rot([S.sbt(es, "nT%d" % i, [128, 8, TT], BF16) for i in range(2)])
            gTs = Rot([S.sbt(es, "gT%d" % i, [128, 22, TT], BF16) for i in range(1)])
            hts = Rot([S.sbt(es, "ht%d" % i, [128, D], F32) for i in range(4)])
            sas = Rot([S.sbt(es, "sa%d" % i, [128, TT], F32) for i in range(3)])
            src = self.x if first else self.out
            for tt in range(T // TT):
                n = nTs.next()
                hs = []
                for s in range(2):
                    ti = tt * 2 + s
                    ht = hts.next()
                    self.norm_T(src[ti * 128:(ti + 1) * 128, :], [] if first else [self.hbuf[ti]], gain, n, s * 128, ht)
                    hs.append(ht)
                g = gTs.next()
                for fc in range(22):
                    pa = self.banks.next()
                    pb = self.banks.next()
                    for kc in range(8):
                        S.mm(pa.t[:, 0:TT], w13.t[:, kc, fc * 128:(fc + 1) * 128], n.t[:, kc, :], kc == 0, kc == 7, [w13, n], [pa])
                    for kc in range(8):
                        S.mm(pb.t[:, 0:TT], w13.t[:, kc, DFF + fc * 128:DFF + (fc + 1) * 128], n.t[:, kc, :], kc == 0, kc == 7, [w13, n], [pb])
                    sa = sas.next()
                    S.act(sa.t[:], pa.t[:, 0:TT], AF.Silu, [pa], [sa])
                    S.tt("dve", g.t[:, fc, :], sa.t[:], pb.t[:, 0:TT], ALU.mult, [sa, pb], [g])
                for s in range(2):
                    ti = tt * 2 + s
                    for half in range(2):
                        po = self.banks.next()
                        for fc in range(22):
                            S.mm(po.t[:], g.t[:, fc, s * 128:(s + 1) * 128], w2.t[:, fc, half * 512:(half + 1) * 512], fc == 0, fc == 21, [g, w2], [po])
                        hsl = hs[s].t[:, half * 512:(half + 1) * 512]
                        S.stt("dve", hsl, po.t[:], 0.5, hsl, ALU.mult, ALU.add, [po, hs[s]], [hs[s]])
                    S.dma("sync", self.out[ti * 128:(ti + 1) * 128, :], hs[s].t[:], reads=[hs[s]], writes=[self.hbuf[ti]])

    def xattn(self, es0, l):
        S = self.S
        TT = 256
        ones = self.cb.t[:, CB_ONES:CB_ONES + 128]
        with contextlib.ExitStack() as es:
            wq = S.sbt(es, "wq", [128, 8, D], BF16)
            wkv = S.sbt(es, "wkv", [128, 8, 2 * D], BF16)
            wo = S.sbt(es, "wo", [128, 8, D], BF16)
            gx = S.sbt(es, "gx", [128, D], F32)
            gm = S.sbt(es, "gm", [128, D], F32)
            self.load_row(gx, l, RP_XA, D)
            self.load_row(gm, l, RP_MEM, D)
            self.load_w(wkv, self.W["xa_wkv"][l], 8)
            self.load_w(wq, self.W["xa_wq"][l], 8)
            self.load_w(wo, self.W["xa_wo"][l], 8)
            mnT = S.sbt(es, "mnT", [128, 8, MEM], BF16)
            KnT = S.sbt(es, "KnT", [128, 8, MEM], BF16)
            V = S.sbt(es, "V", [128, 2, D], BF16)
            qf = S.sbt(es, "qf", [128, 8, TT], F32)
            sq = S.sbt(es, "sq", [128, 8, TT], BF16)
            rrs = Rot([S.sbt(es, "rr%d" % i, [128, 2, TT], F32) for i in range(2)])
            hts = Rot([S.sbt(es, "ht%d" % i, [128, D], F32) for i in range(4)])
            nTs = Rot([S.sbt(es, "nT%d" % i, [128, 8, TT], BF16) for i in range(2)])
            qnT = S.sbt(es, "qnT", [128, 8, TT], BF16)
            oT = S.sbt(es, "oT", [128, 8, TT], BF16)
            Es = Rot([S.sbt(es, "E%d" % i, [128, 2, TT], BF16) for i in range(2)])

            def qknorm(src_w, c0, rhsT, dst, cpoff):
                for fc in range(8):
                    pq = self.banks.next()
                    for kc in range(8):
                        S.mm(pq.t[:, 0:TT], src_w.t[:, kc, c0 + fc * 128:c0 + (fc + 1) * 128], rhsT.t[:, kc, :], kc == 0, kc == 7, [src_w, rhsT], [pq])
                    S.copy("dve", qf.t[:, fc, :], pq.t[:, 0:TT], [pq], [qf])
                    S.act(sq.t[:, fc, :], qf.t[:, fc, :], AF.Square, [qf], [sq])
                import os
                qc = int(os.environ.get("QK_CUT", "99"))
                if qc < 1:
                    return
                for hh in range(4):
                    pss = self.banks.next()
                    for c2 in range(2):
                        S.mm(pss.t[:, 0:TT], ones, sq.t[:, hh * 2 + c2, :], c2 == 0, c2 == 1, [self.cb, sq], [pss])
                    rr = rrs.next()
                    S.act(rr.t[:, 0, :], pss.t[:, 0:TT], AF.Sqrt, [pss], [rr], scale=1.0 / 256, bias=EPS)
                    if qc < 2:
                        continue
                    S.op("dve", lambda e: e.reciprocal(out=rr.t[:, 1, :], in_=rr.t[:, 0, :]), [rr], [rr])
                    if qc < 3:
                        continue
                    for c2 in range(2):
                        fc = hh * 2 + c2
                        S.stt("dve", dst.t[:, fc, :], qf.t[:, fc, :], self.colp.t[:, cpoff + c2:cpoff + c2 + 1], rr.t[:, 1, :], ALU.mult, ALU.mult, [qf, rr, self.colp], [dst])

            for mt in range(2):
                self.norm_T(self.mem[mt * 128:(mt + 1) * 128, :], [], gm, mnT, mt * 128, hts.next())
            import os
            cut = int(os.environ.get("XA_CUT", "99"))
            if cut < 1:
                return
            qknorm(wkv, 0, mnT, KnT, CP_XK)
            if cut < 2:
                return
            for mt in range(2):
                for half in range(2):
                    pv = self.banks.next()
                    for kc in range(8):
                        S.mm(pv.t[:], mnT.t[:, kc, mt * 128:(mt + 1) * 128], wkv.t[:, kc, D + half * 512:D + (half + 1) * 512], kc == 0, kc == 7, [mnT, wkv], [pv])
                    S.copy("act", V.t[:, mt, half * 512:(half + 1) * 512], pv.t[:], [pv], [V])
            if cut < 3:
                return
            for tt in range(T // TT):
                n = nTs.next()
                hs = []
                for s in range(2):
                    ti = tt * 2 + s
                    ht = hts.next()
                    self.norm_T(self.out[ti * 128:(ti + 1) * 128, :], [self.hbuf[ti]], gx, n, s * 128, ht)
                    hs.append(ht)
                qknorm(wq, 0, n, qnT, CP_XQ)
                if cut < 4:
                    continue
                for hh in range(4):
                    E = Es.next()
                    for mc in range(2):
                        ps_ = self.banks.next()
                        for c2 in range(2):
                            S.mm(ps_.t[:, 0:TT], KnT.t[:, hh * 2 + c2, mc * 128:(mc + 1) * 128], qnT.t[:, hh * 2 + c2, :], c2 == 0, c2 == 1, [KnT, qnT], [ps_])
                        S.act(E.t[:, mc, :], ps_.t[:, 0:TT], AF.Exp, [ps_], [E], scale=1.0 / 16)
                    pden = self.banks.next()
                    for mc in range(2):
                        S.mm(pden.t[:, 0:TT], ones, E.t[:, mc, :], mc == 0, mc == 1, [self.cb, E], [pden])
                    rr = rrs.next()
                    S.op("dve", lambda e: e.reciprocal(out=rr.t[:, 0, :], in_=pden.t[:, 0:TT]), [pden], [rr])
                    for dc in range(2):
                        po = self.banks.next()
                        for mc in range(2):
                            S.mm(po.t[:, 0:TT], V.t[:, mc, hh * 256 + dc * 128:hh * 256 + (dc + 1) * 128], E.t[:, mc, :], mc == 0, mc == 1, [V, E], [po])
                        S.tt("dve", oT.t[:, hh * 2 + dc, :], po.t[:, 0:TT], rr.t[:, 0, :], ALU.mult, [po, rr], [oT])
                for s in range(2):
                    ti = tt * 2 + s
                    for half in range(2):
                        po = self.banks.next()
                        for kc in range(8):
                            S.mm(po.t[:], oT.t[:, kc, s * 128:(s + 1) * 128], wo.t[:, kc, half * 512:(half + 1) * 512], kc == 0, kc == 7, [oT, wo], [po])
                        hsl = hs[s].t[:, half * 512:(half + 1) * 512]
                        S.tt("dve", hsl, po.t[:], hsl, ALU.add, [po, hs[s]], [hs[s]])
                    S.dma("sync", self.out[ti * 128:(ti + 1) * 128, :], hs[s].t[:], reads=[hs[s]], writes=[self.hbuf[ti]])

    def mixer(self, es0, l):
        S = self.S
        with contextlib.ExitStack() as es:
            uT = S.sbt(es, "uT", [128, 8, T], BF16)
            with contextlib.ExitStack() as es2:
                gain = S.sbt(es2, "gmix", [128, D], F32)
                self.load_row(gain, l, RP_MIX, D)
                hts = Rot([S.sbt(es2, "ht%d" % i, [128, D], F32) for i in range(3)])
                for ti in range(NT):
                    self.norm_T(self.out[ti * 128:(ti + 1) * 128, :], [self.hbuf[ti]], gain, uT, ti * 128, hts.next())
            S.barrier()
            sub = self.stages_sub()
            if "rw" in sub:
                self.rwkv(l, uT, None)
                S.barrier()
            with contextlib.ExitStack() as es3:
                cf = S.sbt(es3, "cf", [128, CF_N], F32)
                S.dma("sync", cf.t[:], self.cf_d, writes=[cf])
                if "dil" in sub:
                    self.dilated(l, uT, cf)
                    S.barrier()
                if "ret" in sub:
                    self.retention(l, uT, cf)
                    S.barrier()
            S.barrier()
            if "merge" in sub:
                self.merge(l, uT)

    def stages_sub(self):
        if self.stages is None:
            return ("rw", "dil", "ret", "merge")
        return [k for k in ("rw", "dil", "ret", "merge") if k in self.stages]

    def dilated(self, l, uT, cf):
        S = self.S
        with contextlib.ExitStack() as es:
            Wgs = [S.sbt(es, "Wd%d" % i, [128, 8, 1536], BF16) for i in range(2)]
            gq = S.sbt(es, "gq", [128, 1536], F32)
            gk = S.sbt(es, "gk", [128, 1536], F32)
            self.load_row(gq, l, RP_DQ, 1536)
            self.load_row(gk, l, RP_DK, 1536)
            NB = 3
            qks = [S.sbt(es, "dqk%d" % i, [128, 8, 128], BF16) for i in range(NB)]
            Vas = [S.sbt(es, "dVa%d" % i, [128, 8, 65], BF16) for i in range(NB)]
            for v in Vas:
                S.op("dve", lambda e: e.memset(v.t[:], 1.0), [], [v])
            sqs = [S.sbt(es, "dsq%d" % i, [128, 512], F32) for i in range(2)]
            xns = [S.sbt(es, "dxn%d" % i, [128, 8, 64], F32) for i in range(2)]
            tms = [S.sbt(es, "dtm%d" % i, [128, 4, 8, 8], F32) for i in range(2)]
            sms = [S.sbt(es, "dsm%d" % i, [128, 16], F32) for i in range(2)]
            qrs = Rot([S.sbt(es, "dqr%d" % i, [128, 2, 512], BF16) for i in range(2)])
            Ees = Rot([S.sbt(es, "dEe%d" % i, [128, 512], BF16) for i in range(3)])
            Ems = Rot([S.sbt(es, "dEm%d" % i, [128, 4, 512], BF16) for i in range(2)])
            Osb = Rot([S.sbt(es, "dO%d" % i, [128, 520], F32) for i in range(2)])
            mpc = self.cb.t[:, CB_MPC:CB_MPC + 512]
            h8 = lambda ap: ap.rearrange("p (h e) -> p h e", h=8)
            flat = lambda ap: ap.rearrange("p h e -> p (h e)")
            gains = (gq, gk)

            def load_group(g):
                self.load_w(Wgs[g % 2], self.W["w_in"][l][:, C_DIL + g * 1536:C_DIL + (g + 1) * 1536], 8)

            def stage_a(g, d, c, n, idx, W):
                nblk = 32 // d
                combo = c * nblk + n
                tok0 = n * 128 * d + c
                cur = idx % NB
                pq, pk, pv = self.banks.next(), self.banks.next(), self.banks.next()
                pps = (pq, pk)
                for pp, jq in ((pq, 0), (pk, 1), (pv, 2)):
                    for kc in range(8):
                        S.mm(pp.t[:], uT.t[:, kc, tok0:tok0 + 127 * d + 1:d], W.t[:, kc, jq * 512:(jq + 1) * 512], kc == 0, kc == 7, [uT, W], [pp])
                    yield
                S.copy("act", Vas[cur].t[:, :, 0:64], h8(pv.t[:]), [pv], [Vas[cur]])
                qr = qrs.next()
                o = (g * 32 + combo) * 8
                cosb = cf.t[:, CF_DCOS + o:CF_DCOS + o + 8].rearrange("p (a e) -> p a e", a=1).broadcast_to([128, 8, 8])
                sinb = cf.t[:, CF_DSIN + o:CF_DSIN + o + 8].rearrange("p (a e) -> p a e", a=1).broadcast_to([128, 8, 8])
                bc = lambda ap: ap.rearrange("p (h e) -> p h e", e=1).broadcast_to([128, 8, 64])
                steps = [
                    lambda w: S.copy("dve", flat(xns[w].t[:]), pps[w].t[:], [pps[w]], [xns[w]]),
                    lambda w: S.act(sqs[w].t[:], flat(xns[w].t[:]), AF.Square, [xns[w]], [sqs[w]]),
                    lambda w: S.op("dve", lambda e: e.tensor_reduce(out=sms[w].t[:, 0:8], in_=h8(sqs[w].t[:]), axis=AX.X, op=ALU.add), [sqs[w]], [sms[w]]),
                    lambda w: S.act(sms[w].t[:, 8:16], sms[w].t[:, 0:8], AF.Sqrt, [sms[w]], [sms[w]], scale=1.0 / 64, bias=EPS),
                    lambda w: S.op("dve", lambda e: e.reciprocal(out=sms[w].t[:, 0:8], in_=sms[w].t[:, 8:16]), [sms[w]], [sms[w]]),
                    lambda w: S.tt("dve", xns[w].t[:], xns[w].t[:], bc(sms[w].t[:, 0:8]), ALU.mult, [xns[w], sms[w]], [xns[w]]),
                    lambda w: S.tt("dve", xns[w].t[:], xns[w].t[:], h8(gains[w].t[:, g * 512:(g + 1) * 512]), ALU.mult, [xns[w], gains[w]], [xns[w]]),
                    lambda w: S.tt("dve", tms[w].t[:, 0], xns[w].t[:, :, 0:8], cosb, ALU.mult, [xns[w], cf], [tms[w]]),
                    lambda w: S.tt("dve", tms[w].t[:, 1], xns[w].t[:, :, 8:16], sinb, ALU.mult, [xns[w], cf], [tms[w]]),
                    lambda w: S.tt("dve", tms[w].t[:, 2], xns[w].t[:, :, 8:16], cosb, ALU.mult, [xns[w], cf], [tms[w]]),
                    lambda w: S.tt("dve", tms[w].t[:, 3], xns[w].t[:, :, 0:8], sinb, ALU.mult, [xns[w], cf], [tms[w]]),
                    lambda w: S.tt("dve", xns[w].t[:, :, 0:8], tms[w].t[:, 0], tms[w].t[:, 1], ALU.subtract, [tms[w], xns[w]], [xns[w]]),
                    lambda w: S.tt("dve", xns[w].t[:, :, 8:16], tms[w].t[:, 2], tms[w].t[:, 3], ALU.add, [tms[w], xns[w]], [xns[w]]),
                    lambda w: S.copy("act", qr.t[:, w, :], flat(xns[w].t[:]), [xns[w]], [qr]),
                ]
                for st_ in steps:
                    for w in range(2):
                        st_(w)
                        yield
                tb = self.tbanks.next()
                for w in range(2):
                    for j in range(4):
                        S.tr(tb.t[:, (w * 4 + j) * 128:(w * 4 + j + 1) * 128], qr.t[:, w, j * 128:(j + 1) * 128], self.cb, [qr], [tb])
                    yield
                S.copy("act", qks[cur].t[:, 0:8, :], tb.t[:].rearrange("p (k t) -> p k t", k=8), [tb], [qks[cur]])
                yield

            def stage_b(g, d, c, n, idx):
                tok0 = n * 128 * d + c
                cur, prv = idx % NB, (idx - 1) % NB
                qk = qks[cur]
                Em = Ems.next()
                for j in range(4):
                    psd = (self.banks.next(), self.banks.next())
                    for hh in range(2):
                        pb = 64 * hh
                        ps_ = psd[hh]
                        if n > 0:
                            S.mm(ps_.t[:, hh * 256:hh * 256 + 128], qks[prv].t[pb:pb + 64, 4 + j, :], qk.t[pb:pb + 64, j, :], True, True, [qks[prv], qk], [ps_])
                        S.mm(ps_.t[:, hh * 256 + 128:hh * 256 + 256], qk.t[pb:pb + 64, 4 + j, :], qk.t[pb:pb + 64, j, :], True, True, [qk], [ps_])
                    yield
                    Ee = Ees.next()
                    for hh in range(2):
                        ps_ = psd[hh]
                        lo = hh * 256 if n > 0 else hh * 256 + 128
                        hi = hh * 256 + 256
                        S.act(Ee.t[:, lo:hi], ps_.t[:, lo:hi], AF.Exp, [ps_], [Ee], scale=0.125)
                        yield
                    for hh in range(2):
                        lo = hh * 256 if n > 0 else hh * 256 + 128
                        hi = hh * 256 + 256
                        S.tt("dve", Em.t[:, j, lo:hi], Ee.t[:, lo:hi], mpc[:, lo:hi], ALU.mult, [Ee, self.cb], [Em])
                        yield
                O = Osb.next()
                for half in range(2):
                    po = self.banks.next()
                    for hq in range(4):
                        h = half * 4 + hq
                        j, hh = h // 2, h % 2
                        if n > 0:
                            S.mm(po.t[:, hq * 65:(hq + 1) * 65], Em.t[:, j, hh * 256:hh * 256 + 128], Vas[prv].t[:, h, :], True, False, [Em, Vas[prv]], [po])
                        S.mm(po.t[:, hq * 65:(hq + 1) * 65], Em.t[:, j, hh * 256 + 128:hh * 256 + 256], Vas[cur].t[:, h, :], n == 0, True, [Em, Vas[cur]], [po])
                    yield
                    S.copy("act", O.t[:, half * 260:(half + 1) * 260], po.t[:, 0:260], [po], [O])
                    yield
                S.dma("sync", self.oa_d[g, tok0:tok0 + 127 * d + 1:d, :], O.t[:], reads=[O], writes=[self.oa_b[g]])
                yield

            items = []
            for g, d in enumerate(DILS):
                for c in range(d):
                    for n in range(32 // d):
                        items.append((g, d, c, n))
            load_group(0)
            load_group(1)
            loaded = {0, 1}
            prev_b = None
            for idx, (g, d, c, n) in enumerate(items):
                if g == 1 and 2 not in loaded and c == 0 and n == 1:
                    load_group(2)
                    loaded.add(2)
                ga = stage_a(g, d, c, n, idx, Wgs[g % 2])
                interleave([ga, prev_b])
                prev_b = stage_b(g, d, c, n, idx)
            interleave([prev_b])

    def rwkv(self, l, uT, cf):
        S = self.S
        ST = 256
        NJ = 15
        with contextlib.ExitStack() as es:
            W = S.sbt(es, "Wrw", [128, 8, 1824], BF16)
            self.load_w(W, self.W["w_in"][l][:, 0:1824], 8)
            w2s = S.sbt(es, "w2s", [128, 512], BF16)
            a2s = S.sbt(es, "a2s", [128, 512], BF16)
            g2a = S.sbt(es, "g2a", [128, 512], BF16)
            g2b = S.sbt(es, "g2b", [128, 512], BF16)
            S.dma("pool", w2s.t[0:64, :], self.W["rw_w2"][l], writes=[w2s])
            S.dma("pool", a2s.t[64:128, :], self.W["rw_a2"][l], writes=[a2s])
            S.dma("pool", g2a.t[:, :], self.W["rw_g2"][l][0:128, :], writes=[g2a])
            S.dma("pool", g2b.t[0:32, :], self.W["rw_g2"][l][128:160, :], writes=[g2b])
            lnw = S.sbt(es, "lnw", [128, 512], F32)
            lnb = S.sbt(es, "lnb", [128, 512], F32)
            self.load_row(lnw, l, RP_LNW, 512)
            self.load_row(lnb, l, RP_LNB, 512)
            cp = self.colp
            omk = S.sbt(es, "omk", [128, 4], F32)
            S.ts("dve", omk.t[:], cp.t[:, CP_KA:CP_KA + 4], -1.0, ALU.mult, [cp], [omk], s2=1.0, op1=ALU.add)
            rkb = S.sbt(es, "rkb", [128, 32], BF16)
            S.copy("dve", rkb.t[:], cp.t[:, CP_RK:CP_RK + 32], [cp], [rkb])
            onesf = S.sbt(es, "onesf", [128, 128], F32)
            S.op("dve", lambda e: e.memset(onesf.t[:], 1.0), [], [onesf])
            St = S.sbt(es, "St", [128, 4, 128], F32)
            Sb = S.sbt(es, "Sb", [128, 4, 128], BF16)
            Stmp = S.sbt(es, "Stmp", [128, 4, 128], F32)
            prevc = S.sbt(es, "prevc", [128, NJ], F32)
            for t_ in (St, Sb, prevc):
                S.op("dve", lambda e: e.memset(t_.t[:], 0.0), [], [t_])
            PM = S.sbt(es, "PM", [128, NJ, ST], F32)
            Pjs = Rot([S.sbt(es, "Pj%d" % i, [128, ST + 1], F32) for i in range(2)])
            dts = Rot([S.sbt(es, "dt%d" % i, [128, ST], F32) for i in range(2)])
            f32t = lambda nm: S.sbt(es, nm, [128, ST], F32)
            sgT, aT, lwT, LT, LpT, GT, GiT, GpT, kkkT, nrT, kkT, tkT, kmT, beT = [f32t("rw%d" % i) for i in range(14)]
            thw = S.sbt(es, "thw", [128, ST], BF16)
            adb = S.sbt(es, "adb", [128, ST], BF16)
            sgd = S.sbt(es, "sgd", [128, 2, ST], BF16)
            sqk = S.sbt(es, "sqk", [128, ST], BF16)
            AR = S.sbt(es, "AR", [128, 4, 2, 2, 128], BF16)
            BT = S.sbt(es, "BT", [128, 4, ST], BF16)
            KT = S.sbt(es, "KT", [128, 4, ST], BF16)
            VT = S.sbt(es, "VT", [128, 4, ST], BF16)
            RK = S.sbt(es, "RK", [128, 4, ST], BF16)
            GC = S.sbt(es, "GC", [128, 4, 2], F32)
            BKtok = S.sbt(es, "BKtok", [128, 2, 512], BF16)
            Vtok = S.sbt(es, "Vtok", [128, 512], BF16)
            AXt = S.sbt(es, "AXt", [128, 8, 384], BF16)
            AYt = S.sbt(es, "AYt", [128, 8, 256], BF16)
            Bts = [S.sbt(es, "Bt%d" % i, [128, 8, 128], BF16) for i in range(2)]
            Ans = [S.sbt(es, "An%d" % i, [128, 8, 128], BF16) for i in range(2)]
            TTs = [S.sbt(es, "TT%d" % i, [128, 8, 128], BF16) for i in range(2)]
            Xb = S.sbt(es, "Xb", [128, 512], BF16)
            Ub = S.sbt(es, "Ub", [128, 512], BF16)
            ysq = S.sbt(es, "ysq", [128, 512], F32)
            y1 = S.sbt(es, "y1", [128, 512], F32)
            y2 = S.sbt(es, "y2", [128, 512], F32)
            yo = Rot([S.sbt(es, "yo%d" % i, [128, 512], BF16) for i in range(2)])
            m3 = self.cb.t[:, CB_M3:CB_M3 + 384]
            bo2 = self.cb.t[:, CB_BO2:CB_BO2 + 128]
            h8 = lambda ap: ap.rearrange("p (h e) -> p h e", h=8)
            c2 = lambda ap: ap.rearrange("p (c t) -> p c t", c=2)
            col = lambda off, j: cp.t[:, off + j:off + j + 1]
            for st in range(T // ST):
                tok0 = st * ST
                for jc in range(NJ):
                    M = 128 if jc < 14 else 32
                    pp = self.banks.next()
                    for kc in range(8):
                        S.mm(pp.t[0:M, 0:ST], W.t[:, kc, jc * 128:jc * 128 + M], uT.t[:, kc, tok0:tok0 + ST], kc == 0, kc == 7, [W, uT], [pp])
                    Pj = Pjs.next()
                    S.copy("act", Pj.t[0:M, 1:ST + 1], pp.t[0:M, 0:ST], [pp], [Pj])
                    S.copy("dve", Pj.t[0:M, 0:1], prevc.t[0:M, jc:jc + 1], [prevc], [Pj])
                    dtm = dts.next()
                    S.tt("dve", dtm.t[0:M, :], Pj.t[0:M, 0:ST], Pj.t[0:M, 1:ST + 1], ALU.subtract, [Pj], [dtm])
                    S.stt("dve", PM.t[0:M, jc, :], dtm.t[0:M, :], col(CP_MU, jc)[0:M], Pj.t[0:M, 1:ST + 1], ALU.mult, ALU.add, [dtm, Pj, cp], [PM])
                    S.copy("act", prevc.t[0:M, jc:jc + 1], Pj.t[0:M, ST:ST + 1], [Pj], [prevc])
                S.act(thw.t[0:64, :], PM.t[0:64, 12, :], AF.Tanh, [PM], [thw])
                S.copy("act", adb.t[64:128, :], PM.t[64:128, 12, :], [PM], [adb])
                S.act(sgd.t[:, 0, :], PM.t[:, 13, :], AF.Sigmoid, [PM], [sgd])
                S.act(sgd.t[0:32, 1, :], PM.t[0:32, 14, :], AF.Sigmoid, [PM], [sgd])
                for j in range(4):
                    rj, kj, vj = PM.t[:, j, :], PM.t[:, 4 + j, :], PM.t[:, 8 + j, :]
                    px, px2 = self.banks.next(), self.banks.next()
                    S.mm(px.t[:, 0:ST], w2s.t[0:64, j * 128:(j + 1) * 128], thw.t[0:64, :], True, True, [w2s, thw], [px])
                    S.mm(px2.t[:, ST:2 * ST], a2s.t[64:128, j * 128:(j + 1) * 128], adb.t[64:128, :], True, True, [a2s, adb], [px2])
                    S.act(sgT.t[:], px.t[:, 0:ST], AF.Sigmoid, [px, cp], [sgT], bias=col(CP_W0, j))
                    S.act(aT.t[:], px2.t[:, ST:2 * ST], AF.Sigmoid, [px2, cp], [aT], bias=col(CP_A0, j))
                    S.ts("dve", lwT.t[:], sgT.t[:], -0.6065306597126334, ALU.mult, [sgT], [lwT])
                    for ch in range(2):
                        S.op("dve", lambda e: e.tensor_tensor_scan(out=LT.t[:, ch * 128:(ch + 1) * 128], data0=onesf.t[:], data1=lwT.t[:, ch * 128:(ch + 1) * 128], initial=0.0, op0=ALU.mult, op1=ALU.add), [onesf, lwT], [LT])
                    S.tt("dve", LpT.t[:], LT.t[:], lwT.t[:], ALU.subtract, [LT, lwT], [LpT])
                    S.act(GT.t[:], LT.t[:], AF.Exp, [LT], [GT])
                    S.act(GiT.t[:], LT.t[:], AF.Exp, [LT], [GiT], scale=-1.0)
                    S.act(GpT.t[:], LpT.t[:], AF.Exp, [LpT], [GpT])
                    S.copy("dve", GC.t[:, j, :].rearrange("p (c e) -> p c e", e=1), c2(GT.t[:])[:, :, 127:128], [GT], [GC])
                    S.ts("dve", kkkT.t[:], kj, col(CP_KK, j), ALU.mult, [PM, cp], [kkkT])
                    S.act(sqk.t[:], kkkT.t[:], AF.Square, [kkkT], [sqk])
                    pss = self.banks.next()
                    S.mm(pss.t[:, 0:ST], bo2, sqk.t[:], True, True, [self.cb, sqk], [pss])
                    S.act(nrT.t[:], pss.t[:, 0:ST], AF.Sqrt, [pss], [nrT])
                    S.ts("dve", nrT.t[:], nrT.t[:], 1e-12, ALU.max, [nrT], [nrT])
                    S.op("dve", lambda e: e.reciprocal(out=nrT.t[:], in_=nrT.t[:]), [nrT], [nrT])
                    S.tt("dve", kkT.t[:], kkkT.t[:], nrT.t[:], ALU.mult, [kkkT, nrT], [kkT])
                    S.ts("dve", tkT.t[:], aT.t[:], col(CP_KA, j), ALU.mult, [aT, cp, omk], [tkT], s2=omk.t[:, j:j + 1], op1=ALU.add)
                    S.tt("dve", kmT.t[:], kj, tkT.t[:], ALU.mult, [PM, tkT], [kmT])
                    S.tt("dve", beT.t[:], kkT.t[:], aT.t[:], ALU.mult, [kkT, aT], [beT])
                    S.stt("dve", AR.t[:, j, :, 0, :], c2(kkT.t[:]), -1.0, c2(GpT.t[:]), ALU.mult, ALU.mult, [kkT, GpT], [AR])
                    S.tt("dve", AR.t[:, j, :, 1, :], c2(rj), c2(GT.t[:]), ALU.mult, [PM, GT], [AR])
                    S.tt("dve", BT.t[:, j, :], beT.t[:], GiT.t[:], ALU.mult, [beT, GiT], [BT])
                    S.tt("dve", KT.t[:, j, :], kmT.t[:], GiT.t[:], ALU.mult, [kmT, GiT], [KT])
                    S.copy("act", VT.t[:, j, :], vj, [PM], [VT])
                    S.tt("dve", RK.t[:, j, :], rj, kmT.t[:], ALU.mult, [PM, kmT], [RK])
                for ch in range(2):
                    ci = st * 2 + ch
                    tsl = slice(ch * 128, (ch + 1) * 128)
                    tb1, tb2 = self.tbanks.next(), self.tbanks.next()
                    for j in range(4):
                        S.tr(tb1.t[:, j * 128:(j + 1) * 128], BT.t[:, j, tsl], self.cb, [BT], [tb1])
                        S.tr(tb1.t[:, 512 + j * 128:512 + (j + 1) * 128], KT.t[:, j, tsl], self.cb, [KT], [tb1])
                        S.tr(tb2.t[:, j * 128:(j + 1) * 128], VT.t[:, j, tsl], self.cb, [VT], [tb2])
                    S.copy("act", BKtok.t[:].rearrange("p a (k t) -> p (a k) t", k=4), tb1.t[:].rearrange("p (k t) -> p k t", k=8), [tb1], [BKtok])
                    S.copy("act", Vtok.t[:].rearrange("p (k t) -> p k t", k=4), tb2.t[:, 0:512].rearrange("p (k t) -> p k t", k=4), [tb2], [Vtok])
                    pgb = self.banks.next()
                    S.mm(pgb.t[:], sgd.t[:, 0, tsl], g2a.t[:], True, False, [sgd, g2a], [pgb])
                    S.mm(pgb.t[:], sgd.t[0:32, 1, tsl], g2b.t[0:32, :], False, True, [sgd, g2b], [pgb])
                    S.copy("act", y2.t[:], pgb.t[:], [pgb], [y2])
                    pbn = self.banks.next()
                    for j in range(4):
                        S.mm(pbn.t[:, 0:8], RK.t[:, j, tsl], rkb.t[:, j * 8:(j + 1) * 8], j == 0, j == 3, [RK, rkb], [pbn])
                    smb = self.smalls.next()
                    S.copy("act", smb.t[:, 0:8], pbn.t[:, 0:8], [pbn], [smb])
                    for h in range(8):
                        j, pb = h // 2, 64 * (h % 2)
                        bx, by = self.banks.next(), self.banks.next()
                        arf = AR.t[pb:pb + 64, j, ch, :, :].rearrange("p a t -> p (a t)")
                        S.mm(bx.t[:, 0:256], BT.t[pb:pb + 64, j, tsl], arf, True, True, [BT, AR], [bx])
                        S.mm(bx.t[:, 256:384], AR.t[pb:pb + 64, j, ch, 0, :], BT.t[pb:pb + 64, j, tsl], True, True, [BT, AR], [bx])
                        S.mm(by.t[:, 0:256], KT.t[pb:pb + 64, j, tsl], arf, True, True, [KT, AR], [by])
                        S.tt("dve", AXt.t[:, h, :], bx.t[:, 0:384], m3, ALU.mult, [bx, self.cb], [AXt])
                        S.tt("dve", AYt.t[:, h, :], by.t[:, 0:256], m3[:, 0:256], ALU.mult, [by, self.cb], [AYt])
                    idb = self.cb.t[:, CB_ID:CB_ID + 128].rearrange("p (a t) -> p a t", a=1).broadcast_to([128, 8, 128])
                    S.tt("dve", TTs[0].t[:], AXt.t[:, :, 0:128], idb, ALU.add, [AXt, self.cb], [TTs[0]])
                    Btp = lambda h: AXt.t[:, h, 0:128]
                    Ap = lambda h: AXt.t[:, h, 256:384]
                    Btb, Ab = AXt, AXt
                    cur = 0
                    for n in range(1, 7):
                        Bn, An, Tn, Tp = Bts[n % 2], Ans[n % 2], TTs[(cur + 1) % 2], TTs[cur]
                        for grp in range(2):
                            gs = slice(grp * 4, grp * 4 + 4)
                            pA = self.banks.next()
                            pB = self.banks.next() if n < 6 else None
                            for hq in range(4):
                                h = grp * 4 + hq
                                if n < 6:
                                    S.mm(pB.t[:, hq * 128:(hq + 1) * 128], Ap(h), Btp(h), True, True, [Ab, Btb], [pB])
                                S.mm(pA.t[:, hq * 128:(hq + 1) * 128], Btp(h), Ap(h), True, True, [Ab, Btb], [pA])
                            S.copy("act", An.t[:, gs, :].rearrange("p a t -> p (a t)"), pA.t[:], [pA], [An])
                            if n < 6:
                                S.copy("dve", Bn.t[:, gs, :].rearrange("p a t -> p (a t)"), pB.t[:], [pB], [Bn])
                            pT = self.banks.next()
                            for hq in range(4):
                                h = grp * 4 + hq
                                S.mm(pT.t[:, hq * 128:(hq + 1) * 128], An.t[:, h, :], Tp.t[:, h, :], True, True, [An, Tp], [pT])
                            S.tt("dve", Tn.t[:, gs, :].rearrange("p a t -> p (a t)"), pT.t[:], Tp.t[:, gs, :].rearrange("p a t -> p (a t)"), ALU.add, [pT, Tp], [Tn])
                        Btp = (lambda B: (lambda h: B.t[:, h, :]))(Bn)
                        Ap = (lambda A: (lambda h: A.t[:, h, :]))(An)
                        Btb, Ab = Bn, An
                        cur = (cur + 1) % 2
                    TTf = TTs[cur]
                    for j in range(4):
                        S.ts("dve", Stmp.t[:, j, :], St.t[:, j, :], GC.t[:, j, ch:ch + 1], ALU.mult, [St, GC], [Stmp])
                    pX = self.banks.next()
                    for h in range(8):
                        j, pb = h // 2, 64 * (h % 2)
                        S.mm(pX.t[:, h * 64:(h + 1) * 64], AR.t[pb:pb + 64, j, ch, 0, :], Sb.t[pb:pb + 64, j, (h % 2) * 64:(h % 2 + 1) * 64], True, False, [AR, Sb], [pX])
                        S.mm(pX.t[:, h * 64:(h + 1) * 64], AYt.t[:, h, 0:128], Vtok.t[:, h * 64:(h + 1) * 64], False, True, [AYt, Vtok], [pX])
                    S.copy("act", Xb.t[:], pX.t[:], [pX], [Xb])
                    pU = self.banks.next()
                    for h in range(8):
                        S.mm(pU.t[:, h * 64:(h + 1) * 64], TTf.t[:, h, :], Xb.t[:, h * 64:(h + 1) * 64], True, True, [TTf, Xb], [pU])
                    S.copy("dve", Ub.t[:], pU.t[:], [pU], [Ub])
                    pY = self.banks.next()
                    for h in range(8):
                        j, pb = h // 2, 64 * (h % 2)
                        hs_ = slice(h * 64, (h + 1) * 64)
                        S.mm(pY.t[:, hs_], AR.t[pb:pb + 64, j, ch, 1, :], Sb.t[pb:pb + 64, j, (h % 2) * 64:(h % 2 + 1) * 64], True, False, [AR, Sb], [pY])
                        S.mm(pY.t[:, hs_], AXt.t[:, h, 128:256], Ub.t[:, hs_], False, False, [AXt, Ub], [pY])
                        S.mm(pY.t[:, hs_], AYt.t[:, h, 128:256], Vtok.t[:, hs_], False, True, [AYt, Vtok], [pY])
                    pS = self.banks.next()
                    for j in range(4):
                        js = slice(j * 128, (j + 1) * 128)
                        S.mm(pS.t[:, js], BKtok.t[:, 0, js], Ub.t[:, js], True, False, [BKtok, Ub], [pS])
                        S.mm(pS.t[:, js], BKtok.t[:, 1, js], Vtok.t[:, js], False, True, [BKtok, Vtok], [pS])
                    for j in range(4):
                        S.stt("dve", St.t[:, j, :], pS.t[:, j * 128:(j + 1) * 128], GC.t[:, j, ch:ch + 1], Stmp.t[:, j, :], ALU.mult, ALU.add, [pS, GC, Stmp], [St])
                    S.copy("act", Sb.t[:], St.t[:], [St], [Sb])
                    S.copy("dve", y1.t[:], pY.t[:], [pY], [y1])
                    S.act(ysq.t[:], y1.t[:], AF.Square, [y1], [ysq])
                    sm = self.smalls.next()
                    S.op("dve", lambda e: e.tensor_reduce(out=sm.t[:, 0:8], in_=h8(y1.t[:]), axis=AX.X, op=ALU.add), [y1], [sm])
                    sm2 = self.smalls.next()
                    S.op("dve", lambda e: e.tensor_reduce(out=sm2.t[:, 0:8], in_=h8(ysq.t[:]), axis=AX.X, op=ALU.add), [ysq], [sm2])
                    S.ts("dve", sm.t[:, 0:8], sm.t[:, 0:8], 1.0 / 64, ALU.mult, [sm], [sm])
                    S.tt("dve", sm.t[:, 8:16], sm.t[:, 0:8], sm.t[:, 0:8], ALU.mult, [sm], [sm])
                    S.stt("dve", sm2.t[:, 8:16], sm2.t[:, 0:8], 1.0 / 64, sm.t[:, 8:16], ALU.mult, ALU.subtract, [sm, sm2], [sm2])
                    S.act(sm2.t[:, 0:8], sm2.t[:, 8:16], AF.Sqrt, [sm2], [sm2], bias=GN_EPS)
                    S.op("dve", lambda e: e.reciprocal(out=sm2.t[:, 8:16], in_=sm2.t[:, 0:8]), [sm2], [sm2])
                    bc = lambda ap: ap.rearrange("p (h e) -> p h e", e=1).broadcast_to([128, 8, 64])
                    S.tt("dve", h8(y1.t[:]), h8(y1.t[:]), bc(sm.t[:, 0:8]), ALU.subtract, [y1, sm], [y1])
                    S.tt("dve", h8(y1.t[:]), h8(y1.t[:]), bc(sm2.t[:, 8:16]), ALU.mult, [y1, sm2], [y1])
                    S.tt("dve", y1.t[:], y1.t[:], lnw.t[:], ALU.mult, [y1, lnw], [y1])
                    S.tt("dve", y1.t[:], y1.t[:], lnb.t[:], ALU.add, [y1, lnb], [y1])
                    S.tt("dve", h8(ysq.t[:]), h8(Vtok.t[:]), bc(smb.t[:, 0:8]), ALU.mult, [Vtok, smb], [ysq])
                    S.tt("dve", y1.t[:], y1.t[:], ysq.t[:], ALU.add, [y1, ysq], [y1])
                    yb_ = yo.next()
                    S.tt("dve", yb_.t[:], y1.t[:], y2.t[:], ALU.mult, [y1, y2], [yb_])
                    S.dma("sync", self.ya_d[ci * 128:(ci + 1) * 128, :], yb_.t[:], reads=[yb_], writes=[self.ya_b[ci]])

    def retention(self, l, uT, cf):
        S = self.S
        with contextlib.ExitStack() as es:
            W = S.sbt(es, "Wret", [128, 8, 1536], BF16)
            self.load_w(W, self.W["w_in"][l][:, C_RET:C_RET + 1536], 8)
            gret = S.sbt(es, "gret", [128, 512], F32)
            self.load_row(gret, l, RP_RET, 512)
            Sr = S.sbt(es, "Sr", [128, 2, 256], F32)
            Srb = S.sbt(es, "Srb", [128, 2, 256], BF16)
            S.op("dve", lambda e: e.memset(Sr.t[:], 0.0), [], [Sr])
            S.op("dve", lambda e: e.memset(Srb.t[:], 0.0), [], [Srb])
            vbs = Rot([S.sbt(es, "vb%d" % i, [128, 512], BF16) for i in range(2)])
            sgs = Rot([S.sbt(es, "sg%d" % i, [128, 512], F32) for i in range(2)])
            qks = Rot([S.sbt(es, "qk%d" % i, [128, 2, 4, 64], F32) for i in range(2)])
            qkrs = Rot([S.sbt(es, "qkr%d" % i, [128, 2, 4, 64], F32) for i in range(2)])
            tmps = Rot([S.sbt(es, "rt%d" % i, [128, 4, 4, 32], F32) for i in range(2)])
            qts = Rot([S.sbt(es, "qt%d" % i, [128, 3, 256], BF16) for i in range(2)])
            qkTs = Rot([S.sbt(es, "qkT%d" % i, [128, 4, 128], BF16) for i in range(2)])
            sTms = Rot([S.sbt(es, "sTm%d" % i, [128, 4, 128], BF16) for i in range(2)])
            ysqs = Rot([S.sbt(es, "ysq%d" % i, [128, 512], F32) for i in range(2)])
            yns = Rot([S.sbt(es, "yn%d" % i, [128, 512], F32) for i in range(2)])
            ycs = Rot([S.sbt(es, "yc%d" % i, [128, 512], BF16) for i in range(2)])
            Dp = cf.t[:, CF_DP:CF_DP + 512]
            import os
            rc = int(os.environ.get('RET_CUT', '99'))
            for c in range(int(os.environ.get('RET_NCH', NT))):
                tsl = slice(c * 128, (c + 1) * 128)
                if os.environ.get("RET_VAR", "") == "only_tr":
                    tb = self.tbanks.next()
                    for a in range(4):
                        S.tr(tb.t[:, a * 128:(a + 1) * 128], self.junk.t[:, a * 128:(a + 1) * 128], self.cb, [self.junk], [tb])
                    continue
                if os.environ.get("RET_VAR", "") == "mm_tr":
                    pqk = self.banks.next()
                    for kc in range(8):
                        S.mm(pqk.t[:], uT.t[:, kc, tsl], W.t[:, kc, 0:512], kc == 0, kc == 7, [uT, W], [pqk])
                    tb = self.tbanks.next()
                    for a in range(4):
                        S.tr(tb.t[:, a * 128:(a + 1) * 128], self.junk.t[:, a * 128:(a + 1) * 128], self.cb, [self.junk], [tb])
                    continue
                pqk, pv, pg = self.banks.next(), self.banks.next(), self.banks.next()
                for pp, c0 in ((pqk, 0), (pv, 512), (pg, 1024)):
                    for kc in range(8):
                        S.mm(pp.t[:], uT.t[:, kc, tsl], W.t[:, kc, c0:c0 + 512], kc == 0, kc == 7, [uT, W], [pp])
                vb = vbs.next()
                S.copy("act", vb.t[:], pv.t[:], [pv], [vb])
                sg = sgs.next()
                S.act(sg.t[:], pg.t[:], AF.Silu, [pg], [sg])
                qk = qks.next()
                S.copy("dve", qk.t[:].rearrange("p a h e -> p (a h e)"), pqk.t[:], [pqk], [qk])
                if rc < 1:
                    continue
                qkr = qkrs.next()
                tm = tmps.next()
                cosb = cf.t[:, CF_RCOS + c * 32:CF_RCOS + (c + 1) * 32].rearrange("p (a e) -> p a e", a=1).broadcast_to([128, 4, 32])
                sinb = cf.t[:, CF_RSIN + c * 32:CF_RSIN + (c + 1) * 32].rearrange("p (a e) -> p a e", a=1).broadcast_to([128, 4, 32])
                for w in range(2):
                    x1 = qk.t[:, w, :, 0:32]
                    x2 = qk.t[:, w, :, 32:64]
                    eng = "dve"
                    S.tt(eng, tm.t[:, 0], x1, cosb, ALU.mult, [qk, cf], [tm])
                    S.tt(eng, tm.t[:, 1], x2, sinb, ALU.mult, [qk, cf], [tm])
                    S.tt(eng, tm.t[:, 2], x2, cosb, ALU.mult, [qk, cf], [tm])
                    S.tt(eng, tm.t[:, 3], x1, sinb, ALU.mult, [qk, cf], [tm])
                    S.tt(eng, qkr.t[:, w, :, 0:32], tm.t[:, 0], tm.t[:, 1], ALU.subtract, [tm], [qkr])
                    S.tt(eng, qkr.t[:, w, :, 32:64], tm.t[:, 2], tm.t[:, 3], ALU.add, [tm], [qkr])
                if rc < 2:
                    continue
                qt = qts.next()
                qdb = cf.t[:, CF_QD:CF_QD + 4].rearrange("p (h e) -> p h e", e=1).broadcast_to([128, 4, 64])
                kdb = cf.t[:, CF_KD:CF_KD + 4].rearrange("p (h e) -> p h e", e=1).broadcast_to([128, 4, 64])
                v3 = lambda ap: ap.rearrange("p (h e) -> p h e", h=4)
                S.tt("dve", v3(qt.t[:, 0, :]), qkr.t[:, 0], qdb, ALU.mult, [qkr, cf], [qt])
                S.copy("act", v3(qt.t[:, 1, :]), qkr.t[:, 1], [qkr], [qt])
                S.tt("dve", v3(qt.t[:, 2, :]), qkr.t[:, 1], kdb, ALU.mult, [qkr, cf], [qt])
                if rc < 3:
                    continue
                tb = self.tbanks.next()
                rv = os.environ.get("RET_VAR", "")
                for a in range(2):
                    for jj in range(2):
                        if rv == "notr" or (rv == "tr1" and (a or jj)):
                            continue
                        src_ = qt.t[:, a, jj * 128:(jj + 1) * 128]
                        if rv == "xnsrc":
                            src_ = self.junk.t[:, (a * 2 + jj) * 128:(a * 2 + jj + 1) * 128]
                        S.tr(tb.t[:, (a * 2 + jj) * 128:(a * 2 + jj + 1) * 128], src_, self.cb, [qt], [tb])
                qkT = qkTs.next()
                if rv != "nocopy":
                    S.copy("act", qkT.t[:, 0:4, :], tb.t[:, 0:512].rearrange("p (k t) -> p k t", k=4), [tb], [qkT])
                if rc < 4:
                    continue
                psr = (self.banks.next(), self.banks.next())
                for h in range(4):
                    jj, pb = h // 2, 64 * (h % 2)
                    S.mm(psr[h % 2].t[:, h * 128:(h + 1) * 128], qkT.t[pb:pb + 64, 2 + jj, :], qkT.t[pb:pb + 64, jj, :], True, True, [qkT], [psr[h % 2]])
                sTm = sTms.next()
                for h in range(4):
                    S.tt("dve", sTm.t[:, h, :], psr[h % 2].t[:, h * 128:(h + 1) * 128], Dp[:, h * 128:(h + 1) * 128], ALU.mult, [psr[h % 2], cf], [sTm])
                if rc < 5:
                    continue
                po = self.banks.next()
                for h in range(4):
                    jj, pb = h // 2, 64 * (h % 2)
                    S.mm(po.t[:, h * 128:(h + 1) * 128], sTm.t[:, h, :], vb.t[:, h * 128:(h + 1) * 128], True, False, [sTm, vb], [po])
                    S.mm(po.t[:, h * 128:(h + 1) * 128], qkT.t[pb:pb + 64, jj, :], Srb.t[pb:pb + 64, jj, (h % 2) * 128:(h % 2 + 1) * 128], False, True, [qkT, Srb], [po])
                if rc < 6:
                    continue
                pkv = self.banks.next()
                for jj in range(2):
                    S.mm(pkv.t[:, jj * 256:(jj + 1) * 256], qt.t[:, 2, jj * 128:(jj + 1) * 128], vb.t[:, jj * 256:(jj + 1) * 256], True, True, [qt, vb], [pkv])
                for jj in range(2):
                    S.stt("dve", Sr.t[:, jj, :], Sr.t[:, jj, :], cf.t[:, CF_GC + jj:CF_GC + jj + 1], pkv.t[:, jj * 256:(jj + 1) * 256], ALU.mult, ALU.add, [Sr, cf, pkv], [Sr])
                S.copy("act", Srb.t[:], Sr.t[:], [Sr], [Srb])
                if rc < 7:
                    continue
                ysq = ysqs.next()
                S.act(ysq.t[:], po.t[:], AF.Square, [po], [ysq])
                sm = self.smalls.next()
                S.op("dve", lambda e: e.tensor_reduce(out=sm.t[:, 0:4], in_=ysq.t[:].rearrange("p (h e) -> p h e", h=4), axis=AX.X, op=ALU.add), [ysq], [sm])
                S.act(sm.t[:, 4:8], sm.t[:, 0:4], AF.Sqrt, [sm], [sm], scale=1.0 / 128, bias=EPS)
                S.op("dve", lambda e: e.reciprocal(out=sm.t[:, 8:12], in_=sm.t[:, 4:8]), [sm], [sm])
                yn = yns.next()
                S.tt("dve", v3(yn.t[:]), v3(po.t[:]), sm.t[:, 8:12].rearrange("p (h e) -> p h e", e=1).broadcast_to([128, 4, 128]), ALU.mult, [po, sm], [yn])
                S.tt("dve", yn.t[:], yn.t[:], gret.t[:], ALU.mult, [yn, gret], [yn])
                yc = ycs.next()
                S.tt("dve", yc.t[:], yn.t[:], sg.t[:], ALU.mult, [yn, sg], [yc])
                S.dma("sync", self.yc_d[tsl, :], yc.t[:], reads=[yc], writes=[self.yc_b[c]])

    def merge(self, l, uT):
        S = self.S
        with contextlib.ExitStack() as es:
            Wg = S.sbt(es, "Wg", [128, 8, 3072], BF16)
            Wb = S.sbt(es, "Wb", [128, 12, D], BF16)
            Wo = S.sbt(es, "Wo", [128, 8, D], BF16)
            self.load_w(Wg, self.W["w_in"][l][:, C_GATE:C_GATE + 3072], 8)
            for b, nm in enumerate(("w_branch_rwkv", "w_branch_dil", "w_branch_ret")):
                for kc in range(4):
                    S.dma("pool", Wb.t[:, b * 4 + kc, :], self.W[nm][l][kc * 128:(kc + 1) * 128, :], writes=[Wb])
            self.load_w(Wo, self.W["w_out"][l], 8)
            yas = Rot([S.sbt(es, "ya%d" % i, [128, 512], BF16) for i in range(2)])
            ycs = Rot([S.sbt(es, "yc%d" % i, [128, 512], BF16) for i in range(2)])
            ybs = Rot([S.sbt(es, "yb%d" % i, [128, 512], BF16) for i in range(2)])
            oas = Rot([S.sbt(es, "oa%d" % i, [128, 3, 520], F32) for i in range(2)])
            hts = Rot([S.sbt(es, "ht%d" % i, [128, D], F32) for i in range(2)])
            yTs = Rot([S.sbt(es, "yT%d" % i, [128, 12, 128], BF16) for i in range(2)])
            mTs = Rot([S.sbt(es, "mT%d" % i, [128, 8, 128], BF16) for i in range(2)])
            sigs = Rot([S.sbt(es, "sig%d" % i, [128, 384], F32) for i in range(2)])
            prods = Rot([S.sbt(es, "prod%d" % i, [128, 384], F32) for i in range(2)])
            for ti in range(NT):
                tsl = slice(ti * 128, (ti + 1) * 128)
                ya, yc, yb, oa, ht = yas.next(), ycs.next(), ybs.next(), oas.next(), hts.next()
                S.dma("sync", ya.t[:], self.ya_d[tsl, :], reads=[self.ya_b[ti]], writes=[ya])
                S.dma("sync", yc.t[:], self.yc_d[tsl, :], reads=[self.yc_b[ti]], writes=[yc])
                for g in range(3):
                    S.dma("sync", oa.t[:, g, :], self.oa_d[g, tsl, :], reads=[self.oa_b[g]], writes=[oa])
                S.dma("sync", ht.t[:], self.out[tsl, :], reads=[self.hbuf[ti]], writes=[ht])
                S.tt("dve", oa.t[:, 0, :], oa.t[:, 0, :], oa.t[:, 1, :], ALU.add, [oa], [oa])
                S.tt("dve", oa.t[:, 0, :], oa.t[:, 0, :], oa.t[:, 2, :], ALU.add, [oa], [oa])
                o3 = oa.t[:, 0, :].rearrange("p (h e) -> p h e", h=8)
                sm = self.smalls.next()
                S.op("dve", lambda e: e.reciprocal(out=sm.t[:, 0:8].rearrange("p (h e) -> p h e", e=1), in_=o3[:, :, 64:65]), [oa], [sm])
                S.tt("dve", yb.t[:].rearrange("p (h e) -> p h e", h=8), o3[:, :, 0:64], sm.t[:, 0:8].rearrange("p (h e) -> p h e", e=1).broadcast_to([128, 8, 64]), ALU.mult, [oa, sm], [yb])
                yT = yTs.next()
                tb0, tb1 = self.tbanks.next(), self.tbanks.next()
                for kc in range(4):
                    S.tr(tb0.t[:, kc * 128:(kc + 1) * 128], ya.t[:, kc * 128:(kc + 1) * 128], self.cb, [ya], [tb0])
                for kc in range(4):
                    S.tr(tb0.t[:, 512 + kc * 128:512 + (kc + 1) * 128], yb.t[:, kc * 128:(kc + 1) * 128], self.cb, [yb], [tb0])
                for kc in range(4):
                    S.tr(tb1.t[:, kc * 128:(kc + 1) * 128], yc.t[:, kc * 128:(kc + 1) * 128], self.cb, [yc], [tb1])
                S.copy("act", yT.t[:, 0:8, :], tb0.t[:].rearrange("p (k t) -> p k t", k=8), [tb0], [yT])
                S.copy("act", yT.t[:, 8:12, :], tb1.t[:, 0:512].rearrange("p (k t) -> p k t", k=4), [tb1], [yT])
                mT = mTs.next()
                for oc in range(8):
                    pg_, pz = self.banks.next(), self.banks.next()
                    for b in range(3):
                        for kc in range(8):
                            S.mm(pg_.t[:, b * 128:(b + 1) * 128], Wg.t[:, kc, b * 1024 + oc * 128:b * 1024 + (oc + 1) * 128], uT.t[:, kc, tsl], kc == 0, kc == 7, [Wg, uT], [pg_])
                    for b in range(3):
                        for kc in range(4):
                            S.mm(pz.t[:, b * 128:(b + 1) * 128], Wb.t[:, b * 4 + kc, oc * 128:(oc + 1) * 128], yT.t[:, b * 4 + kc, :], kc == 0, kc == 3, [Wb, yT], [pz])
                    sig, prod = sigs.next(), prods.next()
                    S.act(sig.t[:], pg_.t[:, 0:384], AF.Sigmoid, [pg_], [sig])
                    S.tt("dve", prod.t[:], sig.t[:], pz.t[:, 0:384], ALU.mult, [sig, pz], [prod])
                    S.tt("dve", prod.t[:, 0:128], prod.t[:, 0:128], prod.t[:, 128:256], ALU.add, [prod], [prod])
                    S.tt("dve", mT.t[:, oc, :], prod.t[:, 0:128], prod.t[:, 256:384], ALU.add, [prod], [mT])
                for half in range(2):
                    po = self.banks.next()
                    for kc in range(8):
                        S.mm(po.t[:], mT.t[:, kc, :], Wo.t[:, kc, half * 512:(half + 1) * 512], kc == 0, kc == 7, [mT, Wo], [po])
                    hsl = ht.t[:, half * 512:(half + 1) * 512]
                    S.tt("dve", hsl, po.t[:], hsl, ALU.add, [po, ht], [ht])
                S.dma("sync", self.out[tsl, :], ht.t[:], reads=[ht], writes=[self.hbuf[ti]])


_CACHE = {}


def kernel(**inputs):
    stages = inputs.pop("_stages", None)
    debug = inputs.pop("_debug", False)
    ncores = inputs.pop("_ncores", 8)
    key = (None if stages is None else tuple(stages), debug)
    if key not in _CACHE:
        _CACHE[key] = Prog(stages, debug)
    prog = _CACHE[key]
    inp = {k: np.ascontiguousarray(np.asarray(v, dtype=np.float32)) for k, v in inputs.items()}
    colp, rowp = host_layout_params(inp)
    cb, cf = host_constants()
    in_maps = []
    for b in range(ncores):
        m = {"x": inp["x"][b], "mem": inp["mem"][b], "colp": colp, "rowp": rowp, "cb": cb, "cf": cf}
        for n, _ in WEIGHT_SPECS:
            m[n] = inp[n]
        in_maps.append(m)
    if debug == "time":
        res = run_bass_kernel_spmd(prog.nc, in_maps, core_ids=list(range(ncores)), trace=True)
        print("EXEC_TIME_NS", res.exec_time_ns)
    else:
        res = run_bass_kernel_spmd(prog.nc, in_maps, core_ids=list(range(ncores)))
    out = np.stack([np.asarray(r["out"], dtype=np.float32) for r in res.results], axis=0)
    if debug:
        return out, res.results
    return out
```

```python
import contextlib
import numpy as np
import concourse.bass as bass
import concourse.mybir as mybir
from concourse.bass_utils import run_bass_kernel_spmd

F32 = mybir.dt.float32
BF16 = mybir.dt.bfloat16
AF = mybir.ActivationFunctionType
ALU = mybir.AluOpType
AX = mybir.AxisListType


class Buf:
    def __init__(self, name, t=None, unordered=False):
        self.name = name
        self.t = t
        self.writers = {}
        self.readers = {}
        self.unordered = unordered


class _Eng:
    def __init__(self, name, eng, sem, step):
        self.name, self.eng, self.sem, self.step = name, eng, sem, step
        self.count = 0
        self.seen = {}


class Sched:
    N_DMA_SEMS = 12
    same_engine_sync = True

    def __init__(self, nc):
        self.nc = nc
        self.engs = {}
        self.queues = {}
        self._stack = contextlib.ExitStack()
        for name, eng in (("pe", nc.tensor), ("act", nc.scalar), ("dve", nc.vector), ("pool", nc.gpsimd),
                          ("sync", nc.sync)):
            sem = nc.alloc_semaphore("s_" + name)
            self.engs[name] = _Eng(name, eng, sem, 1)
        self.dma_pool = {}
        self.dma_rr = {}
        for q in ("sync", "pool"):
            lst = []
            for i in range(self.N_DMA_SEMS):
                nm = "d_%s%d" % (q, i)
                sem = nc.alloc_semaphore(nm)
                self.engs[nm] = _Eng(nm, None, sem, 16)
                lst.append(nm)
            self.dma_pool[q] = lst
            self.dma_rr[q] = 0
        self.n_ops = 0

    def sb(self, name, shape, dtype, unordered=False):
        t = self.nc.alloc_sbuf_tensor(name, list(shape), dtype)
        return Buf(name, t, unordered)

    def ps(self, name, shape, dtype):
        t = self.nc.alloc_psum_tensor(name, list(shape), dtype)
        return Buf(name, t)

    def block(self):
        return contextlib.nullcontext()

    def _waits(self, E, reads, writes):
        waits = {}
        for b in reads:
            for en, c in b.writers.items():
                if waits.get(en, 0) < c:
                    waits[en] = c
        for b in writes:
            if not b.unordered:
                for en, c in b.writers.items():
                    if waits.get(en, 0) < c:
                        waits[en] = c
            for en, c in b.readers.items():
                if waits.get(en, 0) < c:
                    waits[en] = c
        for en, c in waits.items():
            if E.seen.get(en, 0) >= c:
                continue
            X = self.engs[en]
            E.eng.wait_ge(X.sem, c * X.step)
            E.seen[en] = c

    def op(self, ename, fn, reads=(), writes=()):
        E = self.engs[ename]
        self._waits(E, reads, writes)
        ins = fn(E.eng)
        E.count += 1
        ins.then_inc(E.sem, 1)
        if not (self.same_engine_sync and ename != 'pe'):
            E.seen[ename] = E.count
        for b in reads:
            b.readers[ename] = E.count
        for b in writes:
            if b.unordered:
                b.writers[ename] = E.count
            else:
                b.writers = {ename: E.count}
                b.readers = {}
        self.n_ops += 1
        return ins

    def dma(self, q, out, in_, reads=(), writes=()):
        Q = self.engs[q]
        pool = self.dma_pool[q]
        vn = pool[self.dma_rr[q] % len(pool)]
        self.dma_rr[q] += 1
        V = self.engs[vn]
        if V.count > 0 and Q.seen.get(vn, 0) < V.count:
            Q.eng.wait_ge(V.sem, V.count * 16)
            Q.seen[vn] = V.count
        self._waits(Q, reads, writes)
        ins = Q.eng.dma_start(out=out, in_=in_)
        V.count += 1
        ins.then_inc(V.sem, 16)
        for b in reads:
            b.readers[vn] = V.count
        for b in writes:
            if b.unordered:
                b.writers[vn] = V.count
            else:
                b.writers = {vn: V.count}
                b.readers = {}
        self.n_ops += 1
        return ins

    def finish(self):
        Q = self.engs["sync"]
        for en, X in self.engs.items():
            if en == "sync" or X.count == 0:
                continue
            if Q.seen.get(en, 0) < X.count:
                Q.eng.wait_ge(X.sem, X.count * X.step)
                Q.seen[en] = X.count


    def barrier(self):
        names = ["pe", "act", "dve", "pool", "sync"]
        for en in names:
            E = self.engs[en]
            for xn, X in self.engs.items():
                if xn == en or X.count == 0:
                    continue
                if E.seen.get(xn, 0) < X.count:
                    E.eng.wait_ge(X.sem, X.count * X.step)
                    E.seen[xn] = X.count

    def sbt(self, es, name, shape, dtype, unordered=False):
        self._uid = getattr(self, "_uid", 0) + 1
        name = "sb%d_%s" % (self._uid, name)
        t = es.enter_context(self.nc.sbuf_tensor(name, list(shape), dtype))
        return Buf(name, t, unordered)

    def mm(self, out, lhsT, rhs, start, stop, reads, writes):
        return self.op("pe", lambda e: e.matmul(out, lhsT=lhsT, rhs=rhs, start=start, stop=stop), reads, writes)

    def tr(self, out, in_, ident, reads, writes):
        return self.op("pe", lambda e: e.transpose(out, in_, ident.t[:, 0:128]), list(reads) + [ident], writes)

    def act(self, out, in_, func, reads, writes, **kw):
        return self.op("act", lambda e: e.activation(out=out, in_=in_, func=func, **kw), reads, writes)

    def copy(self, eng, out, in_, reads, writes):
        if eng == "act":
            return self.op("act", lambda e: e.activation(out=out, in_=in_, func=AF.Copy), reads, writes)
        return self.op(eng, lambda e: e.tensor_copy(out=out, in_=in_), reads, writes)

    def tt(self, eng, out, in0, in1, op, reads, writes):
        return self.op(eng, lambda e: e.tensor_tensor(out=out, in0=in0, in1=in1, op=op), reads, writes)

    def ts(self, eng, out, in0, s1, op0, reads, writes, s2=None, op1=None):
        if op1 is None:
            return self.op(eng, lambda e: e.tensor_scalar(out=out, in0=in0, scalar1=s1, scalar2=None, op0=op0), reads, writes)
        return self.op(eng, lambda e: e.tensor_scalar(out=out, in0=in0, scalar1=s1, scalar2=s2, op0=op0, op1=op1), reads, writes)

    def stt(self, eng, out, in0, scalar, in1, op0, op1, reads, writes):
        return self.op(eng, lambda e: e.scalar_tensor_tensor(out=out, in0=in0, scalar=scalar, in1=in1, op0=op0, op1=op1), reads, writes)


def interleave(gens):
    gens = [g for g in gens if g is not None]
    while gens:
        for g in list(gens):
            try:
                next(g)
            except StopIteration:
                gens.remove(g)


class Rot:
    def __init__(self, items):
        self.items = list(items)
        self.i = 0

    def next(self):
        b = self.items[self.i % len(self.items)]
        self.i += 1
        return b


T = 4096
D = 1024
DFF = 2816
MEM = 256
NT = T // 128
EPS = 1e-6
GN_EPS = 64e-5
C_RW, C_DIL, C_RET, C_GATE = 0, 1824, 1824 + 4608, 1824 + 4608 + 1536
DILS = (1, 4, 16)
CP_MU, CP_W0, CP_A0, CP_KK, CP_KA, CP_XQ, CP_XK, CP_RK, CP_N = 0, 15, 19, 23, 27, 31, 33, 35, 67
RP_FFN1, RP_MIX, RP_XA, RP_MEM, RP_FFN2 = 0, 1024, 2048, 3072, 4096
RP_LNW, RP_LNB, RP_DQ, RP_DK, RP_RET, RP_N = 5120, 5632, 6144, 7680, 9216, 9728
CB_ID, CB_ONES, CB_BO2, CB_M3, CB_MPC, CB_N = 0, 128, 256, 384, 768, 1280
CF_DP, CF_QD, CF_KD, CF_GC, CF_DCOS, CF_DSIN, CF_RCOS, CF_RSIN, CF_N = 0, 512, 516, 520, 522, 1290, 2058, 3082, 4106


def host_constants():
    p = np.arange(128)
    cb = np.zeros((128, CB_N), np.float32)
    cb[:, CB_ID:CB_ID + 128] = np.eye(128)
    cb[:, CB_ONES:CB_ONES + 128] = 1.0
    cb[:, CB_BO2:CB_BO2 + 128] = (p[:, None] // 64 == p[None, :] // 64)
    cb[:, CB_M3:CB_M3 + 128] = (p[:, None] < p[None, :])
    cb[:, CB_M3 + 128:CB_M3 + 256] = (p[:, None] <= p[None, :])
    cb[:, CB_M3 + 256:CB_M3 + 384] = (p[None, :] < p[:, None])
    for hh in range(2):
        cb[:, CB_MPC + hh * 256:CB_MPC + hh * 256 + 128] = (p[:, None] >= p[None, :])
        cb[:, CB_MPC + hh * 256 + 128:CB_MPC + hh * 256 + 256] = (p[:, None] <= p[None, :])
    cf = np.zeros((128, CF_N), np.float64)
    gam = 1.0 - 2.0 ** (-5.0 - np.arange(4))
    for h in range(4):
        cf[:, CF_DP + h * 128:CF_DP + (h + 1) * 128] = (gam[h] ** (-(p[:, None] + 1.0))) * (p[:, None] <= p[None, :]) / 8.0
        cf[:, CF_QD + h] = gam[h] ** (p + 1.0)
        cf[:, CF_KD + h] = gam[h] ** (127.0 - p) / 8.0
    for j in range(2):
        cf[:, CF_GC + j] = gam[2 * j + p // 64] ** 128.0
    invf = 500000.0 ** (-np.arange(8, dtype=np.float64) / 8.0)
    for g, d in enumerate(DILS):
        nblk = 32 // d
        for c in range(d):
            for n in range(nblk):
                combo = c * nblk + n
                pos = ((n * 128 + p) * d + c).astype(np.float32)
                ang = (pos[:, None] * invf[None, :].astype(np.float32)).astype(np.float32)
                o = (g * 32 + combo) * 8
                cf[:, CF_DCOS + o:CF_DCOS + o + 8] = np.cos(ang)
                cf[:, CF_DSIN + o:CF_DSIN + o + 8] = np.sin(ang)
    invr = (10000.0 ** (-np.arange(32, dtype=np.float64) / 32.0)).astype(np.float32)
    for c in range(32):
        pos = (c * 128 + p).astype(np.float32)
        ang = (pos[:, None] * invr[None, :]).astype(np.float32)
        cf[:, CF_RCOS + c * 32:CF_RCOS + (c + 1) * 32] = np.cos(ang)
        cf[:, CF_RSIN + c * 32:CF_RSIN + (c + 1) * 32] = np.sin(ang)
    return cb.astype(np.float32), cf.astype(np.float32)


def host_layout_params(inp):
    L = 2
    colp = np.zeros((L, 128, CP_N), np.float32)
    rowp = np.zeros((L, RP_N), np.float32)
    p = np.arange(128)
    for l in range(L):
        mu = np.zeros(15 * 128, np.float32)
        mu[:1824] = inp["rw_mu"][l]
        colp[l, :, CP_MU:CP_MU + 15] = mu.reshape(15, 128).T
        for off, key in ((CP_W0, "rw_w0"), (CP_A0, "rw_a0"), (CP_KK, "rw_k_k"), (CP_KA, "rw_k_a")):
            colp[l, :, off:off + 4] = inp[key][l].reshape(4, 128).T
        colp[l, :, CP_XQ:CP_XQ + 2] = inp["xa_q_norm"][l].reshape(2, 128).T
        colp[l, :, CP_XK:CP_XK + 2] = inp["xa_k_norm"][l].reshape(2, 128).T
        rk = inp["rw_r_k"][l]
        for j in range(4):
            for h in range(8):
                if h // 2 == j:
                    rows = p[(p // 64) == (h % 2)]
                    colp[l, rows, CP_RK + j * 8 + h] = rk[h * 64 + (rows % 64)]
        for off, key in ((RP_FFN1, "norm_ffn1"), (RP_MIX, "norm_mix"), (RP_XA, "norm_xattn"), (RP_MEM, "norm_mem"),
                         (RP_FFN2, "norm_ffn2")):
            rowp[l, off:off + 1024] = inp[key][l]
        rowp[l, RP_LNW:RP_LNW + 512] = inp["rw_ln_w"][l]
        rowp[l, RP_LNB:RP_LNB + 512] = inp["rw_ln_b"][l]
        for g in range(3):
            rowp[l, RP_DQ + g * 512:RP_DQ + (g + 1) * 512] = np.tile(inp["dil_q_norm"][l, g], 8)
            rowp[l, RP_DK + g * 512:RP_DK + (g + 1) * 512] = np.tile(inp["dil_k_norm"][l, g], 8)
        rowp[l, RP_RET:RP_RET + 512] = inp["ret_norm"][l]
    return colp, rowp


WEIGHT_SPECS = [
    ("ffn1_w13", [2, D, 2 * DFF]), ("ffn1_w2", [2, DFF, D]), ("w_in", [2, D, 11040]),
    ("rw_w2", [2, 64, 512]), ("rw_a2", [2, 64, 512]), ("rw_g2", [2, 160, 512]),
    ("w_branch_rwkv", [2, 512, D]), ("w_branch_dil", [2, 512, D]), ("w_branch_ret", [2, 512, D]),
    ("w_out", [2, D, D]), ("xa_wq", [2, D, D]), ("xa_wkv", [2, D, 2 * D]), ("xa_wo", [2, D, D]),
    ("ffn2_w13", [2, D, 2 * DFF]), ("ffn2_w2", [2, DFF, D]),
]


class Prog:
    def __init__(self, stages=None, debug=False):
        self.stages = stages
        nc = self.nc = bass.Bass("TRN2", target_bir_lowering=False)
        self.S = S = Sched(nc)
        dt = lambda n, sh, ty=F32, kind="ExternalInput": nc.dram_tensor(n, sh, ty, kind=kind).ap()
        self.x = dt("x", [T, D])
        self.mem = dt("mem", [MEM, D])
        self.W = {n: dt(n, sh) for n, sh in WEIGHT_SPECS}
        self.colp_d = dt("colp", [2, 128, CP_N])
        self.rowp_d = dt("rowp", [2, RP_N])
        self.cb_d = dt("cb", [128, CB_N])
        self.cf_d = dt("cf", [128, CF_N])
        self.out = dt("out", [T, D], F32, "ExternalOutput")
        okind = "ExternalOutput" if debug else "Internal"
        self.ya_d = dt("ya_s", [T, 512], BF16, okind)
        self.yc_d = dt("yc_s", [T, 512], BF16, okind)
        self.oa_d = dt("oa_s", [3, T, 520], F32, okind)
        self.hbuf = [Buf("h%d" % i) for i in range(NT)]
        self.ya_b = [Buf("ya%d" % i) for i in range(NT)]
        self.yc_b = [Buf("yc%d" % i) for i in range(NT)]
        self.oa_b = [Buf("oa%d" % g, unordered=True) for g in range(3)]
        self.bank_list = [S.ps("pb%d" % i, [128, 512], F32) for i in range(6)]
        self.banks = Rot(self.bank_list)
        self.tbanks = Rot([S.ps("pt%d" % i, [128, 1024], BF16) for i in range(2)])
        with contextlib.ExitStack() as es:
            self.cb = S.sbt(es, "cb", [128, CB_N], BF16)
            S.dma("pool", self.cb.t[:], self.cb_d, writes=[self.cb])
            self.ident = self.cb
            self.colp = S.sbt(es, "colp", [128, CP_N], F32)
            self.smalls = Rot([S.sbt(es, "sm%d" % i, [128, 16], F32) for i in range(8)])
            self.junk = S.sbt(es, "junk", [128, 1024], BF16)
            self.xn = Rot([S.sbt(es, "xn%d" % i, [128, 1024], BF16) for i in range(2)])
            for l in range(2):
                self.l = l
                S.barrier()
                S.dma("sync", self.colp.t[:], self.colp_d[l], writes=[self.colp])
                self.run("cp", self.copy_x)
                self.run("ffn1", lambda: self.ffn(es, l, RP_FFN1, self.W["ffn1_w13"][l], self.W["ffn1_w2"][l], first=(l == 0)))
                self.run("mix", lambda: self.mixer(es, l))
                self.run("xa", lambda: self.xattn(es, l))
                self.run("ffn2", lambda: self.ffn(es, l, RP_FFN2, self.W["ffn2_w13"][l], self.W["ffn2_w2"][l], first=False))
            S.barrier()
            S.finish()

    def run(self, name, fn):
        key = "%s%d" % (name, self.l)
        if self.stages is not None and key not in self.stages:
            return
        if self.stages is None and name == "cp":
            return
        fn()
        self.S.barrier()

    def copy_x(self):
        with contextlib.ExitStack() as es:
            hts = Rot([self.S.sbt(es, "cpx%d" % i, [128, D], F32) for i in range(2)])
            for ti in range(NT):
                ht = hts.next()
                self.S.dma("sync", ht.t[:], self.x[ti * 128:(ti + 1) * 128, :], writes=[ht])
                self.S.dma("sync", self.out[ti * 128:(ti + 1) * 128, :], ht.t[:], reads=[ht], writes=[self.hbuf[ti]])

    def idn(self):
        return self.cb.t[:, CB_ID:CB_ID + 128]

    def norm_T(self, src_ap, src_bufs, gain, dstT, col0, ht, nparts=128):
        S = self.S
        S.dma("sync", ht.t[:], src_ap, reads=src_bufs, writes=[ht])
        sm = self.smalls.next()
        S.act(self.junk.t[:], ht.t[:], AF.Square, [ht], [self.junk, sm], accum_out=sm.t[:, 0:1])
        S.act(sm.t[:, 1:2], sm.t[:, 0:1], AF.Sqrt, [sm], [sm], scale=1.0 / D, bias=EPS)
        S.op("dve", lambda e: e.reciprocal(out=sm.t[:, 2:3], in_=sm.t[:, 1:2]), [sm], [sm])
        xn = self.xn.next()
        S.stt("dve", xn.t[:], ht.t[:], sm.t[:, 2:3], gain.t[:], ALU.mult, ALU.mult, [ht, sm, gain], [xn])
        tb = self.tbanks.next()
        for kc in range(8):
            S.tr(tb.t[:, kc * 128:(kc + 1) * 128], xn.t[:, kc * 128:(kc + 1) * 128], self.cb, [xn], [tb])
        S.copy("act", dstT.t[:, 0:8, col0:col0 + 128], tb.t[:].rearrange("p (k t) -> p k t", k=8), [tb], [dstT])

    def load_w(self, dst, src_ap, nk, rows=128):
        for kc in range(nk):
            self.S.dma("pool", dst.t[0:rows, kc, :], src_ap[kc * 128:kc * 128 + rows, :], writes=[dst])

    def load_row(self, dst, l, off, n):
        self.S.dma("sync", dst.t[:, 0:n], self.rowp_d[l, off:off + n].partition_broadcast(128), writes=[dst])

    def ffn(self, es0, l, rp_off, w13_ap, w2_ap, first):
        S = self.S
        TT = 256
        with contextlib.ExitStack() as es:
            w13 = S.sbt(es, "w13", [128, 8, 2 * DFF], BF16)
            w2 = S.sbt(es, "w2", [128, 22, D], BF16)
            gain = S.sbt(es, "gain", [128, D], F32)
            self.load_row(gain, l, rp_off, D)
            self.load_w(w13, w13_ap, 8)
            self.load_w(w2, w2_ap, 22)
            nTs = Rot([S.sbt(es, "nT%d" % i, [128, 8, TT], BF16) for i in range(2)])
            gTs = Rot([S.sbt(es, "gT%d" % i, [128, 22, TT], BF16) for i in range(1)])
            hts = Rot([S.sbt(es, "ht%d" % i, [128, D], F32) for i in range(4)])
            sas = Rot([S.sbt(es, "sa%d" % i, [128, TT], F32) for i in range(3)])
            src = self.x if first else self.out
            for tt in range(T // TT):
                n = nTs.next()
                hs = []
                for s in range(2):
                    ti = tt * 2 + s
                    ht = hts.next()
                    self.norm_T(src[ti * 128:(ti + 1) * 128, :], [] if first else [self.hbuf[ti]], gain, n, s * 128, ht)
                    hs.append(ht)
                g = gTs.next()
                for fc in range(22):
                    pa = self.banks.next()
                    pb = self.banks.next()
                    for kc in range(8):
                        S.mm(pa.t[:, 0:TT], w13.t[:, kc, fc * 128:(fc + 1) * 128], n.t[:, kc, :], kc == 0, kc == 7, [w13, n], [pa])
                    for kc in range(8):
                        S.mm(pb.t[:, 0:TT], w13.t[:, kc, DFF + fc * 128:DFF + (fc + 1) * 128], n.t[:, kc, :], kc == 0, kc == 7, [w13, n], [pb])
                    sa = sas.next()
                    S.act(sa.t[:], pa.t[:, 0:TT], AF.Silu, [pa], [sa])
                    S.tt("dve", g.t[:, fc, :], sa.t[:], pb.t[:, 0:TT], ALU.mult, [sa, pb], [g])
                for s in range(2):
                    ti = tt * 2 + s
                    for half in range(2):
                        po = self.banks.next()
                        for fc in range(22):
                            S.mm(po.t[:], g.t[:, fc, s * 128:(s + 1) * 128], w2.t[:, fc, half * 512:(half + 1) * 512], fc == 0, fc == 21, [g, w2], [po])
                        hsl = hs[s].t[:, half * 512:(half + 1) * 512]
                        S.stt("dve", hsl, po.t[:], 0.5, hsl, ALU.mult, ALU.add, [po, hs[s]], [hs[s]])
                    S.dma("sync", self.out[ti * 128:(ti + 1) * 128, :], hs[s].t[:], reads=[hs[s]], writes=[self.hbuf[ti]])

    def xattn(self, es0, l):
        S = self.S
        TT = 256
        ones = self.cb.t[:, CB_ONES:CB_ONES + 128]
        with contextlib.ExitStack() as es:
            wq = S.sbt(es, "wq", [128, 8, D], BF16)
            wkv = S.sbt(es, "wkv", [128, 8, 2 * D], BF16)
            wo = S.sbt(es, "wo", [128, 8, D], BF16)
            gx = S.sbt(es, "gx", [128, D], F32)
            gm = S.sbt(es, "gm", [128, D], F32)
            self.load_row(gx, l, RP_XA, D)
            self.load_row(gm, l, RP_MEM, D)
            self.load_w(wkv, self.W["xa_wkv"][l], 8)
            self.load_w(wq, self.W["xa_wq"][l], 8)
            self.load_w(wo, self.W["xa_wo"][l], 8)
            mnT = S.sbt(es, "mnT", [128, 8, MEM], BF16)
            KnT = S.sbt(es, "KnT", [128, 8, MEM], BF16)
            V = S.sbt(es, "V", [128, 2, D], BF16)
            qf = S.sbt(es, "qf", [128, 8, TT], F32)
            sq = S.sbt(es, "sq", [128, 8, TT], BF16)
            rrs = Rot([S.sbt(es, "rr%d" % i, [128, 2, TT], F32) for i in range(2)])
            hts = Rot([S.sbt(es, "ht%d" % i, [128, D], F32) for i in range(4)])
            nTs = Rot([S.sbt(es, "nT%d" % i, [128, 8, TT], BF16) for i in range(2)])
            qnT = S.sbt(es, "qnT", [128, 8, TT], BF16)
            oT = S.sbt(es, "oT", [128, 8, TT], BF16)
            Es = Rot([S.sbt(es, "E%d" % i, [128, 2, TT], BF16) for i in range(2)])

            def qknorm(src_w, c0, rhsT, dst, cpoff):
                for fc in range(8):
                    pq = self.banks.next()
                    for kc in range(8):
                        S.mm(pq.t[:, 0:TT], src_w.t[:, kc, c0 + fc * 128:c0 + (fc + 1) * 128], rhsT.t[:, kc, :], kc == 0, kc == 7, [src_w, rhsT], [pq])
                    S.copy("dve", qf.t[:, fc, :], pq.t[:, 0:TT], [pq], [qf])
                    S.act(sq.t[:, fc, :], qf.t[:, fc, :], AF.Square, [qf], [sq])
                import os
                qc = int(os.environ.get("QK_CUT", "99"))
                if qc < 1:
                    return
                for hh in range(4):
                    pss = self.banks.next()
                    for c2 in range(2):
                        S.mm(pss.t[:, 0:TT], ones, sq.t[:, hh * 2 + c2, :], c2 == 0, c2 == 1, [self.cb, sq], [pss])
                    rr = rrs.next()
                    S.act(rr.t[:, 0, :], pss.t[:, 0:TT], AF.Sqrt, [pss], [rr], scale=1.0 / 256, bias=EPS)
                    if qc < 2:
                        continue
                    S.op("dve", lambda e: e.reciprocal(out=rr.t[:, 1, :], in_=rr.t[:, 0, :]), [rr], [rr])
                    if qc < 3:
                        continue
                    for c2 in range(2):
                        fc = hh * 2 + c2
                        S.stt("dve", dst.t[:, fc, :], qf.t[:, fc, :], self.colp.t[:, cpoff + c2:cpoff + c2 + 1], rr.t[:, 1, :], ALU.mult, ALU.mult, [qf, rr, self.colp], [dst])

            for mt in range(2):
                self.norm_T(self.mem[mt * 128:(mt + 1) * 128, :], [], gm, mnT, mt * 128, hts.next())
            import os
            cut = int(os.environ.get("XA_CUT", "99"))
            if cut < 1:
                return
            qknorm(wkv, 0, mnT, KnT, CP_XK)
            if cut < 2:
                return
            for mt in range(2):
                for half in range(2):
                    pv = self.banks.next()
                    for kc in range(8):
                        S.mm(pv.t[:], mnT.t[:, kc, mt * 128:(mt + 1) * 128], wkv.t[:, kc, D + half * 512:D + (half + 1) * 512], kc == 0, kc == 7, [mnT, wkv], [pv])
                    S.copy("act", V.t[:, mt, half * 512:(half + 1) * 512], pv.t[:], [pv], [V])
            if cut < 3:
                return
            for tt in range(T // TT):
                n = nTs.next()
                hs = []
                for s in range(2):
                    ti = tt * 2 + s
                    ht = hts.next()
                    self.norm_T(self.out[ti * 128:(ti + 1) * 128, :], [self.hbuf[ti]], gx, n, s * 128, ht)
                    hs.append(ht)
                qknorm(wq, 0, n, qnT, CP_XQ)
                if cut < 4:
                    continue
                for hh in range(4):
                    E = Es.next()
                    for mc in range(2):
                        ps_ = self.banks.next()
                        for c2 in range(2):
                            S.mm(ps_.t[:, 0:TT], KnT.t[:, hh * 2 + c2, mc * 128:(mc + 1) * 128], qnT.t[:, hh * 2 + c2, :], c2 == 0, c2 == 1, [KnT, qnT], [ps_])
                        S.act(E.t[:, mc, :], ps_.t[:, 0:TT], AF.Exp, [ps_], [E], scale=1.0 / 16)
                    pden = self.banks.next()
                    for mc in range(2):
                        S.mm(pden.t[:, 0:TT], ones, E.t[:, mc, :], mc == 0, mc == 1, [self.cb, E], [pden])
                    rr = rrs.next()
                    S.op("dve", lambda e: e.reciprocal(out=rr.t[:, 0, :], in_=pden.t[:, 0:TT]), [pden], [rr])
                    for dc in range(2):
                        po = self.banks.next()
                        for mc in range(2):
                            S.mm(po.t[:, 0:TT], V.t[:, mc, hh * 256 + dc * 128:hh * 256 + (dc + 1) * 128], E.t[:, mc, :], mc == 0, mc == 1, [V, E], [po])
                        S.tt("dve", oT.t[:, hh * 2 + dc, :], po.t[:, 0:TT], rr.t[:, 0, :], ALU.mult, [po, rr], [oT])
                for s in range(2):
                    ti = tt * 2 + s
                    for half in range(2):
                        po = self.banks.next()
                        for kc in range(8):
                            S.mm(po.t[:], oT.t[:, kc, s * 128:(s + 1) * 128], wo.t[:, kc, half * 512:(half + 1) * 512], kc == 0, kc == 7, [oT, wo], [po])
                        hsl = hs[s].t[:, half * 512:(half + 1) * 512]
                        S.tt("dve", hsl, po.t[:], hsl, ALU.add, [po, hs[s]], [hs[s]])
                    S.dma("sync", self.out[ti * 128:(ti + 1) * 128, :], hs[s].t[:], reads=[hs[s]], writes=[self.hbuf[ti]])

    def mixer(self, es0, l):
        S = self.S
        with contextlib.ExitStack() as es:
            sub = self.stages_sub()
            if "rw" in sub:
                self.rwkv(l)
                S.barrier()
            uT = S.sbt(es, "uT", [128, 8, T], BF16)
            with contextlib.ExitStack() as es2:
                gain = S.sbt(es2, "gmix", [128, D], F32)
                self.load_row(gain, l, RP_MIX, D)
                hts = Rot([S.sbt(es2, "ht%d" % i, [128, D], F32) for i in range(3)])
                for ti in range(NT):
                    self.norm_T(self.out[ti * 128:(ti + 1) * 128, :], [self.hbuf[ti]], gain, uT, ti * 128, hts.next())
            S.barrier()
            with contextlib.ExitStack() as es3:
                cf = S.sbt(es3, "cf", [128, CF_N], F32)
                S.dma("sync", cf.t[:], self.cf_d, writes=[cf])
                if "dil" in sub:
                    self.dilated(l, uT, cf)
                    S.barrier()
                if "ret" in sub:
                    self.retention(l, uT, cf)
                    S.barrier()
            S.barrier()
            if "merge" in sub:
                self.merge(l, uT)

    def stages_sub(self):
        if self.stages is None:
            return ("rw", "dil", "ret", "merge")
        return [k for k in ("rw", "dil", "ret", "merge") if k in self.stages]

    def dilated(self, l, uT, cf):
        S = self.S
        with contextlib.ExitStack() as es:
            Wgs = [S.sbt(es, "Wd%d" % i, [128, 8, 1536], BF16) for i in range(2)]
            gq = S.sbt(es, "gq", [128, 1536], F32)
            gk = S.sbt(es, "gk", [128, 1536], F32)
            self.load_row(gq, l, RP_DQ, 1536)
            self.load_row(gk, l, RP_DK, 1536)
            NB = 3
            qks = [S.sbt(es, "dqk%d" % i, [128, 8, 128], BF16) for i in range(NB)]
            Vas = [S.sbt(es, "dVa%d" % i, [128, 8, 65], BF16) for i in range(NB)]
            for v in Vas:
                S.op("dve", lambda e: e.memset(v.t[:], 1.0), [], [v])
            sqs = [S.sbt(es, "dsq%d" % i, [128, 512], F32) for i in range(2)]
            xns = [S.sbt(es, "dxn%d" % i, [128, 8, 64], F32) for i in range(2)]
            tms = [S.sbt(es, "dtm%d" % i, [128, 4, 8, 8], F32) for i in range(2)]
            sms = [S.sbt(es, "dsm%d" % i, [128, 16], F32) for i in range(2)]
            qrs = Rot([S.sbt(es, "dqr%d" % i, [128, 2, 512], BF16) for i in range(2)])
            Ees = Rot([S.sbt(es, "dEe%d" % i, [128, 512], BF16) for i in range(3)])
            Ems = Rot([S.sbt(es, "dEm%d" % i, [128, 4, 512], BF16) for i in range(2)])
            Osb = Rot([S.sbt(es, "dO%d" % i, [128, 520], F32) for i in range(2)])
            mpc = self.cb.t[:, CB_MPC:CB_MPC + 512]
            h8 = lambda ap: ap.rearrange("p (h e) -> p h e", h=8)
            flat = lambda ap: ap.rearrange("p h e -> p (h e)")
            gains = (gq, gk)

            def load_group(g):
                self.load_w(Wgs[g % 2], self.W["w_in"][l][:, C_DIL + g * 1536:C_DIL + (g + 1) * 1536], 8)

            bankA, bankB = Rot(self.bank_list[0:3]), Rot(self.bank_list[3:6])

            def stage_a(g, d, c, n, idx, W):
                bk = bankA
                nblk = 32 // d
                combo = c * nblk + n
                tok0 = n * 128 * d + c
                cur = idx % NB
                pq, pk, pv = bk.next(), bk.next(), bk.next()
                pps = (pq, pk)
                for pp, jq in ((pq, 0), (pk, 1), (pv, 2)):
                    for kc in range(8):
                        S.mm(pp.t[:], uT.t[:, kc, tok0:tok0 + 127 * d + 1:d], W.t[:, kc, jq * 512:(jq + 1) * 512], kc == 0, kc == 7, [uT, W], [pp])
                    yield
                S.copy("act", Vas[cur].t[:, :, 0:64], h8(pv.t[:]), [pv], [Vas[cur]])
                qr = qrs.next()
                o = (g * 32 + combo) * 8
                cosb = cf.t[:, CF_DCOS + o:CF_DCOS + o + 8].rearrange("p (a e) -> p a e", a=1).broadcast_to([128, 8, 8])
                sinb = cf.t[:, CF_DSIN + o:CF_DSIN + o + 8].rearrange("p (a e) -> p a e", a=1).broadcast_to([128, 8, 8])
                bc = lambda ap: ap.rearrange("p (h e) -> p h e", e=1).broadcast_to([128, 8, 64])
                steps = [
                    lambda w: S.copy("dve", flat(xns[w].t[:]), pps[w].t[:], [pps[w]], [xns[w]]),
                    lambda w: S.act(sqs[w].t[:], flat(xns[w].t[:]), AF.Square, [xns[w]], [sqs[w]]),
                    lambda w: S.op("dve", lambda e: e.tensor_reduce(out=sms[w].t[:, 0:8], in_=h8(sqs[w].t[:]), axis=AX.X, op=ALU.add), [sqs[w]], [sms[w]]),
                    lambda w: S.act(sms[w].t[:, 8:16], sms[w].t[:, 0:8], AF.Sqrt, [sms[w]], [sms[w]], scale=1.0 / 64, bias=EPS),
                    lambda w: S.op("dve", lambda e: e.reciprocal(out=sms[w].t[:, 0:8], in_=sms[w].t[:, 8:16]), [sms[w]], [sms[w]]),
                    lambda w: S.tt("dve", xns[w].t[:], xns[w].t[:], bc(sms[w].t[:, 0:8]), ALU.mult, [xns[w], sms[w]], [xns[w]]),
                    lambda w: S.tt("dve", xns[w].t[:], xns[w].t[:], h8(gains[w].t[:, g * 512:(g + 1) * 512]), ALU.mult, [xns[w], gains[w]], [xns[w]]),
                    lambda w: S.tt("dve", tms[w].t[:, 0], xns[w].t[:, :, 0:8], cosb, ALU.mult, [xns[w], cf], [tms[w]]),
                    lambda w: S.tt("dve", tms[w].t[:, 1], xns[w].t[:, :, 8:16], sinb, ALU.mult, [xns[w], cf], [tms[w]]),
                    lambda w: S.tt("dve", tms[w].t[:, 2], xns[w].t[:, :, 8:16], cosb, ALU.mult, [xns[w], cf], [tms[w]]),
                    lambda w: S.tt("dve", tms[w].t[:, 3], xns[w].t[:, :, 0:8], sinb, ALU.mult, [xns[w], cf], [tms[w]]),
                    lambda w: S.tt("dve", xns[w].t[:, :, 0:8], tms[w].t[:, 0], tms[w].t[:, 1], ALU.subtract, [tms[w], xns[w]], [xns[w]]),
                    lambda w: S.tt("dve", xns[w].t[:, :, 8:16], tms[w].t[:, 2], tms[w].t[:, 3], ALU.add, [tms[w], xns[w]], [xns[w]]),
                    lambda w: S.copy("act", qr.t[:, w, :], flat(xns[w].t[:]), [xns[w]], [qr]),
                ]
                for st_ in steps:
                    for w in range(2):
                        st_(w)
                        yield
                tb = self.tbanks.next()
                for w in range(2):
                    for j in range(4):
                        S.tr(tb.t[:, (w * 4 + j) * 128:(w * 4 + j + 1) * 128], qr.t[:, w, j * 128:(j + 1) * 128], self.cb, [qr], [tb])
                    yield
                S.copy("act", qks[cur].t[:, 0:8, :], tb.t[:].rearrange("p (k t) -> p k t", k=8), [tb], [qks[cur]])
                yield

            def stage_b(g, d, c, n, idx):
                bk = bankB
                tok0 = n * 128 * d + c
                cur, prv = idx % NB, (idx - 1) % NB
                qk = qks[cur]
                Em = Ems.next()
                for j in range(4):
                    psd = (bk.next(), bk.next())
                    for hh in range(2):
                        pb = 64 * hh
                        ps_ = psd[hh]
                        if n > 0:
                            S.mm(ps_.t[:, hh * 256:hh * 256 + 128], qks[prv].t[pb:pb + 64, 4 + j, :], qk.t[pb:pb + 64, j, :], True, True, [qks[prv], qk], [ps_])
                        S.mm(ps_.t[:, hh * 256 + 128:hh * 256 + 256], qk.t[pb:pb + 64, 4 + j, :], qk.t[pb:pb + 64, j, :], True, True, [qk], [ps_])
                    yield
                    Ee = Ees.next()
                    for hh in range(2):
                        ps_ = psd[hh]
                        lo = hh * 256 if n > 0 else hh * 256 + 128
                        hi = hh * 256 + 256
                        S.act(Ee.t[:, lo:hi], ps_.t[:, lo:hi], AF.Exp, [ps_], [Ee], scale=0.125)
                        yield
                    for hh in range(2):
                        lo = hh * 256 if n > 0 else hh * 256 + 128
                        hi = hh * 256 + 256
                        S.tt("dve", Em.t[:, j, lo:hi], Ee.t[:, lo:hi], mpc[:, lo:hi], ALU.mult, [Ee, self.cb], [Em])
                        yield
                O = Osb.next()
                for half in range(2):
                    po = bk.next()
                    for hq in range(4):
                        h = half * 4 + hq
                        j, hh = h // 2, h % 2
                        if n > 0:
                            S.mm(po.t[:, hq * 65:(hq + 1) * 65], Em.t[:, j, hh * 256:hh * 256 + 128], Vas[prv].t[:, h, :], True, False, [Em, Vas[prv]], [po])
                        S.mm(po.t[:, hq * 65:(hq + 1) * 65], Em.t[:, j, hh * 256 + 128:hh * 256 + 256], Vas[cur].t[:, h, :], n == 0, True, [Em, Vas[cur]], [po])
                    yield
                    S.copy("act", O.t[:, half * 260:(half + 1) * 260], po.t[:, 0:260], [po], [O])
                    yield
                S.dma("sync", self.oa_d[g, tok0:tok0 + 127 * d + 1:d, :], O.t[:], reads=[O], writes=[self.oa_b[g]])
                yield

            items = []
            for g, d in enumerate(DILS):
                for c in range(d):
                    for n in range(32 // d):
                        items.append((g, d, c, n))
            load_group(0)
            load_group(1)
            loaded = {0, 1}
            prev_b = None
            for idx, (g, d, c, n) in enumerate(items):
                if g == 1 and 2 not in loaded and c == 0 and n == 1:
                    load_group(2)
                    loaded.add(2)
                ga = stage_a(g, d, c, n, idx, Wgs[g % 2])
                interleave([ga, prev_b])
                prev_b = stage_b(g, d, c, n, idx)
            interleave([prev_b])

    def rwkv(self, l):
        S = self.S
        ST = 256
        NJ = 15
        with contextlib.ExitStack() as es:
            W = S.sbt(es, "Wrw", [128, 8, 1824], BF16)
            self.load_w(W, self.W["w_in"][l][:, 0:1824], 8)
            w2s = S.sbt(es, "w2s", [128, 512], BF16)
            a2s = S.sbt(es, "a2s", [128, 512], BF16)
            g2a = S.sbt(es, "g2a", [128, 512], BF16)
            g2b = S.sbt(es, "g2b", [128, 512], BF16)
            S.dma("pool", w2s.t[0:64, :], self.W["rw_w2"][l], writes=[w2s])
            S.dma("pool", a2s.t[64:128, :], self.W["rw_a2"][l], writes=[a2s])
            S.dma("pool", g2a.t[:, :], self.W["rw_g2"][l][0:128, :], writes=[g2a])
            S.dma("pool", g2b.t[0:32, :], self.W["rw_g2"][l][128:160, :], writes=[g2b])
            lnw = S.sbt(es, "lnw", [128, 512], F32)
            lnb = S.sbt(es, "lnb", [128, 512], F32)
            gain = S.sbt(es, "grw", [128, D], F32)
            self.load_row(lnw, l, RP_LNW, 512)
            self.load_row(lnb, l, RP_LNB, 512)
            self.load_row(gain, l, RP_MIX, D)
            hts = Rot([S.sbt(es, "rht%d" % i, [128, D], F32) for i in range(2)])
            nTs = [S.sbt(es, "rnT%d" % i, [128, 8, ST], BF16) for i in range(2)]
            cp = self.colp
            omk = S.sbt(es, "omk", [128, 4], F32)
            S.ts("dve", omk.t[:], cp.t[:, CP_KA:CP_KA + 4], -1.0, ALU.mult, [cp], [omk], s2=1.0, op1=ALU.add)
            rkb = S.sbt(es, "rkb", [128, 32], BF16)
            S.copy("dve", rkb.t[:], cp.t[:, CP_RK:CP_RK + 32], [cp], [rkb])
            onesf = S.sbt(es, "onesf", [128, 128], F32)
            S.op("dve", lambda e: e.memset(onesf.t[:], 1.0), [], [onesf])
            St = S.sbt(es, "St", [128, 4, 128], F32)
            Sb = S.sbt(es, "Sb", [128, 4, 128], BF16)
            Stmp = S.sbt(es, "Stmp", [128, 4, 128], F32)
            prevc = S.sbt(es, "prevc", [128, NJ], F32)
            for t_ in (St, Sb, prevc):
                S.op("dve", lambda e: e.memset(t_.t[:], 0.0), [], [t_])
            PM = S.sbt(es, "PM", [128, NJ, ST], F32)
            Pjs = Rot([S.sbt(es, "Pj%d" % i, [128, ST + 1], F32) for i in range(2)])
            dts = Rot([S.sbt(es, "dt%d" % i, [128, ST], F32) for i in range(2)])
            f32t = lambda nm: S.sbt(es, nm, [128, ST], F32)
            sgT, aT, lwT, LT, LpT, GT, GiT, GpT, kkkT, nrT, kkT, tkT, kmT, beT = [f32t("rw%d" % i) for i in range(14)]
            thw = S.sbt(es, "thw", [128, ST], BF16)
            adb = S.sbt(es, "adb", [128, ST], BF16)
            sqk = S.sbt(es, "sqk", [128, ST], BF16)
            two = lambda nm, sh, ty: [S.sbt(es, "%s%d" % (nm, i), sh, ty) for i in range(2)]
            sgd2 = two("sgd", [128, 2, ST], BF16)
            AR2 = two("AR", [128, 4, 2, 2, 128], BF16)
            BT2 = two("BT", [128, 4, ST], BF16)
            KT2 = two("KT", [128, 4, ST], BF16)
            VT2 = two("VT", [128, 4, ST], BF16)
            RK2 = two("RK", [128, 4, ST], BF16)
            GC2 = two("GC", [128, 4, 2], F32)
            BKtok2 = two("BKtok", [128, 2, 512], BF16)
            Vtok2 = two("Vtok", [128, 512], BF16)
            AXt2 = two("AXt", [128, 8, 384], BF16)
            AYt2 = two("AYt", [128, 8, 256], BF16)
            y22 = two("y2g", [128, 512], F32)
            smb2 = two("smb", [128, 16], F32)
            TT2 = [two("TT%d_" % i, [128, 8, 128], BF16) for i in range(2)]
            Bts = two("Bt", [128, 8, 128], BF16)
            Ans = two("An", [128, 8, 128], BF16)
            Xb = S.sbt(es, "Xb", [128, 512], BF16)
            Ub = S.sbt(es, "Ub", [128, 512], BF16)
            ysq = S.sbt(es, "ysq", [128, 512], F32)
            y1 = S.sbt(es, "y1", [128, 512], F32)
            yo = Rot([S.sbt(es, "yo%d" % i, [128, 512], BF16) for i in range(2)])
            smr = [S.sbt(es, "smr%d" % i, [128, 16], F32) for i in range(2)]
            m3 = self.cb.t[:, CB_M3:CB_M3 + 384]
            bo2 = self.cb.t[:, CB_BO2:CB_BO2 + 128]
            h8 = lambda ap: ap.rearrange("p (h e) -> p h e", h=8)
            c2 = lambda ap: ap.rearrange("p (c t) -> p c t", c=2)
            col = lambda off, j: cp.t[:, off + j:off + j + 1]
            TTfin = {}

            bankP = bankN = bankR = self.banks

            def prep(st):
                bk = bankP
                sp = st % 2
                nT, sgd, AR, BT, KT, VT, RK, GC = nTs[sp], sgd2[sp], AR2[sp], BT2[sp], KT2[sp], VT2[sp], RK2[sp], GC2[sp]
                for s_ in range(2):
                    ti = st * 2 + s_
                    self.norm_T(self.out[ti * 128:(ti + 1) * 128, :], [self.hbuf[ti]], gain, nT, s_ * 128, hts.next())
                    yield
                for jc in range(NJ):
                    M = 128 if jc < 14 else 32
                    pp = bk.next()
                    for kc in range(8):
                        S.mm(pp.t[0:M, 0:ST], W.t[:, kc, jc * 128:jc * 128 + M], nT.t[:, kc, :], kc == 0, kc == 7, [W, nT], [pp])
                    Pj = Pjs.next()
                    S.copy("act", Pj.t[0:M, 1:ST + 1], pp.t[0:M, 0:ST], [pp], [Pj])
                    S.copy("dve", Pj.t[0:M, 0:1], prevc.t[0:M, jc:jc + 1], [prevc], [Pj])
                    yield
                    dtm = dts.next()
                    S.tt("dve", dtm.t[0:M, :], Pj.t[0:M, 0:ST], Pj.t[0:M, 1:ST + 1], ALU.subtract, [Pj], [dtm])
                    S.stt("dve", PM.t[0:M, jc, :], dtm.t[0:M, :], col(CP_MU, jc)[0:M], Pj.t[0:M, 1:ST + 1], ALU.mult, ALU.add, [dtm, Pj, cp], [PM])
                    S.copy("act", prevc.t[0:M, jc:jc + 1], Pj.t[0:M, ST:ST + 1], [Pj], [prevc])
                    yield
                S.act(thw.t[0:64, :], PM.t[0:64, 12, :], AF.Tanh, [PM], [thw])
                S.copy("act", adb.t[64:128, :], PM.t[64:128, 12, :], [PM], [adb])
                S.act(sgd.t[:, 0, :], PM.t[:, 13, :], AF.Sigmoid, [PM], [sgd])
                S.act(sgd.t[0:32, 1, :], PM.t[0:32, 14, :], AF.Sigmoid, [PM], [sgd])
                yield
                for j in range(4):
                    rj, kj, vj = PM.t[:, j, :], PM.t[:, 4 + j, :], PM.t[:, 8 + j, :]
                    px, px2 = bk.next(), bk.next()
                    S.mm(px.t[:, 0:ST], w2s.t[0:64, j * 128:(j + 1) * 128], thw.t[0:64, :], True, True, [w2s, thw], [px])
                    S.mm(px2.t[:, ST:2 * ST], a2s.t[64:128, j * 128:(j + 1) * 128], adb.t[64:128, :], True, True, [a2s, adb], [px2])
                    S.act(sgT.t[:], px.t[:, 0:ST], AF.Sigmoid, [px, cp], [sgT], bias=col(CP_W0, j))
                    S.act(aT.t[:], px2.t[:, ST:2 * ST], AF.Sigmoid, [px2, cp], [aT], bias=col(CP_A0, j))
                    yield
                    S.ts("dve", lwT.t[:], sgT.t[:], -0.6065306597126334, ALU.mult, [sgT], [lwT])
                    S.ts("dve", kkkT.t[:], kj, col(CP_KK, j), ALU.mult, [PM, cp], [kkkT])
                    yield
                    for ch in range(2):
                        S.op("dve", lambda e: e.tensor_tensor_scan(out=LT.t[:, ch * 128:(ch + 1) * 128], data0=onesf.t[:], data1=lwT.t[:, ch * 128:(ch + 1) * 128], initial=0.0, op0=ALU.mult, op1=ALU.add), [onesf, lwT], [LT])
                    S.act(sqk.t[:], kkkT.t[:], AF.Square, [kkkT], [sqk])
                    yield
                    pss = bk.next()
                    S.mm(pss.t[:, 0:ST], bo2, sqk.t[:], True, True, [self.cb, sqk], [pss])
                    S.act(nrT.t[:], pss.t[:, 0:ST], AF.Sqrt, [pss], [nrT])
                    S.tt("dve", LpT.t[:], LT.t[:], lwT.t[:], ALU.subtract, [LT, lwT], [LpT])
                    S.act(GT.t[:], LT.t[:], AF.Exp, [LT], [GT])
                    yield
                    S.act(GiT.t[:], LT.t[:], AF.Exp, [LT], [GiT], scale=-1.0)
                    S.act(GpT.t[:], LpT.t[:], AF.Exp, [LpT], [GpT])
                    S.ts("dve", tkT.t[:], aT.t[:], col(CP_KA, j), ALU.mult, [aT, cp, omk], [tkT], s2=omk.t[:, j:j + 1], op1=ALU.add)
                    yield
                    S.copy("dve", GC.t[:, j, :].rearrange("p (c e) -> p c e", e=1), c2(GT.t[:])[:, :, 127:128], [GT], [GC])
                    S.tt("dve", kmT.t[:], kj, tkT.t[:], ALU.mult, [PM, tkT], [kmT])
                    yield
                    S.ts("dve", nrT.t[:], nrT.t[:], 1e-12, ALU.max, [nrT], [nrT])
                    S.tt("dve", AR.t[:, j, :, 1, :], c2(rj), c2(GT.t[:]), ALU.mult, [PM, GT], [AR])
                    S.op("dve", lambda e: e.reciprocal(out=nrT.t[:], in_=nrT.t[:]), [nrT], [nrT])
                    yield
                    S.tt("dve", KT.t[:, j, :], kmT.t[:], GiT.t[:], ALU.mult, [kmT, GiT], [KT])
                    S.tt("dve", kkT.t[:], kkkT.t[:], nrT.t[:], ALU.mult, [kkkT, nrT], [kkT])
                    S.copy("act", VT.t[:, j, :], vj, [PM], [VT])
                    yield
                    S.tt("dve", RK.t[:, j, :], rj, kmT.t[:], ALU.mult, [PM, kmT], [RK])
                    S.tt("dve", beT.t[:], kkT.t[:], aT.t[:], ALU.mult, [kkT, aT], [beT])
                    yield
                    S.stt("dve", AR.t[:, j, :, 0, :], c2(kkT.t[:]), -1.0, c2(GpT.t[:]), ALU.mult, ALU.mult, [kkT, GpT], [AR])
                    S.tt("dve", BT.t[:, j, :], beT.t[:], GiT.t[:], ALU.mult, [beT, GiT], [BT])
                    yield

            def neu(ci):
                bk = bankN
                st, ch, sp, cq = ci // 2, ci % 2, (ci // 2) % 2, ci % 2
                sgd, AR, BT, KT, VT, RK = sgd2[sp], AR2[sp], BT2[sp], KT2[sp], VT2[sp], RK2[sp]
                BKtok, Vtok, AXt, AYt, y2, smb, TTs = BKtok2[cq], Vtok2[cq], AXt2[cq], AYt2[cq], y22[cq], smb2[cq], TT2[cq]
                tsl = slice(ch * 128, (ch + 1) * 128)
                tb1, tb2 = self.tbanks.next(), self.tbanks.next()
                for j in range(4):
                    S.tr(tb1.t[:, j * 128:(j + 1) * 128], BT.t[:, j, tsl], self.cb, [BT], [tb1])
                    S.tr(tb1.t[:, 512 + j * 128:512 + (j + 1) * 128], KT.t[:, j, tsl], self.cb, [KT], [tb1])
                    S.tr(tb2.t[:, j * 128:(j + 1) * 128], VT.t[:, j, tsl], self.cb, [VT], [tb2])
                S.copy("act", BKtok.t[:].rearrange("p a (k t) -> p (a k) t", k=4), tb1.t[:].rearrange("p (k t) -> p k t", k=8), [tb1], [BKtok])
                S.copy("act", Vtok.t[:].rearrange("p (k t) -> p k t", k=4), tb2.t[:, 0:512].rearrange("p (k t) -> p k t", k=4), [tb2], [Vtok])
                yield
                pgb = bk.next()
                S.mm(pgb.t[:], sgd.t[:, 0, tsl], g2a.t[:], True, False, [sgd, g2a], [pgb])
                S.mm(pgb.t[:], sgd.t[0:32, 1, tsl], g2b.t[0:32, :], False, True, [sgd, g2b], [pgb])
                S.copy("act", y2.t[:], pgb.t[:], [pgb], [y2])
                yield
                pbn = bk.next()
                for j in range(4):
                    S.mm(pbn.t[:, 0:8], RK.t[:, j, tsl], rkb.t[:, j * 8:(j + 1) * 8], j == 0, j == 3, [RK, rkb], [pbn])
                S.copy("act", smb.t[:, 0:8], pbn.t[:, 0:8], [pbn], [smb])
                yield
                for h in range(8):
                    j, pb = h // 2, 64 * (h % 2)
                    bx, by = bk.next(), bk.next()
                    arf = AR.t[pb:pb + 64, j, ch, :, :].rearrange("p a t -> p (a t)")
                    S.mm(bx.t[:, 0:256], BT.t[pb:pb + 64, j, tsl], arf, True, True, [BT, AR], [bx])
                    S.mm(bx.t[:, 256:384], AR.t[pb:pb + 64, j, ch, 0, :], BT.t[pb:pb + 64, j, tsl], True, True, [BT, AR], [bx])
                    S.mm(by.t[:, 0:256], KT.t[pb:pb + 64, j, tsl], arf, True, True, [KT, AR], [by])
                    S.tt("dve", AXt.t[:, h, :], bx.t[:, 0:384], m3, ALU.mult, [bx, self.cb], [AXt])
                    S.tt("dve", AYt.t[:, h, :], by.t[:, 0:256], m3[:, 0:256], ALU.mult, [by, self.cb], [AYt])
                    yield
                idb = self.cb.t[:, CB_ID:CB_ID + 128].rearrange("p (a t) -> p a t", a=1).broadcast_to([128, 8, 128])
                S.tt("dve", TTs[0].t[:], AXt.t[:, :, 0:128], idb, ALU.add, [AXt, self.cb], [TTs[0]])
                yield
                Btp = lambda h: AXt.t[:, h, 0:128]
                Ap = lambda h: AXt.t[:, h, 256:384]
                Btb, Ab = AXt, AXt
                cur = 0
                for n in range(1, 7):
                    Bn, An, Tn, Tp = Bts[n % 2], Ans[n % 2], TTs[(cur + 1) % 2], TTs[cur]
                    for grp in range(2):
                        gs = slice(grp * 4, grp * 4 + 4)
                        pA = bk.next()
                        pB = bk.next() if n < 6 else None
                        for hq in range(4):
                            h = grp * 4 + hq
                            if n < 6:
                                S.mm(pB.t[:, hq * 128:(hq + 1) * 128], Ap(h), Btp(h), True, True, [Ab, Btb], [pB])
                            S.mm(pA.t[:, hq * 128:(hq + 1) * 128], Btp(h), Ap(h), True, True, [Ab, Btb], [pA])
                        S.copy("act", An.t[:, gs, :].rearrange("p a t -> p (a t)"), pA.t[:], [pA], [An])
                        if n < 6:
                            S.copy("dve", Bn.t[:, gs, :].rearrange("p a t -> p (a t)"), pB.t[:], [pB], [Bn])
                        yield
                        pT = bk.next()
                        for hq in range(4):
                            h = grp * 4 + hq
                            S.mm(pT.t[:, hq * 128:(hq + 1) * 128], An.t[:, h, :], Tp.t[:, h, :], True, True, [An, Tp], [pT])
                        S.tt("dve", Tn.t[:, gs, :].rearrange("p a t -> p (a t)"), pT.t[:], Tp.t[:, gs, :].rearrange("p a t -> p (a t)"), ALU.add, [pT, Tp], [Tn])
                        yield
                    Btp = (lambda B: (lambda h: B.t[:, h, :]))(Bn)
                    Ap = (lambda A: (lambda h: A.t[:, h, :]))(An)
                    Btb, Ab = Bn, An
                    cur = (cur + 1) % 2
                TTfin[ci] = TTs[cur]

            def rec(ci):
                bk = bankR
                st, ch, sp, cq = ci // 2, ci % 2, (ci // 2) % 2, ci % 2
                AR, GC = AR2[sp], GC2[sp]
                BKtok, Vtok, AXt, AYt, y2, smb = BKtok2[cq], Vtok2[cq], AXt2[cq], AYt2[cq], y22[cq], smb2[cq]
                TTf = TTfin[ci]
                for j in range(4):
                    S.ts("dve", Stmp.t[:, j, :], St.t[:, j, :], GC.t[:, j, ch:ch + 1], ALU.mult, [St, GC], [Stmp])
                yield
                pX = bk.next()
                for h in range(8):
                    j, pb = h // 2, 64 * (h % 2)
                    S.mm(pX.t[:, h * 64:(h + 1) * 64], AR.t[pb:pb + 64, j, ch, 0, :], Sb.t[pb:pb + 64, j, (h % 2) * 64:(h % 2 + 1) * 64], True, False, [AR, Sb], [pX])
                    S.mm(pX.t[:, h * 64:(h + 1) * 64], AYt.t[:, h, 0:128], Vtok.t[:, h * 64:(h + 1) * 64], False, True, [AYt, Vtok], [pX])
                S.copy("act", Xb.t[:], pX.t[:], [pX], [Xb])
                yield
                pU = bk.next()
                for h in range(8):
                    S.mm(pU.t[:, h * 64:(h + 1) * 64], TTf.t[:, h, :], Xb.t[:, h * 64:(h + 1) * 64], True, True, [TTf, Xb], [pU])
                S.copy("dve", Ub.t[:], pU.t[:], [pU], [Ub])
                yield
                pS = bk.next()
                for j in range(4):
                    js = slice(j * 128, (j + 1) * 128)
                    S.mm(pS.t[:, js], BKtok.t[:, 0, js], Ub.t[:, js], True, False, [BKtok, Ub], [pS])
                    S.mm(pS.t[:, js], BKtok.t[:, 1, js], Vtok.t[:, js], False, True, [BKtok, Vtok], [pS])
                pY = bk.next()
                for h in range(8):
                    j, pb = h // 2, 64 * (h % 2)
                    hs_ = slice(h * 64, (h + 1) * 64)
                    S.mm(pY.t[:, hs_], AR.t[pb:pb + 64, j, ch, 1, :], Sb.t[pb:pb + 64, j, (h % 2) * 64:(h % 2 + 1) * 64], True, False, [AR, Sb], [pY])
                    S.mm(pY.t[:, hs_], AXt.t[:, h, 128:256], Ub.t[:, hs_], False, False, [AXt, Ub], [pY])
                    S.mm(pY.t[:, hs_], AYt.t[:, h, 128:256], Vtok.t[:, hs_], False, True, [AYt, Vtok], [pY])
                for j in range(4):
                    S.stt("dve", St.t[:, j, :], pS.t[:, j * 128:(j + 1) * 128], GC.t[:, j, ch:ch + 1], Stmp.t[:, j, :], ALU.mult, ALU.add, [pS, GC, Stmp], [St])
                S.copy("act", Sb.t[:], St.t[:], [St], [Sb])
                sm, sm2 = smr
                S.copy("dve", y1.t[:], pY.t[:], [pY], [y1])
                yield
                S.act(ysq.t[:], y1.t[:], AF.Square, [y1], [ysq])
                S.op("dve", lambda e: e.tensor_reduce(out=sm.t[:, 0:8], in_=h8(y1.t[:]), axis=AX.X, op=ALU.add), [y1], [sm])
                yield
                S.op("dve", lambda e: e.tensor_reduce(out=sm2.t[:, 0:8], in_=h8(ysq.t[:]), axis=AX.X, op=ALU.add), [ysq], [sm2])
                yield
                S.ts("dve", sm.t[:, 0:8], sm.t[:, 0:8], 1.0 / 64, ALU.mult, [sm], [sm])
                yield
                S.tt("dve", sm.t[:, 8:16], sm.t[:, 0:8], sm.t[:, 0:8], ALU.mult, [sm], [sm])
                yield
                S.stt("dve", sm2.t[:, 8:16], sm2.t[:, 0:8], 1.0 / 64, sm.t[:, 8:16], ALU.mult, ALU.subtract, [sm, sm2], [sm2])
                yield
                S.act(sm2.t[:, 0:8], sm2.t[:, 8:16], AF.Sqrt, [sm2], [sm2], bias=GN_EPS)
                yield
                S.op("dve", lambda e: e.reciprocal(out=sm2.t[:, 8:16], in_=sm2.t[:, 0:8]), [sm2], [sm2])
                yield
                bc = lambda ap: ap.rearrange("p (h e) -> p h e", e=1).broadcast_to([128, 8, 64])
                S.tt("dve", h8(y1.t[:]), h8(y1.t[:]), bc(sm.t[:, 0:8]), ALU.subtract, [y1, sm], [y1])
                S.tt("dve", h8(ysq.t[:]), h8(Vtok.t[:]), bc(smb.t[:, 0:8]), ALU.mult, [Vtok, smb], [ysq])
                yield
                S.tt("dve", h8(y1.t[:]), h8(y1.t[:]), bc(sm2.t[:, 8:16]), ALU.mult, [y1, sm2], [y1])
                yield
                S.tt("dve", y1.t[:], y1.t[:], lnw.t[:], ALU.mult, [y1, lnw], [y1])
                yield
                S.tt("dve", y1.t[:], y1.t[:], lnb.t[:], ALU.add, [y1, lnb], [y1])
                yield
                S.tt("dve", y1.t[:], y1.t[:], ysq.t[:], ALU.add, [y1, ysq], [y1])
                yield
                yb_ = yo.next()
                S.tt("dve", yb_.t[:], y1.t[:], y2.t[:], ALU.mult, [y1, y2], [yb_])
                S.dma("sync", self.ya_d[ci * 128:(ci + 1) * 128, :], yb_.t[:], reads=[yb_], writes=[self.ya_b[ci]])
                yield

            NCH = NT
            interleave([prep(0)])
            interleave([neu(0)])
            for ci in range(NCH):
                st = ci // 2
                streams = [rec(ci)]
                if ci + 1 < NCH:
                    streams.append(neu(ci + 1))
                if ci % 2 == 0 and st + 1 < T // ST:
                    streams.append(prep(st + 1))
                interleave(streams)

    def retention(self, l, uT, cf):
        S = self.S
        with contextlib.ExitStack() as es:
            W = S.sbt(es, "Wret", [128, 8, 1536], BF16)
            self.load_w(W, self.W["w_in"][l][:, C_RET:C_RET + 1536], 8)
            gret = S.sbt(es, "gret", [128, 512], F32)
            self.load_row(gret, l, RP_RET, 512)
            Sr = S.sbt(es, "Sr", [128, 2, 256], F32)
            Srb = S.sbt(es, "Srb", [128, 2, 256], BF16)
            S.op("dve", lambda e: e.memset(Sr.t[:], 0.0), [], [Sr])
            S.op("dve", lambda e: e.memset(Srb.t[:], 0.0), [], [Srb])
            vbs = Rot([S.sbt(es, "vb%d" % i, [128, 512], BF16) for i in range(2)])
            sgs = Rot([S.sbt(es, "sg%d" % i, [128, 512], F32) for i in range(2)])
            qks = Rot([S.sbt(es, "qk%d" % i, [128, 2, 4, 64], F32) for i in range(2)])
            qkrs = Rot([S.sbt(es, "qkr%d" % i, [128, 2, 4, 64], F32) for i in range(2)])
            tmps = Rot([S.sbt(es, "rt%d" % i, [128, 4, 4, 32], F32) for i in range(2)])
            qts = Rot([S.sbt(es, "qt%d" % i, [128, 3, 256], BF16) for i in range(2)])
            qkTs = Rot([S.sbt(es, "qkT%d" % i, [128, 4, 128], BF16) for i in range(2)])
            sTms = Rot([S.sbt(es, "sTm%d" % i, [128, 4, 128], BF16) for i in range(2)])
            ysqs = Rot([S.sbt(es, "ysq%d" % i, [128, 512], F32) for i in range(2)])
            yns = Rot([S.sbt(es, "yn%d" % i, [128, 512], F32) for i in range(2)])
            ycs = Rot([S.sbt(es, "yc%d" % i, [128, 512], BF16) for i in range(2)])
            Dp = cf.t[:, CF_DP:CF_DP + 512]
            import os
            rc = int(os.environ.get('RET_CUT', '99'))
            for c in range(int(os.environ.get('RET_NCH', NT))):
                tsl = slice(c * 128, (c + 1) * 128)
                if os.environ.get("RET_VAR", "") == "only_tr":
                    tb = self.tbanks.next()
                    for a in range(4):
                        S.tr(tb.t[:, a * 128:(a + 1) * 128], self.junk.t[:, a * 128:(a + 1) * 128], self.cb, [self.junk], [tb])
                    continue
                if os.environ.get("RET_VAR", "") == "mm_tr":
                    pqk = self.banks.next()
                    for kc in range(8):
                        S.mm(pqk.t[:], uT.t[:, kc, tsl], W.t[:, kc, 0:512], kc == 0, kc == 7, [uT, W], [pqk])
                    tb = self.tbanks.next()
                    for a in range(4):
                        S.tr(tb.t[:, a * 128:(a + 1) * 128], self.junk.t[:, a * 128:(a + 1) * 128], self.cb, [self.junk], [tb])
                    continue
                pqk, pv, pg = self.banks.next(), self.banks.next(), self.banks.next()
                for pp, c0 in ((pqk, 0), (pv, 512), (pg, 1024)):
                    for kc in range(8):
                        S.mm(pp.t[:], uT.t[:, kc, tsl], W.t[:, kc, c0:c0 + 512], kc == 0, kc == 7, [uT, W], [pp])
                vb = vbs.next()
                S.copy("act", vb.t[:], pv.t[:], [pv], [vb])
                sg = sgs.next()
                S.act(sg.t[:], pg.t[:], AF.Silu, [pg], [sg])
                qk = qks.next()
                S.copy("dve", qk.t[:].rearrange("p a h e -> p (a h e)"), pqk.t[:], [pqk], [qk])
                if rc < 1:
                    continue
                qkr = qkrs.next()
                tm = tmps.next()
                cosb = cf.t[:, CF_RCOS + c * 32:CF_RCOS + (c + 1) * 32].rearrange("p (a e) -> p a e", a=1).broadcast_to([128, 4, 32])
                sinb = cf.t[:, CF_RSIN + c * 32:CF_RSIN + (c + 1) * 32].rearrange("p (a e) -> p a e", a=1).broadcast_to([128, 4, 32])
                for w in range(2):
                    x1 = qk.t[:, w, :, 0:32]
                    x2 = qk.t[:, w, :, 32:64]
                    eng = "dve"
                    S.tt(eng, tm.t[:, 0], x1, cosb, ALU.mult, [qk, cf], [tm])
                    S.tt(eng, tm.t[:, 1], x2, sinb, ALU.mult, [qk, cf], [tm])
                    S.tt(eng, tm.t[:, 2], x2, cosb, ALU.mult, [qk, cf], [tm])
                    S.tt(eng, tm.t[:, 3], x1, sinb, ALU.mult, [qk, cf], [tm])
                    S.tt(eng, qkr.t[:, w, :, 0:32], tm.t[:, 0], tm.t[:, 1], ALU.subtract, [tm], [qkr])
                    S.tt(eng, qkr.t[:, w, :, 32:64], tm.t[:, 2], tm.t[:, 3], ALU.add, [tm], [qkr])
                if rc < 2:
                    continue
                qt = qts.next()
                qdb = cf.t[:, CF_QD:CF_QD + 4].rearrange("p (h e) -> p h e", e=1).broadcast_to([128, 4, 64])
                kdb = cf.t[:, CF_KD:CF_KD + 4].rearrange("p (h e) -> p h e", e=1).broadcast_to([128, 4, 64])
                v3 = lambda ap: ap.rearrange("p (h e) -> p h e", h=4)
                S.tt("dve", v3(qt.t[:, 0, :]), qkr.t[:, 0], qdb, ALU.mult, [qkr, cf], [qt])
                S.copy("act", v3(qt.t[:, 1, :]), qkr.t[:, 1], [qkr], [qt])
                S.tt("dve", v3(qt.t[:, 2, :]), qkr.t[:, 1], kdb, ALU.mult, [qkr, cf], [qt])
                if rc < 3:
                    continue
                tb = self.tbanks.next()
                rv = os.environ.get("RET_VAR", "")
                for a in range(2):
                    for jj in range(2):
                        if rv == "notr" or (rv == "tr1" and (a or jj)):
                            continue
                        src_ = qt.t[:, a, jj * 128:(jj + 1) * 128]
                        if rv == "xnsrc":
                            src_ = self.junk.t[:, (a * 2 + jj) * 128:(a * 2 + jj + 1) * 128]
                        S.tr(tb.t[:, (a * 2 + jj) * 128:(a * 2 + jj + 1) * 128], src_, self.cb, [qt], [tb])
                qkT = qkTs.next()
                if rv != "nocopy":
                    S.copy("act", qkT.t[:, 0:4, :], tb.t[:, 0:512].rearrange("p (k t) -> p k t", k=4), [tb], [qkT])
                if rc < 4:
                    continue
                psr = (self.banks.next(), self.banks.next())
                for h in range(4):
                    jj, pb = h // 2, 64 * (h % 2)
                    S.mm(psr[h % 2].t[:, h * 128:(h + 1) * 128], qkT.t[pb:pb + 64, 2 + jj, :], qkT.t[pb:pb + 64, jj, :], True, True, [qkT], [psr[h % 2]])
                sTm = sTms.next()
                for h in range(4):
                    S.tt("dve", sTm.t[:, h, :], psr[h % 2].t[:, h * 128:(h + 1) * 128], Dp[:, h * 128:(h + 1) * 128], ALU.mult, [psr[h % 2], cf], [sTm])
                if rc < 5:
                    continue
                po = self.banks.next()
                for h in range(4):
                    jj, pb = h // 2, 64 * (h % 2)
                    S.mm(po.t[:, h * 128:(h + 1) * 128], sTm.t[:, h, :], vb.t[:, h * 128:(h + 1) * 128], True, False, [sTm, vb], [po])
                    S.mm(po.t[:, h * 128:(h + 1) * 128], qkT.t[pb:pb + 64, jj, :], Srb.t[pb:pb + 64, jj, (h % 2) * 128:(h % 2 + 1) * 128], False, True, [qkT, Srb], [po])
                if rc < 6:
                    continue
                pkv = self.banks.next()
                for jj in range(2):
                    S.mm(pkv.t[:, jj * 256:(jj + 1) * 256], qt.t[:, 2, jj * 128:(jj + 1) * 128], vb.t[:, jj * 256:(jj + 1) * 256], True, True, [qt, vb], [pkv])
                for jj in range(2):
                    S.stt("dve", Sr.t[:, jj, :], Sr.t[:, jj, :], cf.t[:, CF_GC + jj:CF_GC + jj + 1], pkv.t[:, jj * 256:(jj + 1) * 256], ALU.mult, ALU.add, [Sr, cf, pkv], [Sr])
                S.copy("act", Srb.t[:], Sr.t[:], [Sr], [Srb])
                if rc < 7:
                    continue
                ysq = ysqs.next()
                S.act(ysq.t[:], po.t[:], AF.Square, [po], [ysq])
                sm = self.smalls.next()
                S.op("dve", lambda e: e.tensor_reduce(out=sm.t[:, 0:4], in_=ysq.t[:].rearrange("p (h e) -> p h e", h=4), axis=AX.X, op=ALU.add), [ysq], [sm])
                S.act(sm.t[:, 4:8], sm.t[:, 0:4], AF.Sqrt, [sm], [sm], scale=1.0 / 128, bias=EPS)
                S.op("dve", lambda e: e.reciprocal(out=sm.t[:, 8:12], in_=sm.t[:, 4:8]), [sm], [sm])
                yn = yns.next()
                S.tt("dve", v3(yn.t[:]), v3(po.t[:]), sm.t[:, 8:12].rearrange("p (h e) -> p h e", e=1).broadcast_to([128, 4, 128]), ALU.mult, [po, sm], [yn])
                S.tt("dve", yn.t[:], yn.t[:], gret.t[:], ALU.mult, [yn, gret], [yn])
                yc = ycs.next()
                S.tt("dve", yc.t[:], yn.t[:], sg.t[:], ALU.mult, [yn, sg], [yc])
                S.dma("sync", self.yc_d[tsl, :], yc.t[:], reads=[yc], writes=[self.yc_b[c]])

    def merge(self, l, uT):
        S = self.S
        with contextlib.ExitStack() as es:
            Wg = S.sbt(es, "Wg", [128, 8, 3072], BF16)
            Wb = S.sbt(es, "Wb", [128, 12, D], BF16)
            Wo = S.sbt(es, "Wo", [128, 8, D], BF16)
            self.load_w(Wg, self.W["w_in"][l][:, C_GATE:C_GATE + 3072], 8)
            for b, nm in enumerate(("w_branch_rwkv", "w_branch_dil", "w_branch_ret")):
                for kc in range(4):
                    S.dma("pool", Wb.t[:, b * 4 + kc, :], self.W[nm][l][kc * 128:(kc + 1) * 128, :], writes=[Wb])
            self.load_w(Wo, self.W["w_out"][l], 8)
            yas = Rot([S.sbt(es, "ya%d" % i, [128, 512], BF16) for i in range(2)])
            ycs = Rot([S.sbt(es, "yc%d" % i, [128, 512], BF16) for i in range(2)])
            ybs = Rot([S.sbt(es, "yb%d" % i, [128, 512], BF16) for i in range(2)])
            oas = Rot([S.sbt(es, "oa%d" % i, [128, 3, 520], F32) for i in range(2)])
            hts = Rot([S.sbt(es, "ht%d" % i, [128, D], F32) for i in range(2)])
            yTs = Rot([S.sbt(es, "yT%d" % i, [128, 12, 128], BF16) for i in range(2)])
            mTs = Rot([S.sbt(es, "mT%d" % i, [128, 8, 128], BF16) for i in range(2)])
            sigs = Rot([S.sbt(es, "sig%d" % i, [128, 384], F32) for i in range(2)])
            prods = Rot([S.sbt(es, "prod%d" % i, [128, 384], F32) for i in range(2)])
            for ti in range(NT):
                tsl = slice(ti * 128, (ti + 1) * 128)
                ya, yc, yb, oa, ht = yas.next(), ycs.next(), ybs.next(), oas.next(), hts.next()
                S.dma("sync", ya.t[:], self.ya_d[tsl, :], reads=[self.ya_b[ti]], writes=[ya])
                S.dma("sync", yc.t[:], self.yc_d[tsl, :], reads=[self.yc_b[ti]], writes=[yc])
                for g in range(3):
                    S.dma("sync", oa.t[:, g, :], self.oa_d[g, tsl, :], reads=[self.oa_b[g]], writes=[oa])
                S.dma("sync", ht.t[:], self.out[tsl, :], reads=[self.hbuf[ti]], writes=[ht])
                S.tt("dve", oa.t[:, 0, :], oa.t[:, 0, :], oa.t[:, 1, :], ALU.add, [oa], [oa])
                S.tt("dve", oa.t[:, 0, :], oa.t[:, 0, :], oa.t[:, 2, :], ALU.add, [oa], [oa])
                o3 = oa.t[:, 0, :].rearrange("p (h e) -> p h e", h=8)
                sm = self.smalls.next()
                S.op("dve", lambda e: e.reciprocal(out=sm.t[:, 0:8].rearrange("p (h e) -> p h e", e=1), in_=o3[:, :, 64:65]), [oa], [sm])
                S.tt("dve", yb.t[:].rearrange("p (h e) -> p h e", h=8), o3[:, :, 0:64], sm.t[:, 0:8].rearrange("p (h e) -> p h e", e=1).broadcast_to([128, 8, 64]), ALU.mult, [oa, sm], [yb])
                yT = yTs.next()
                tb0, tb1 = self.tbanks.next(), self.tbanks.next()
                for kc in range(4):
                    S.tr(tb0.t[:, kc * 128:(kc + 1) * 128], ya.t[:, kc * 128:(kc + 1) * 128], self.cb, [ya], [tb0])
                for kc in range(4):
                    S.tr(tb0.t[:, 512 + kc * 128:512 + (kc + 1) * 128], yb.t[:, kc * 128:(kc + 1) * 128], self.cb, [yb], [tb0])
                for kc in range(4):
                    S.tr(tb1.t[:, kc * 128:(kc + 1) * 128], yc.t[:, kc * 128:(kc + 1) * 128], self.cb, [yc], [tb1])
                S.copy("act", yT.t[:, 0:8, :], tb0.t[:].rearrange("p (k t) -> p k t", k=8), [tb0], [yT])
                S.copy("act", yT.t[:, 8:12, :], tb1.t[:, 0:512].rearrange("p (k t) -> p k t", k=4), [tb1], [yT])
                mT = mTs.next()
                for oc in range(8):
                    pg_, pz = self.banks.next(), self.banks.next()
                    for b in range(3):
                        for kc in range(8):
                            S.mm(pg_.t[:, b * 128:(b + 1) * 128], Wg.t[:, kc, b * 1024 + oc * 128:b * 1024 + (oc + 1) * 128], uT.t[:, kc, tsl], kc == 0, kc == 7, [Wg, uT], [pg_])
                    for b in range(3):
                        for kc in range(4):
                            S.mm(pz.t[:, b * 128:(b + 1) * 128], Wb.t[:, b * 4 + kc, oc * 128:(oc + 1) * 128], yT.t[:, b * 4 + kc, :], kc == 0, kc == 3, [Wb, yT], [pz])
                    sig, prod = sigs.next(), prods.next()
                    S.act(sig.t[:], pg_.t[:, 0:384], AF.Sigmoid, [pg_], [sig])
                    S.tt("dve", prod.t[:], sig.t[:], pz.t[:, 0:384], ALU.mult, [sig, pz], [prod])
                    S.tt("dve", prod.t[:, 0:128], prod.t[:, 0:128], prod.t[:, 128:256], ALU.add, [prod], [prod])
                    S.tt("dve", mT.t[:, oc, :], prod.t[:, 0:128], prod.t[:, 256:384], ALU.add, [prod], [mT])
                for half in range(2):
                    po = self.banks.next()
                    for kc in range(8):
                        S.mm(po.t[:], mT.t[:, kc, :], Wo.t[:, kc, half * 512:(half + 1) * 512], kc == 0, kc == 7, [mT, Wo], [po])
                    hsl = ht.t[:, half * 512:(half + 1) * 512]
                    S.tt("dve", hsl, po.t[:], hsl, ALU.add, [po, ht], [ht])
                S.dma("sync", self.out[tsl, :], ht.t[:], reads=[ht], writes=[self.hbuf[ti]])


_CACHE = {}


def kernel(**inputs):
    stages = inputs.pop("_stages", None)
    debug = inputs.pop("_debug", False)
    ncores = inputs.pop("_ncores", 8)
    key = (None if stages is None else tuple(stages), debug)
    if key not in _CACHE:
        _CACHE[key] = Prog(stages, debug)
    prog = _CACHE[key]
    inp = {k: np.ascontiguousarray(np.asarray(v, dtype=np.float32)) for k, v in inputs.items()}
    colp, rowp = host_layout_params(inp)
    cb, cf = host_constants()
    in_maps = []
    for b in range(ncores):
        m = {"x": inp["x"][b], "mem": inp["mem"][b], "colp": colp, "rowp": rowp, "cb": cb, "cf": cf}
        for n, _ in WEIGHT_SPECS:
            m[n] = inp[n]
        in_maps.append(m)
    if debug == "time":
        res = run_bass_kernel_spmd(prog.nc, in_maps, core_ids=list(range(ncores)), trace=True)
        print("EXEC_TIME_NS", res.exec_time_ns)
    else:
        res = run_bass_kernel_spmd(prog.nc, in_maps, core_ids=list(range(ncores)))
    out = np.stack([np.asarray(r["out"], dtype=np.float32) for r in res.results], axis=0)
    if debug:
        return out, res.results
    return out
```

```python
import contextlib
import numpy as np
import concourse.bass as bass
import concourse.mybir as mybir
from concourse.bass_utils import run_bass_kernel_spmd

F32 = mybir.dt.float32
BF16 = mybir.dt.bfloat16
AF = mybir.ActivationFunctionType
ALU = mybir.AluOpType
AX = mybir.AxisListType


class Buf:
    def __init__(self, name, t=None, unordered=False):
        self.name = name
        self.t = t
        self.writers = {}
        self.readers = {}
        self.unordered = unordered


class _Eng:
    def __init__(self, name, eng, sem, step):
        self.name, self.eng, self.sem, self.step = name, eng, sem, step
        self.count = 0
        self.seen = {}


class Sched:
    N_DMA_SEMS = 12
    same_engine_sync = True

    def __init__(self, nc):
        self.nc = nc
        self.engs = {}
        self.queues = {}
        self._stack = contextlib.ExitStack()
        for name, eng in (("pe", nc.tensor), ("act", nc.scalar), ("dve", nc.vector), ("pool", nc.gpsimd),
                          ("sync", nc.sync)):
            sem = nc.alloc_semaphore("s_" + name)
            self.engs[name] = _Eng(name, eng, sem, 1)
        self.dma_pool = {}
        self.dma_rr = {}
        for q in ("sync", "pool"):
            lst = []
            for i in range(self.N_DMA_SEMS):
                nm = "d_%s%d" % (q, i)
                sem = nc.alloc_semaphore(nm)
                self.engs[nm] = _Eng(nm, None, sem, 16)
                lst.append(nm)
            self.dma_pool[q] = lst
            self.dma_rr[q] = 0
        self.n_ops = 0

    def sb(self, name, shape, dtype, unordered=False):
        t = self.nc.alloc_sbuf_tensor(name, list(shape), dtype)
        return Buf(name, t, unordered)

    def ps(self, name, shape, dtype):
        t = self.nc.alloc_psum_tensor(name, list(shape), dtype)
        return Buf(name, t)

    def block(self):
        return contextlib.nullcontext()

    def _waits(self, E, reads, writes):
        waits = {}
        for b in reads:
            for en, c in b.writers.items():
                if waits.get(en, 0) < c:
                    waits[en] = c
        for b in writes:
            if not b.unordered:
                for en, c in b.writers.items():
                    if waits.get(en, 0) < c:
                        waits[en] = c
            for en, c in b.readers.items():
                if waits.get(en, 0) < c:
                    waits[en] = c
        for en, c in waits.items():
            if E.seen.get(en, 0) >= c:
                continue
            X = self.engs[en]
            E.eng.wait_ge(X.sem, c * X.step)
            E.seen[en] = c

    def op(self, ename, fn, reads=(), writes=()):
        E = self.engs[ename]
        self._waits(E, reads, writes)
        ins = fn(E.eng)
        E.count += 1
        ins.then_inc(E.sem, 1)
        if not (self.same_engine_sync and ename != 'pe'):
            E.seen[ename] = E.count
        for b in reads:
            b.readers[ename] = E.count
        for b in writes:
            if b.unordered:
                b.writers[ename] = E.count
            else:
                b.writers = {ename: E.count}
                b.readers = {}
        self.n_ops += 1
        return ins

    def dma(self, q, out, in_, reads=(), writes=()):
        Q = self.engs[q]
        pool = self.dma_pool[q]
        vn = pool[self.dma_rr[q] % len(pool)]
        self.dma_rr[q] += 1
        V = self.engs[vn]
        if V.count > 0 and Q.seen.get(vn, 0) < V.count:
            Q.eng.wait_ge(V.sem, V.count * 16)
            Q.seen[vn] = V.count
        self._waits(Q, reads, writes)
        ins = Q.eng.dma_start(out=out, in_=in_)
        V.count += 1
        ins.then_inc(V.sem, 16)
        for b in reads:
            b.readers[vn] = V.count
        for b in writes:
            if b.unordered:
                b.writers[vn] = V.count
            else:
                b.writers = {vn: V.count}
                b.readers = {}
        self.n_ops += 1
        return ins

    def finish(self):
        Q = self.engs["sync"]
        for en, X in self.engs.items():
            if en == "sync" or X.count == 0:
                continue
            if Q.seen.get(en, 0) < X.count:
                Q.eng.wait_ge(X.sem, X.count * X.step)
                Q.seen[en] = X.count


    def barrier(self):
        names = ["pe", "act", "dve", "pool", "sync"]
        for en in names:
            E = self.engs[en]
            for xn, X in self.engs.items():
                if xn == en or X.count == 0:
                    continue
                if E.seen.get(xn, 0) < X.count:
                    E.eng.wait_ge(X.sem, X.count * X.step)
                    E.seen[xn] = X.count

    def sbt(self, es, name, shape, dtype, unordered=False):
        self._uid = getattr(self, "_uid", 0) + 1
        name = "sb%d_%s" % (self._uid, name)
        t = es.enter_context(self.nc.sbuf_tensor(name, list(shape), dtype))
        return Buf(name, t, unordered)

    def mm(self, out, lhsT, rhs, start, stop, reads, writes):
        return self.op("pe", lambda e: e.matmul(out, lhsT=lhsT, rhs=rhs, start=start, stop=stop), reads, writes)

    def tr(self, out, in_, ident, reads, writes):
        return self.op("pe", lambda e: e.transpose(out, in_, ident.t[:, 0:128]), list(reads) + [ident], writes)

    def act(self, out, in_, func, reads, writes, **kw):
        return self.op("act", lambda e: e.activation(out=out, in_=in_, func=func, **kw), reads, writes)

    def copy(self, eng, out, in_, reads, writes):
        if eng == "act":
            return self.op("act", lambda e: e.activation(out=out, in_=in_, func=AF.Copy), reads, writes)
        return self.op(eng, lambda e: e.tensor_copy(out=out, in_=in_), reads, writes)

    def tt(self, eng, out, in0, in1, op, reads, writes):
        return self.op(eng, lambda e: e.tensor_tensor(out=out, in0=in0, in1=in1, op=op), reads, writes)

    def ts(self, eng, out, in0, s1, op0, reads, writes, s2=None, op1=None):
        if op1 is None:
            return self.op(eng, lambda e: e.tensor_scalar(out=out, in0=in0, scalar1=s1, scalar2=None, op0=op0), reads, writes)
        return self.op(eng, lambda e: e.tensor_scalar(out=out, in0=in0, scalar1=s1, scalar2=s2, op0=op0, op1=op1), reads, writes)

    def stt(self, eng, out, in0, scalar, in1, op0, op1, reads, writes):
        return self.op(eng, lambda e: e.scalar_tensor_tensor(out=out, in0=in0, scalar=scalar, in1=in1, op0=op0, op1=op1), reads, writes)


def interleave(gens):
    gens = [g for g in gens if g is not None]
    while gens:
        for g in list(gens):
            try:
                next(g)
            except StopIteration:
                gens.remove(g)


class Rot:
    def __init__(self, items):
        self.items = list(items)
        self.i = 0

    def next(self):
        b = self.items[self.i % len(self.items)]
        self.i += 1
        return b


T = 4096
D = 1024
DFF = 2816
MEM = 256
NT = T // 128
EPS = 1e-6
GN_EPS = 64e-5
C_RW, C_DIL, C_RET, C_GATE = 0, 1824, 1824 + 4608, 1824 + 4608 + 1536
DILS = (1, 4, 16)
CP_MU, CP_W0, CP_A0, CP_KK, CP_KA, CP_XQ, CP_XK, CP_RK, CP_N = 0, 15, 19, 23, 27, 31, 33, 35, 67
RP_FFN1, RP_MIX, RP_XA, RP_MEM, RP_FFN2 = 0, 1024, 2048, 3072, 4096
RP_LNW, RP_LNB, RP_DQ, RP_DK, RP_RET, RP_N = 5120, 5632, 6144, 7680, 9216, 9728
CB_ID, CB_ONES, CB_BO2, CB_M3, CB_MPC, CB_N = 0, 128, 256, 384, 768, 1280
CF_DP, CF_QD, CF_KD, CF_GC, CF_DCOS, CF_DSIN, CF_RCOS, CF_RSIN, CF_N = 0, 512, 516, 520, 522, 1290, 2058, 3082, 4106


def host_constants():
    p = np.arange(128)
    cb = np.zeros((128, CB_N), np.float32)
    cb[:, CB_ID:CB_ID + 128] = np.eye(128)
    cb[:, CB_ONES:CB_ONES + 128] = 1.0
    cb[:, CB_BO2:CB_BO2 + 128] = (p[:, None] // 64 == p[None, :] // 64)
    cb[:, CB_M3:CB_M3 + 128] = (p[:, None] < p[None, :])
    cb[:, CB_M3 + 128:CB_M3 + 256] = (p[:, None] <= p[None, :])
    cb[:, CB_M3 + 256:CB_M3 + 384] = (p[None, :] < p[:, None])
    for hh in range(2):
        cb[:, CB_MPC + hh * 256:CB_MPC + hh * 256 + 128] = (p[:, None] >= p[None, :])
        cb[:, CB_MPC + hh * 256 + 128:CB_MPC + hh * 256 + 256] = (p[:, None] <= p[None, :])
    cf = np.zeros((128, CF_N), np.float64)
    gam = 1.0 - 2.0 ** (-5.0 - np.arange(4))
    for h in range(4):
        cf[:, CF_DP + h * 128:CF_DP + (h + 1) * 128] = (gam[h] ** (-(p[:, None] + 1.0))) * (p[:, None] <= p[None, :]) / 8.0
        cf[:, CF_QD + h] = gam[h] ** (p + 1.0)
        cf[:, CF_KD + h] = gam[h] ** (127.0 - p) / 8.0
    for j in range(2):
        cf[:, CF_GC + j] = gam[2 * j + p // 64] ** 128.0
    invf = 500000.0 ** (-np.arange(8, dtype=np.float64) / 8.0)
    for g, d in enumerate(DILS):
        nblk = 32 // d
        for c in range(d):
            for n in range(nblk):
                combo = c * nblk + n
                pos = ((n * 128 + p) * d + c).astype(np.float32)
                ang = (pos[:, None] * invf[None, :].astype(np.float32)).astype(np.float32)
                o = (g * 32 + combo) * 8
                cf[:, CF_DCOS + o:CF_DCOS + o + 8] = np.cos(ang)
                cf[:, CF_DSIN + o:CF_DSIN + o + 8] = np.sin(ang)
    invr = (10000.0 ** (-np.arange(32, dtype=np.float64) / 32.0)).astype(np.float32)
    for c in range(32):
        pos = (c * 128 + p).astype(np.float32)
        ang = (pos[:, None] * invr[None, :]).astype(np.float32)
        cf[:, CF_RCOS + c * 32:CF_RCOS + (c + 1) * 32] = np.cos(ang)
        cf[:, CF_RSIN + c * 32:CF_RSIN + (c + 1) * 32] = np.sin(ang)
    return cb.astype(np.float32), cf.astype(np.float32)


def host_layout_params(inp):
    L = 2
    colp = np.zeros((L, 128, CP_N), np.float32)
    rowp = np.zeros((L, RP_N), np.float32)
    p = np.arange(128)
    for l in range(L):
        mu = np.zeros(15 * 128, np.float32)
        mu[:1824] = inp["rw_mu"][l]
        colp[l, :, CP_MU:CP_MU + 15] = mu.reshape(15, 128).T
        for off, key in ((CP_W0, "rw_w0"), (CP_A0, "rw_a0"), (CP_KK, "rw_k_k"), (CP_KA, "rw_k_a")):
            colp[l, :, off:off + 4] = inp[key][l].reshape(4, 128).T
        colp[l, :, CP_XQ:CP_XQ + 2] = inp["xa_q_norm"][l].reshape(2, 128).T
        colp[l, :, CP_XK:CP_XK + 2] = inp["xa_k_norm"][l].reshape(2, 128).T
        rk = inp["rw_r_k"][l]
        for j in range(4):
            for h in range(8):
                if h // 2 == j:
                    rows = p[(p // 64) == (h % 2)]
                    colp[l, rows, CP_RK + j * 8 + h] = rk[h * 64 + (rows % 64)]
        for off, key in ((RP_FFN1, "norm_ffn1"), (RP_MIX, "norm_mix"), (RP_XA, "norm_xattn"), (RP_MEM, "norm_mem"),
                         (RP_FFN2, "norm_ffn2")):
            rowp[l, off:off + 1024] = inp[key][l]
        rowp[l, RP_LNW:RP_LNW + 512] = inp["rw_ln_w"][l]
        rowp[l, RP_LNB:RP_LNB + 512] = inp["rw_ln_b"][l]
        for g in range(3):
            rowp[l, RP_DQ + g * 512:RP_DQ + (g + 1) * 512] = np.tile(inp["dil_q_norm"][l, g], 8)
            rowp[l, RP_DK + g * 512:RP_DK + (g + 1) * 512] = np.tile(inp["dil_k_norm"][l, g], 8)
        rowp[l, RP_RET:RP_RET + 512] = inp["ret_norm"][l]
    return colp, rowp


WEIGHT_SPECS = [
    ("ffn1_w13", [2, D, 2 * DFF]), ("ffn1_w2", [2, DFF, D]), ("w_in", [2, D, 11040]),
    ("rw_w2", [2, 64, 512]), ("rw_a2", [2, 64, 512]), ("rw_g2", [2, 160, 512]),
    ("w_branch_rwkv", [2, 512, D]), ("w_branch_dil", [2, 512, D]), ("w_branch_ret", [2, 512, D]),
    ("w_out", [2, D, D]), ("xa_wq", [2, D, D]), ("xa_wkv", [2, D, 2 * D]), ("xa_wo", [2, D, D]),
    ("ffn2_w13", [2, D, 2 * DFF]), ("ffn2_w2", [2, DFF, D]),
]


class Prog:
    def __init__(self, stages=None, debug=False):
        self.stages = stages
        nc = self.nc = bass.Bass("TRN2", target_bir_lowering=False)
        self.S = S = Sched(nc)
        dt = lambda n, sh, ty=F32, kind="ExternalInput": nc.dram_tensor(n, sh, ty, kind=kind).ap()
        self.x = dt("x", [T, D])
        self.mem = dt("mem", [MEM, D])
        self.W = {n: dt(n, sh) for n, sh in WEIGHT_SPECS}
        self.colp_d = dt("colp", [2, 128, CP_N])
        self.rowp_d = dt("rowp", [2, RP_N])
        self.cb_d = dt("cb", [128, CB_N])
        self.cf_d = dt("cf", [128, CF_N])
        self.out = dt("out", [T, D], F32, "ExternalOutput")
        okind = "ExternalOutput" if debug else "Internal"
        self.ya_d = dt("ya_s", [T, 512], BF16, okind)
        self.yc_d = dt("yc_s", [T, 512], BF16, okind)
        self.oa_d = dt("oa_s", [3, T, 520], F32, okind)
        self.hbuf = [Buf("h%d" % i) for i in range(NT)]
        self.ya_b = [Buf("ya%d" % i) for i in range(NT)]
        self.yc_b = [Buf("yc%d" % i) for i in range(NT)]
        self.oa_b = [Buf("oa%d" % g, unordered=True) for g in range(3)]
        self.bank_list = [S.ps("pb%d" % i, [128, 512], F32) for i in range(6)]
        self.banks = Rot(self.bank_list)
        self.tbanks = Rot([S.ps("pt%d" % i, [128, 1024], BF16) for i in range(2)])
        with contextlib.ExitStack() as es:
            self.cb = S.sbt(es, "cb", [128, CB_N], BF16)
            S.dma("pool", self.cb.t[:], self.cb_d, writes=[self.cb])
            self.ident = self.cb
            self.colp = S.sbt(es, "colp", [128, CP_N], F32)
            self.smalls = Rot([S.sbt(es, "sm%d" % i, [128, 16], F32) for i in range(8)])
            self.junk = S.sbt(es, "junk", [128, 1024], BF16)
            self.xn = Rot([S.sbt(es, "xn%d" % i, [128, 1024], BF16) for i in range(2)])
            for l in range(2):
                self.l = l
                S.barrier()
                S.dma("sync", self.colp.t[:], self.colp_d[l], writes=[self.colp])
                self.run("cp", self.copy_x)
                self.run("ffn1", lambda: self.ffn(es, l, RP_FFN1, self.W["ffn1_w13"][l], self.W["ffn1_w2"][l], first=(l == 0)))
                self.run("mix", lambda: self.mixer(es, l))
                self.run("xa", lambda: self.xattn(es, l))
                self.run("ffn2", lambda: self.ffn(es, l, RP_FFN2, self.W["ffn2_w13"][l], self.W["ffn2_w2"][l], first=False))
            S.barrier()
            S.finish()

    def run(self, name, fn):
        key = "%s%d" % (name, self.l)
        if self.stages is not None and key not in self.stages:
            return
        if self.stages is None and name == "cp":
            return
        fn()
        self.S.barrier()

    def copy_x(self):
        with contextlib.ExitStack() as es:
            hts = Rot([self.S.sbt(es, "cpx%d" % i, [128, D], F32) for i in range(2)])
            for ti in range(NT):
                ht = hts.next()
                self.S.dma("sync", ht.t[:], self.x[ti * 128:(ti + 1) * 128, :], writes=[ht])
                self.S.dma("sync", self.out[ti * 128:(ti + 1) * 128, :], ht.t[:], reads=[ht], writes=[self.hbuf[ti]])

    def idn(self):
        return self.cb.t[:, CB_ID:CB_ID + 128]

    def norm_T(self, src_ap, src_bufs, gain, dstT, col0, ht, nparts=128):
        S = self.S
        S.dma("sync", ht.t[:], src_ap, reads=src_bufs, writes=[ht])
        sm = self.smalls.next()
        S.act(self.junk.t[:], ht.t[:], AF.Square, [ht], [self.junk, sm], accum_out=sm.t[:, 0:1])
        S.act(sm.t[:, 1:2], sm.t[:, 0:1], AF.Sqrt, [sm], [sm], scale=1.0 / D, bias=EPS)
        S.op("dve", lambda e: e.reciprocal(out=sm.t[:, 2:3], in_=sm.t[:, 1:2]), [sm], [sm])
        xn = self.xn.next()
        S.stt("dve", xn.t[:], ht.t[:], sm.t[:, 2:3], gain.t[:], ALU.mult, ALU.mult, [ht, sm, gain], [xn])
        tb = self.tbanks.next()
        for kc in range(8):
            S.tr(tb.t[:, kc * 128:(kc + 1) * 128], xn.t[:, kc * 128:(kc + 1) * 128], self.cb, [xn], [tb])
        S.copy("act", dstT.t[:, 0:8, col0:col0 + 128], tb.t[:].rearrange("p (k t) -> p k t", k=8), [tb], [dstT])

    def load_w(self, dst, src_ap, nk, rows=128):
        was = dst.unordered
        dst.unordered = True
        for kc in range(nk):
            self.S.dma("pool", dst.t[0:rows, kc, :], src_ap[kc * 128:kc * 128 + rows, :], writes=[dst])
        dst.unordered = was

    def load_row(self, dst, l, off, n):
        self.S.dma("sync", dst.t[:, 0:n], self.rowp_d[l, off:off + n].partition_broadcast(128), writes=[dst])

    def ffn(self, es0, l, rp_off, w13_ap, w2_ap, first):
        S = self.S
        TT = 256
        with contextlib.ExitStack() as es:
            w13 = S.sbt(es, "w13", [128, 8, 2 * DFF], BF16)
            w2 = S.sbt(es, "w2", [128, 22, D], BF16)
            gain = S.sbt(es, "gain", [128, D], F32)
            self.load_row(gain, l, rp_off, D)
            FCB = ((0, 3), (3, 8), (8, 15), (15, 22))
            w13b = [Buf("w13b%d" % i, w13.t, unordered=True) for i in range(len(FCB))]
            w2b = [Buf("w2b%d" % i, w2.t, unordered=True) for i in range(len(FCB))]
            blk_of = {}
            for bi, (f0, f1) in enumerate(FCB):
                for fc in range(f0, f1):
                    blk_of[fc] = bi
                for half in range(2):
                    c0, c1 = half * DFF + f0 * 128, half * DFF + f1 * 128
                    for kc in range(8):
                        S.dma("pool", w13.t[:, kc, c0:c1], w13_ap[kc * 128:(kc + 1) * 128, c0:c1], writes=[w13b[bi]])
            for bi, (f0, f1) in enumerate(FCB):
                for fc in range(f0, f1):
                    S.dma("pool", w2.t[:, fc, :], w2_ap[fc * 128:(fc + 1) * 128, :], writes=[w2b[bi]])
            nTs = Rot([S.sbt(es, "nT%d" % i, [128, 8, TT], BF16) for i in range(2)])
            gTs = Rot([S.sbt(es, "gT%d" % i, [128, 22, TT], BF16) for i in range(1)])
            hts = Rot([S.sbt(es, "ht%d" % i, [128, D], F32) for i in range(4)])
            sas = Rot([S.sbt(es, "sa%d" % i, [128, TT], F32) for i in range(3)])
            src = self.x if first else self.out
            for tt in range(T // TT):
                n = nTs.next()
                hs = []
                for s in range(2):
                    ti = tt * 2 + s
                    ht = hts.next()
                    self.norm_T(src[ti * 128:(ti + 1) * 128, :], [] if first else [self.hbuf[ti]], gain, n, s * 128, ht)
                    hs.append(ht)
                g = gTs.next()
                for fc in range(22):
                    pa = self.banks.next()
                    pb = self.banks.next()
                    for kc in range(8):
                        S.mm(pa.t[:, 0:TT], w13.t[:, kc, fc * 128:(fc + 1) * 128], n.t[:, kc, :], kc == 0, kc == 7, [w13b[blk_of[fc]], n], [pa])
                    for kc in range(8):
                        S.mm(pb.t[:, 0:TT], w13.t[:, kc, DFF + fc * 128:DFF + (fc + 1) * 128], n.t[:, kc, :], kc == 0, kc == 7, [w13b[blk_of[fc]], n], [pb])
                    sa = sas.next()
                    S.act(sa.t[:], pa.t[:, 0:TT], AF.Silu, [pa], [sa])
                    S.tt("dve", g.t[:, fc, :], sa.t[:], pb.t[:, 0:TT], ALU.mult, [sa, pb], [g])
                for s in range(2):
                    ti = tt * 2 + s
                    for half in range(2):
                        po = self.banks.next()
                        for fc in range(22):
                            S.mm(po.t[:], g.t[:, fc, s * 128:(s + 1) * 128], w2.t[:, fc, half * 512:(half + 1) * 512], fc == 0, fc == 21, [g, w2b[blk_of[fc]]], [po])
                        hsl = hs[s].t[:, half * 512:(half + 1) * 512]
                        S.stt("dve", hsl, po.t[:], 0.5, hsl, ALU.mult, ALU.add, [po, hs[s]], [hs[s]])
                    S.dma("sync", self.out[ti * 128:(ti + 1) * 128, :], hs[s].t[:], reads=[hs[s]], writes=[self.hbuf[ti]])

    def xattn(self, es0, l):
        S = self.S
        TT = 256
        ones = self.cb.t[:, CB_ONES:CB_ONES + 128]
        with contextlib.ExitStack() as es:
            wq = S.sbt(es, "wq", [128, 8, D], BF16)
            wkv = S.sbt(es, "wkv", [128, 8, 2 * D], BF16)
            wo = S.sbt(es, "wo", [128, 8, D], BF16)
            gx = S.sbt(es, "gx", [128, D], F32)
            gm = S.sbt(es, "gm", [128, D], F32)
            self.load_row(gx, l, RP_XA, D)
            self.load_row(gm, l, RP_MEM, D)
            self.load_w(wkv, self.W["xa_wkv"][l], 8)
            self.load_w(wq, self.W["xa_wq"][l], 8)
            self.load_w(wo, self.W["xa_wo"][l], 8)
            mnT = S.sbt(es, "mnT", [128, 8, MEM], BF16)
            KnT = S.sbt(es, "KnT", [128, 8, MEM], BF16)
            V = S.sbt(es, "V", [128, 2, D], BF16)
            qf = S.sbt(es, "qf", [128, 8, TT], F32)
            sq = S.sbt(es, "sq", [128, 8, TT], BF16)
            rrs = Rot([S.sbt(es, "rr%d" % i, [128, 2, TT], F32) for i in range(2)])
            hts = Rot([S.sbt(es, "ht%d" % i, [128, D], F32) for i in range(4)])
            nTs = Rot([S.sbt(es, "nT%d" % i, [128, 8, TT], BF16) for i in range(2)])
            qnT = S.sbt(es, "qnT", [128, 8, TT], BF16)
            oT = S.sbt(es, "oT", [128, 8, TT], BF16)
            Es = Rot([S.sbt(es, "E%d" % i, [128, 2, TT], BF16) for i in range(2)])

            def qknorm(src_w, c0, rhsT, dst, cpoff):
                for fc in range(8):
                    pq = self.banks.next()
                    for kc in range(8):
                        S.mm(pq.t[:, 0:TT], src_w.t[:, kc, c0 + fc * 128:c0 + (fc + 1) * 128], rhsT.t[:, kc, :], kc == 0, kc == 7, [src_w, rhsT], [pq])
                    S.copy("dve", qf.t[:, fc, :], pq.t[:, 0:TT], [pq], [qf])
                    S.act(sq.t[:, fc, :], qf.t[:, fc, :], AF.Square, [qf], [sq])
                import os
                qc = int(os.environ.get("QK_CUT", "99"))
                if qc < 1:
                    return
                for hh in range(4):
                    pss = self.banks.next()
                    for c2 in range(2):
                        S.mm(pss.t[:, 0:TT], ones, sq.t[:, hh * 2 + c2, :], c2 == 0, c2 == 1, [self.cb, sq], [pss])
                    rr = rrs.next()
                    S.act(rr.t[:, 0, :], pss.t[:, 0:TT], AF.Sqrt, [pss], [rr], scale=1.0 / 256, bias=EPS)
                    if qc < 2:
                        continue
                    S.op("dve", lambda e: e.reciprocal(out=rr.t[:, 1, :], in_=rr.t[:, 0, :]), [rr], [rr])
                    if qc < 3:
                        continue
                    for c2 in range(2):
                        fc = hh * 2 + c2
                        S.stt("dve", dst.t[:, fc, :], qf.t[:, fc, :], self.colp.t[:, cpoff + c2:cpoff + c2 + 1], rr.t[:, 1, :], ALU.mult, ALU.mult, [qf, rr, self.colp], [dst])

            for mt in range(2):
                self.norm_T(self.mem[mt * 128:(mt + 1) * 128, :], [], gm, mnT, mt * 128, hts.next())
            import os
            cut = int(os.environ.get("XA_CUT", "99"))
            if cut < 1:
                return
            qknorm(wkv, 0, mnT, KnT, CP_XK)
            if cut < 2:
                return
            for mt in range(2):
                for half in range(2):
                    pv = self.banks.next()
                    for kc in range(8):
                        S.mm(pv.t[:], mnT.t[:, kc, mt * 128:(mt + 1) * 128], wkv.t[:, kc, D + half * 512:D + (half + 1) * 512], kc == 0, kc == 7, [mnT, wkv], [pv])
                    S.copy("act", V.t[:, mt, half * 512:(half + 1) * 512], pv.t[:], [pv], [V])
            if cut < 3:
                return
            for tt in range(T // TT):
                n = nTs.next()
                hs = []
                for s in range(2):
                    ti = tt * 2 + s
                    ht = hts.next()
                    self.norm_T(self.out[ti * 128:(ti + 1) * 128, :], [self.hbuf[ti]], gx, n, s * 128, ht)
                    hs.append(ht)
                qknorm(wq, 0, n, qnT, CP_XQ)
                if cut < 4:
                    continue
                for hh in range(4):
                    E = Es.next()
                    for mc in range(2):
                        ps_ = self.banks.next()
                        for c2 in range(2):
                            S.mm(ps_.t[:, 0:TT], KnT.t[:, hh * 2 + c2, mc * 128:(mc + 1) * 128], qnT.t[:, hh * 2 + c2, :], c2 == 0, c2 == 1, [KnT, qnT], [ps_])
                        S.act(E.t[:, mc, :], ps_.t[:, 0:TT], AF.Exp, [ps_], [E], scale=1.0 / 16)
                    pden = self.banks.next()
                    for mc in range(2):
                        S.mm(pden.t[:, 0:TT], ones, E.t[:, mc, :], mc == 0, mc == 1, [self.cb, E], [pden])
                    rr = rrs.next()
                    S.op("dve", lambda e: e.reciprocal(out=rr.t[:, 0, :], in_=pden.t[:, 0:TT]), [pden], [rr])
                    for dc in range(2):
                        po = self.banks.next()
                        for mc in range(2):
                            S.mm(po.t[:, 0:TT], V.t[:, mc, hh * 256 + dc * 128:hh * 256 + (dc + 1) * 128], E.t[:, mc, :], mc == 0, mc == 1, [V, E], [po])
                        S.tt("dve", oT.t[:, hh * 2 + dc, :], po.t[:, 0:TT], rr.t[:, 0, :], ALU.mult, [po, rr], [oT])
                for s in range(2):
                    ti = tt * 2 + s
                    for half in range(2):
                        po = self.banks.next()
                        for kc in range(8):
                            S.mm(po.t[:], oT.t[:, kc, s * 128:(s + 1) * 128], wo.t[:, kc, half * 512:(half + 1) * 512], kc == 0, kc == 7, [oT, wo], [po])
                        hsl = hs[s].t[:, half * 512:(half + 1) * 512]
                        S.tt("dve", hsl, po.t[:], hsl, ALU.add, [po, hs[s]], [hs[s]])
                    S.dma("sync", self.out[ti * 128:(ti + 1) * 128, :], hs[s].t[:], reads=[hs[s]], writes=[self.hbuf[ti]])

    def mixer(self, es0, l):
        S = self.S
        with contextlib.ExitStack() as es:
            sub = self.stages_sub()
            if "rw" in sub:
                self.rwkv(l)
                S.barrier()
            uT = S.sbt(es, "uT", [128, 8, T], BF16)
            with contextlib.ExitStack() as es2:
                gain = S.sbt(es2, "gmix", [128, D], F32)
                self.load_row(gain, l, RP_MIX, D)
                hts = Rot([S.sbt(es2, "ht%d" % i, [128, D], F32) for i in range(3)])
                for ti in range(NT):
                    self.norm_T(self.out[ti * 128:(ti + 1) * 128, :], [self.hbuf[ti]], gain, uT, ti * 128, hts.next())
            S.barrier()
            with contextlib.ExitStack() as es3:
                cf = S.sbt(es3, "cf", [128, CF_N], F32)
                S.dma("sync", cf.t[:], self.cf_d, writes=[cf])
                if "dil" in sub:
                    self.dilated(l, uT, cf)
                    S.barrier()
                if "ret" in sub:
                    self.retention(l, uT, cf)
                    S.barrier()
            S.barrier()
            if "merge" in sub:
                self.merge(l, uT)

    def stages_sub(self):
        if self.stages is None:
            return ("rw", "dil", "ret", "merge")
        return [k for k in ("rw", "dil", "ret", "merge") if k in self.stages]

    def dilated(self, l, uT, cf):
        S = self.S
        with contextlib.ExitStack() as es:
            Wgs = [S.sbt(es, "Wd%d" % i, [128, 8, 1536], BF16) for i in range(2)]
            gq = S.sbt(es, "gq", [128, 1536], F32)
            gk = S.sbt(es, "gk", [128, 1536], F32)
            self.load_row(gq, l, RP_DQ, 1536)
            self.load_row(gk, l, RP_DK, 1536)
            NB = 3
            qks = [S.sbt(es, "dqk%d" % i, [128, 8, 128], BF16) for i in range(NB)]
            Vas = [S.sbt(es, "dVa%d" % i, [128, 8, 65], BF16) for i in range(NB)]
            for v in Vas:
                S.op("dve", lambda e: e.memset(v.t[:], 1.0), [], [v])
            sqs = [S.sbt(es, "dsq%d" % i, [128, 512], F32) for i in range(2)]
            xns = [S.sbt(es, "dxn%d" % i, [128, 8, 64], F32) for i in range(2)]
            tms = [S.sbt(es, "dtm%d" % i, [128, 4, 8, 8], F32) for i in range(2)]
            sms = [S.sbt(es, "dsm%d" % i, [128, 16], F32) for i in range(2)]
            qrs = Rot([S.sbt(es, "dqr%d" % i, [128, 2, 512], BF16) for i in range(2)])
            Ees = Rot([S.sbt(es, "dEe%d" % i, [128, 512], BF16) for i in range(3)])
            Ems = Rot([S.sbt(es, "dEm%d" % i, [128, 4, 512], BF16) for i in range(2)])
            Osb = Rot([S.sbt(es, "dO%d" % i, [128, 520], F32) for i in range(2)])
            mpc = self.cb.t[:, CB_MPC:CB_MPC + 512]
            h8 = lambda ap: ap.rearrange("p (h e) -> p h e", h=8)
            flat = lambda ap: ap.rearrange("p h e -> p (h e)")
            gains = (gq, gk)

            def load_group(g):
                self.load_w(Wgs[g % 2], self.W["w_in"][l][:, C_DIL + g * 1536:C_DIL + (g + 1) * 1536], 8)

            bankA, bankB = Rot(self.bank_list[0:3]), Rot(self.bank_list[3:6])

            def stage_a(g, d, c, n, idx, W):
                bk = bankA
                nblk = 32 // d
                combo = c * nblk + n
                tok0 = n * 128 * d + c
                cur = idx % NB
                pq, pk, pv = bk.next(), bk.next(), bk.next()
                pps = (pq, pk)
                for pp, jq in ((pq, 0), (pk, 1), (pv, 2)):
                    for kc in range(8):
                        S.mm(pp.t[:], uT.t[:, kc, tok0:tok0 + 127 * d + 1:d], W.t[:, kc, jq * 512:(jq + 1) * 512], kc == 0, kc == 7, [uT, W], [pp])
                    yield
                S.copy("act", Vas[cur].t[:, :, 0:64], h8(pv.t[:]), [pv], [Vas[cur]])
                qr = qrs.next()
                o = (g * 32 + combo) * 8
                cosb = cf.t[:, CF_DCOS + o:CF_DCOS + o + 8].rearrange("p (a e) -> p a e", a=1).broadcast_to([128, 8, 8])
                sinb = cf.t[:, CF_DSIN + o:CF_DSIN + o + 8].rearrange("p (a e) -> p a e", a=1).broadcast_to([128, 8, 8])
                bc = lambda ap: ap.rearrange("p (h e) -> p h e", e=1).broadcast_to([128, 8, 64])
                steps = [
                    lambda w: S.copy("dve", flat(xns[w].t[:]), pps[w].t[:], [pps[w]], [xns[w]]),
                    lambda w: S.act(sqs[w].t[:], flat(xns[w].t[:]), AF.Square, [xns[w]], [sqs[w]]),
                    lambda w: S.op("dve", lambda e: e.tensor_reduce(out=sms[w].t[:, 0:8], in_=h8(sqs[w].t[:]), axis=AX.X, op=ALU.add), [sqs[w]], [sms[w]]),
                    lambda w: S.act(sms[w].t[:, 8:16], sms[w].t[:, 0:8], AF.Sqrt, [sms[w]], [sms[w]], scale=1.0 / 64, bias=EPS),
                    lambda w: S.op("dve", lambda e: e.reciprocal(out=sms[w].t[:, 0:8], in_=sms[w].t[:, 8:16]), [sms[w]], [sms[w]]),
                    lambda w: S.tt("dve", xns[w].t[:], xns[w].t[:], bc(sms[w].t[:, 0:8]), ALU.mult, [xns[w], sms[w]], [xns[w]]),
                    lambda w: S.tt("dve", xns[w].t[:], xns[w].t[:], h8(gains[w].t[:, g * 512:(g + 1) * 512]), ALU.mult, [xns[w], gains[w]], [xns[w]]),
                    lambda w: S.tt("dve", tms[w].t[:, 0], xns[w].t[:, :, 0:8], cosb, ALU.mult, [xns[w], cf], [tms[w]]),
                    lambda w: S.tt("dve", tms[w].t[:, 1], xns[w].t[:, :, 8:16], sinb, ALU.mult, [xns[w], cf], [tms[w]]),
                    lambda w: S.tt("dve", tms[w].t[:, 2], xns[w].t[:, :, 8:16], cosb, ALU.mult, [xns[w], cf], [tms[w]]),
                    lambda w: S.tt("dve", tms[w].t[:, 3], xns[w].t[:, :, 0:8], sinb, ALU.mult, [xns[w], cf], [tms[w]]),
                    lambda w: S.tt("dve", xns[w].t[:, :, 0:8], tms[w].t[:, 0], tms[w].t[:, 1], ALU.subtract, [tms[w], xns[w]], [xns[w]]),
                    lambda w: S.tt("dve", xns[w].t[:, :, 8:16], tms[w].t[:, 2], tms[w].t[:, 3], ALU.add, [tms[w], xns[w]], [xns[w]]),
                    lambda w: S.copy("act", qr.t[:, w, :], flat(xns[w].t[:]), [xns[w]], [qr]),
                ]
                for st_ in steps:
                    for w in range(2):
                        st_(w)
                        yield
                tb = self.tbanks.next()
                for w in range(2):
                    for j in range(4):
                        S.tr(tb.t[:, (w * 4 + j) * 128:(w * 4 + j + 1) * 128], qr.t[:, w, j * 128:(j + 1) * 128], self.cb, [qr], [tb])
                    yield
                S.copy("act", qks[cur].t[:, 0:8, :], tb.t[:].rearrange("p (k t) -> p k t", k=8), [tb], [qks[cur]])
                yield

            def stage_b(g, d, c, n, idx):
                bk = bankB
                tok0 = n * 128 * d + c
                cur, prv = idx % NB, (idx - 1) % NB
                qk = qks[cur]
                Em = Ems.next()
                for j in range(4):
                    psd = (bk.next(), bk.next())
                    for hh in range(2):
                        pb = 64 * hh
                        ps_ = psd[hh]
                        if n > 0:
                            S.mm(ps_.t[:, hh * 256:hh * 256 + 128], qks[prv].t[pb:pb + 64, 4 + j, :], qk.t[pb:pb + 64, j, :], True, True, [qks[prv], qk], [ps_])
                        S.mm(ps_.t[:, hh * 256 + 128:hh * 256 + 256], qk.t[pb:pb + 64, 4 + j, :], qk.t[pb:pb + 64, j, :], True, True, [qk], [ps_])
                    yield
                    Ee = Ees.next()
                    for hh in range(2):
                        ps_ = psd[hh]
                        lo = hh * 256 if n > 0 else hh * 256 + 128
                        hi = hh * 256 + 256
                        S.act(Ee.t[:, lo:hi], ps_.t[:, lo:hi], AF.Exp, [ps_], [Ee], scale=0.125)
                        yield
                    for hh in range(2):
                        lo = hh * 256 if n > 0 else hh * 256 + 128
                        hi = hh * 256 + 256
                        S.tt("dve", Em.t[:, j, lo:hi], Ee.t[:, lo:hi], mpc[:, lo:hi], ALU.mult, [Ee, self.cb], [Em])
                        yield
                O = Osb.next()
                for half in range(2):
                    po = bk.next()
                    for hq in range(4):
                        h = half * 4 + hq
                        j, hh = h // 2, h % 2
                        if n > 0:
                            S.mm(po.t[:, hq * 65:(hq + 1) * 65], Em.t[:, j, hh * 256:hh * 256 + 128], Vas[prv].t[:, h, :], True, False, [Em, Vas[prv]], [po])
                        S.mm(po.t[:, hq * 65:(hq + 1) * 65], Em.t[:, j, hh * 256 + 128:hh * 256 + 256], Vas[cur].t[:, h, :], n == 0, True, [Em, Vas[cur]], [po])
                    yield
                    S.copy("act", O.t[:, half * 260:(half + 1) * 260], po.t[:, 0:260], [po], [O])
                    yield
                S.dma("sync", self.oa_d[g, tok0:tok0 + 127 * d + 1:d, :], O.t[:], reads=[O], writes=[self.oa_b[g]])
                yield

            items = []
            for g, d in enumerate(DILS):
                for c in range(d):
                    for n in range(32 // d):
                        items.append((g, d, c, n))
            load_group(0)
            load_group(1)
            loaded = {0, 1}
            prev_b = None
            for idx, (g, d, c, n) in enumerate(items):
                if g == 1 and 2 not in loaded and c == 0 and n == 1:
                    load_group(2)
                    loaded.add(2)
                ga = stage_a(g, d, c, n, idx, Wgs[g % 2])
                interleave([ga, prev_b])
                prev_b = stage_b(g, d, c, n, idx)
            interleave([prev_b])

    def rwkv(self, l):
        S = self.S
        ST = 256
        NJ = 15
        with contextlib.ExitStack() as es:
            W = S.sbt(es, "Wrw", [128, 8, 1824], BF16)
            self.load_w(W, self.W["w_in"][l][:, 0:1824], 8)
            w2s = S.sbt(es, "w2s", [128, 512], BF16)
            a2s = S.sbt(es, "a2s", [128, 512], BF16)
            g2a = S.sbt(es, "g2a", [128, 512], BF16)
            g2b = S.sbt(es, "g2b", [128, 512], BF16)
            S.dma("pool", w2s.t[0:64, :], self.W["rw_w2"][l], writes=[w2s])
            S.dma("pool", a2s.t[64:128, :], self.W["rw_a2"][l], writes=[a2s])
            S.dma("pool", g2a.t[:, :], self.W["rw_g2"][l][0:128, :], writes=[g2a])
            S.dma("pool", g2b.t[0:32, :], self.W["rw_g2"][l][128:160, :], writes=[g2b])
            lnw = S.sbt(es, "lnw", [128, 512], F32)
            lnb = S.sbt(es, "lnb", [128, 512], F32)
            gain = S.sbt(es, "grw", [128, D], F32)
            self.load_row(lnw, l, RP_LNW, 512)
            self.load_row(lnb, l, RP_LNB, 512)
            self.load_row(gain, l, RP_MIX, D)
            hts = Rot([S.sbt(es, "rht%d" % i, [128, D], F32) for i in range(2)])
            nTs = [S.sbt(es, "rnT%d" % i, [128, 8, ST], BF16) for i in range(2)]
            cp = self.colp
            omk = S.sbt(es, "omk", [128, 4], F32)
            S.ts("dve", omk.t[:], cp.t[:, CP_KA:CP_KA + 4], -1.0, ALU.mult, [cp], [omk], s2=1.0, op1=ALU.add)
            rkb = S.sbt(es, "rkb", [128, 32], BF16)
            S.copy("dve", rkb.t[:], cp.t[:, CP_RK:CP_RK + 32], [cp], [rkb])
            onesf = S.sbt(es, "onesf", [128, 128], F32)
            S.op("dve", lambda e: e.memset(onesf.t[:], 1.0), [], [onesf])
            St = S.sbt(es, "St", [128, 4, 128], F32)
            Sb = S.sbt(es, "Sb", [128, 4, 128], BF16)
            Stmp = S.sbt(es, "Stmp", [128, 4, 128], F32)
            prevc = S.sbt(es, "prevc", [128, NJ], F32)
            for t_ in (St, Sb, prevc):
                S.op("dve", lambda e: e.memset(t_.t[:], 0.0), [], [t_])
            PM = S.sbt(es, "PM", [128, NJ, ST], F32)
            Pjs = Rot([S.sbt(es, "Pj%d" % i, [128, ST + 1], F32) for i in range(2)])
            dts = Rot([S.sbt(es, "dt%d" % i, [128, ST], F32) for i in range(2)])
            f32t = lambda nm: S.sbt(es, nm, [128, ST], F32)
            sgT, aT, lwT, LT, LpT, GT, GiT, GpT, kkkT, nrT, kkT, tkT, kmT, beT = [f32t("rw%d" % i) for i in range(14)]
            thw = S.sbt(es, "thw", [128, ST], BF16)
            adb = S.sbt(es, "adb", [128, ST], BF16)
            sqk = S.sbt(es, "sqk", [128, ST], BF16)
            two = lambda nm, sh, ty: [S.sbt(es, "%s%d" % (nm, i), sh, ty) for i in range(2)]
            sgd2 = two("sgd", [128, 2, ST], BF16)
            AR2 = two("AR", [128, 4, 2, 2, 128], BF16)
            BT2 = two("BT", [128, 4, ST], BF16)
            KT2 = two("KT", [128, 4, ST], BF16)
            VT2 = two("VT", [128, 4, ST], BF16)
            RK2 = two("RK", [128, 4, ST], BF16)
            GC2 = two("GC", [128, 4, 2], F32)
            BKtok2 = two("BKtok", [128, 2, 512], BF16)
            Vtok2 = two("Vtok", [128, 512], BF16)
            AXt2 = two("AXt", [128, 8, 384], BF16)
            AYt2 = two("AYt", [128, 8, 256], BF16)
            y22 = two("y2g", [128, 512], F32)
            smb2 = two("smb", [128, 16], F32)
            TT2 = [two("TT%d_" % i, [128, 8, 128], BF16) for i in range(2)]
            Bts = two("Bt", [128, 8, 128], BF16)
            Ans = two("An", [128, 8, 128], BF16)
            Xb = S.sbt(es, "Xb", [128, 512], BF16)
            Ub = S.sbt(es, "Ub", [128, 512], BF16)
            ysq = S.sbt(es, "ysq", [128, 512], F32)
            y1 = S.sbt(es, "y1", [128, 512], F32)
            yo = Rot([S.sbt(es, "yo%d" % i, [128, 512], BF16) for i in range(2)])
            smr = [S.sbt(es, "smr%d" % i, [128, 16], F32) for i in range(2)]
            m3 = self.cb.t[:, CB_M3:CB_M3 + 384]
            bo2 = self.cb.t[:, CB_BO2:CB_BO2 + 128]
            h8 = lambda ap: ap.rearrange("p (h e) -> p h e", h=8)
            c2 = lambda ap: ap.rearrange("p (c t) -> p c t", c=2)
            col = lambda off, j: cp.t[:, off + j:off + j + 1]
            TTfin = {}

            bankP = bankN = bankR = self.banks

            def prep(st):
                bk = bankP
                sp = st % 2
                nT, sgd, AR, BT, KT, VT, RK, GC = nTs[sp], sgd2[sp], AR2[sp], BT2[sp], KT2[sp], VT2[sp], RK2[sp], GC2[sp]
                for s_ in range(2):
                    ti = st * 2 + s_
                    self.norm_T(self.out[ti * 128:(ti + 1) * 128, :], [self.hbuf[ti]], gain, nT, s_ * 128, hts.next())
                    yield
                for jc in range(NJ):
                    M = 128 if jc < 14 else 32
                    pp = bk.next()
                    for kc in range(8):
                        S.mm(pp.t[0:M, 0:ST], W.t[:, kc, jc * 128:jc * 128 + M], nT.t[:, kc, :], kc == 0, kc == 7, [W, nT], [pp])
                    Pj = Pjs.next()
                    S.copy("act", Pj.t[0:M, 1:ST + 1], pp.t[0:M, 0:ST], [pp], [Pj])
                    S.copy("dve", Pj.t[0:M, 0:1], prevc.t[0:M, jc:jc + 1], [prevc], [Pj])
                    yield
                    dtm = dts.next()
                    S.tt("dve", dtm.t[0:M, :], Pj.t[0:M, 0:ST], Pj.t[0:M, 1:ST + 1], ALU.subtract, [Pj], [dtm])
                    S.stt("dve", PM.t[0:M, jc, :], dtm.t[0:M, :], col(CP_MU, jc)[0:M], Pj.t[0:M, 1:ST + 1], ALU.mult, ALU.add, [dtm, Pj, cp], [PM])
                    S.copy("act", prevc.t[0:M, jc:jc + 1], Pj.t[0:M, ST:ST + 1], [Pj], [prevc])
                    yield
                S.act(thw.t[0:64, :], PM.t[0:64, 12, :], AF.Tanh, [PM], [thw])
                S.copy("act", adb.t[64:128, :], PM.t[64:128, 12, :], [PM], [adb])
                S.act(sgd.t[:, 0, :], PM.t[:, 13, :], AF.Sigmoid, [PM], [sgd])
                S.act(sgd.t[0:32, 1, :], PM.t[0:32, 14, :], AF.Sigmoid, [PM], [sgd])
                yield
                for j in range(4):
                    rj, kj, vj = PM.t[:, j, :], PM.t[:, 4 + j, :], PM.t[:, 8 + j, :]
                    px, px2 = bk.next(), bk.next()
                    S.mm(px.t[:, 0:ST], w2s.t[0:64, j * 128:(j + 1) * 128], thw.t[0:64, :], True, True, [w2s, thw], [px])
                    S.mm(px2.t[:, ST:2 * ST], a2s.t[64:128, j * 128:(j + 1) * 128], adb.t[64:128, :], True, True, [a2s, adb], [px2])
                    S.act(sgT.t[:], px.t[:, 0:ST], AF.Sigmoid, [px, cp], [sgT], bias=col(CP_W0, j))
                    S.act(aT.t[:], px2.t[:, ST:2 * ST], AF.Sigmoid, [px2, cp], [aT], bias=col(CP_A0, j))
                    yield
                    S.ts("dve", lwT.t[:], sgT.t[:], -0.6065306597126334, ALU.mult, [sgT], [lwT])
                    S.ts("dve", kkkT.t[:], kj, col(CP_KK, j), ALU.mult, [PM, cp], [kkkT])
                    yield
                    for ch in range(2):
                        S.op("dve", lambda e: e.tensor_tensor_scan(out=LT.t[:, ch * 128:(ch + 1) * 128], data0=onesf.t[:], data1=lwT.t[:, ch * 128:(ch + 1) * 128], initial=0.0, op0=ALU.mult, op1=ALU.add), [onesf, lwT], [LT])
                    S.act(sqk.t[:], kkkT.t[:], AF.Square, [kkkT], [sqk])
                    yield
                    pss = bk.next()
                    S.mm(pss.t[:, 0:ST], bo2, sqk.t[:], True, True, [self.cb, sqk], [pss])
                    S.act(nrT.t[:], pss.t[:, 0:ST], AF.Sqrt, [pss], [nrT])
                    S.tt("dve", LpT.t[:], LT.t[:], lwT.t[:], ALU.subtract, [LT, lwT], [LpT])
                    S.act(GT.t[:], LT.t[:], AF.Exp, [LT], [GT])
                    yield
                    S.act(GiT.t[:], LT.t[:], AF.Exp, [LT], [GiT], scale=-1.0)
                    S.act(GpT.t[:], LpT.t[:], AF.Exp, [LpT], [GpT])
                    S.ts("dve", tkT.t[:], aT.t[:], col(CP_KA, j), ALU.mult, [aT, cp, omk], [tkT], s2=omk.t[:, j:j + 1], op1=ALU.add)
                    yield
                    S.copy("dve", GC.t[:, j, :].rearrange("p (c e) -> p c e", e=1), c2(GT.t[:])[:, :, 127:128], [GT], [GC])
                    S.tt("dve", kmT.t[:], kj, tkT.t[:], ALU.mult, [PM, tkT], [kmT])
                    yield
                    S.ts("dve", nrT.t[:], nrT.t[:], 1e-12, ALU.max, [nrT], [nrT])
                    S.tt("dve", AR.t[:, j, :, 1, :], c2(rj), c2(GT.t[:]), ALU.mult, [PM, GT], [AR])
                    S.op("dve", lambda e: e.reciprocal(out=nrT.t[:], in_=nrT.t[:]), [nrT], [nrT])
                    yield
                    S.tt("dve", KT.t[:, j, :], kmT.t[:], GiT.t[:], ALU.mult, [kmT, GiT], [KT])
                    S.tt("dve", kkT.t[:], kkkT.t[:], nrT.t[:], ALU.mult, [kkkT, nrT], [kkT])
                    S.copy("act", VT.t[:, j, :], vj, [PM], [VT])
                    yield
                    S.tt("dve", RK.t[:, j, :], rj, kmT.t[:], ALU.mult, [PM, kmT], [RK])
                    S.tt("dve", beT.t[:], kkT.t[:], aT.t[:], ALU.mult, [kkT, aT], [beT])
                    yield
                    S.stt("dve", AR.t[:, j, :, 0, :], c2(kkT.t[:]), -1.0, c2(GpT.t[:]), ALU.mult, ALU.mult, [kkT, GpT], [AR])
                    S.tt("dve", BT.t[:, j, :], beT.t[:], GiT.t[:], ALU.mult, [beT, GiT], [BT])
                    yield

            def neu(ci):
                bk = bankN
                st, ch, sp, cq = ci // 2, ci % 2, (ci // 2) % 2, ci % 2
                sgd, AR, BT, KT, VT, RK = sgd2[sp], AR2[sp], BT2[sp], KT2[sp], VT2[sp], RK2[sp]
                BKtok, Vtok, AXt, AYt, y2, smb, TTs = BKtok2[cq], Vtok2[cq], AXt2[cq], AYt2[cq], y22[cq], smb2[cq], TT2[cq]
                tsl = slice(ch * 128, (ch + 1) * 128)
                tb1, tb2 = self.tbanks.next(), self.tbanks.next()
                for j in range(4):
                    S.tr(tb1.t[:, j * 128:(j + 1) * 128], BT.t[:, j, tsl], self.cb, [BT], [tb1])
                    S.tr(tb1.t[:, 512 + j * 128:512 + (j + 1) * 128], KT.t[:, j, tsl], self.cb, [KT], [tb1])
                    S.tr(tb2.t[:, j * 128:(j + 1) * 128], VT.t[:, j, tsl], self.cb, [VT], [tb2])
                S.copy("act", BKtok.t[:].rearrange("p a (k t) -> p (a k) t", k=4), tb1.t[:].rearrange("p (k t) -> p k t", k=8), [tb1], [BKtok])
                S.copy("act", Vtok.t[:].rearrange("p (k t) -> p k t", k=4), tb2.t[:, 0:512].rearrange("p (k t) -> p k t", k=4), [tb2], [Vtok])
                yield
                pgb = bk.next()
                S.mm(pgb.t[:], sgd.t[:, 0, tsl], g2a.t[:], True, False, [sgd, g2a], [pgb])
                S.mm(pgb.t[:], sgd.t[0:32, 1, tsl], g2b.t[0:32, :], False, True, [sgd, g2b], [pgb])
                S.copy("act", y2.t[:], pgb.t[:], [pgb], [y2])
                yield
                pbn = bk.next()
                for j in range(4):
                    S.mm(pbn.t[:, 0:8], RK.t[:, j, tsl], rkb.t[:, j * 8:(j + 1) * 8], j == 0, j == 3, [RK, rkb], [pbn])
                S.copy("act", smb.t[:, 0:8], pbn.t[:, 0:8], [pbn], [smb])
                yield
                for h in range(8):
                    j, pb = h // 2, 64 * (h % 2)
                    bx, by = bk.next(), bk.next()
                    arf = AR.t[pb:pb + 64, j, ch, :, :].rearrange("p a t -> p (a t)")
                    S.mm(bx.t[:, 0:256], BT.t[pb:pb + 64, j, tsl], arf, True, True, [BT, AR], [bx])
                    S.mm(bx.t[:, 256:384], AR.t[pb:pb + 64, j, ch, 0, :], BT.t[pb:pb + 64, j, tsl], True, True, [BT, AR], [bx])
                    S.mm(by.t[:, 0:256], KT.t[pb:pb + 64, j, tsl], arf, True, True, [KT, AR], [by])
                    S.tt("dve", AXt.t[:, h, :], bx.t[:, 0:384], m3, ALU.mult, [bx, self.cb], [AXt])
                    S.tt("dve", AYt.t[:, h, :], by.t[:, 0:256], m3[:, 0:256], ALU.mult, [by, self.cb], [AYt])
                    yield
                idb = self.cb.t[:, CB_ID:CB_ID + 128].rearrange("p (a t) -> p a t", a=1).broadcast_to([128, 8, 128])
                S.tt("dve", TTs[0].t[:], AXt.t[:, :, 0:128], idb, ALU.add, [AXt, self.cb], [TTs[0]])
                yield
                Btp = lambda h: AXt.t[:, h, 0:128]
                Ap = lambda h: AXt.t[:, h, 256:384]
                Btb, Ab = AXt, AXt
                cur = 0
                for n in range(1, 7):
                    Bn, An, Tn, Tp = Bts[n % 2], Ans[n % 2], TTs[(cur + 1) % 2], TTs[cur]
                    for grp in range(2):
                        gs = slice(grp * 4, grp * 4 + 4)
                        pA = bk.next()
                        pB = bk.next() if n < 6 else None
                        for hq in range(4):
                            h = grp * 4 + hq
                            if n < 6:
                                S.mm(pB.t[:, hq * 128:(hq + 1) * 128], Ap(h), Btp(h), True, True, [Ab, Btb], [pB])
                            S.mm(pA.t[:, hq * 128:(hq + 1) * 128], Btp(h), Ap(h), True, True, [Ab, Btb], [pA])
                        S.copy("act", An.t[:, gs, :].rearrange("p a t -> p (a t)"), pA.t[:], [pA], [An])
                        if n < 6:
                            S.copy("dve", Bn.t[:, gs, :].rearrange("p a t -> p (a t)"), pB.t[:], [pB], [Bn])
                        yield
                        pT = bk.next()
                        for hq in range(4):
                            h = grp * 4 + hq
                            S.mm(pT.t[:, hq * 128:(hq + 1) * 128], An.t[:, h, :], Tp.t[:, h, :], True, True, [An, Tp], [pT])
                        S.tt("dve", Tn.t[:, gs, :].rearrange("p a t -> p (a t)"), pT.t[:], Tp.t[:, gs, :].rearrange("p a t -> p (a t)"), ALU.add, [pT, Tp], [Tn])
                        yield
                    Btp = (lambda B: (lambda h: B.t[:, h, :]))(Bn)
                    Ap = (lambda A: (lambda h: A.t[:, h, :]))(An)
                    Btb, Ab = Bn, An
                    cur = (cur + 1) % 2
                TTfin[ci] = TTs[cur]

            def rec(ci):
                bk = bankR
                st, ch, sp, cq = ci // 2, ci % 2, (ci // 2) % 2, ci % 2
                AR, GC = AR2[sp], GC2[sp]
                BKtok, Vtok, AXt, AYt, y2, smb = BKtok2[cq], Vtok2[cq], AXt2[cq], AYt2[cq], y22[cq], smb2[cq]
                TTf = TTfin[ci]
                for j in range(4):
                    S.ts("dve", Stmp.t[:, j, :], St.t[:, j, :], GC.t[:, j, ch:ch + 1], ALU.mult, [St, GC], [Stmp])
                yield
                pX = bk.next()
                for h in range(8):
                    j, pb = h // 2, 64 * (h % 2)
                    S.mm(pX.t[:, h * 64:(h + 1) * 64], AR.t[pb:pb + 64, j, ch, 0, :], Sb.t[pb:pb + 64, j, (h % 2) * 64:(h % 2 + 1) * 64], True, False, [AR, Sb], [pX])
                    S.mm(pX.t[:, h * 64:(h + 1) * 64], AYt.t[:, h, 0:128], Vtok.t[:, h * 64:(h + 1) * 64], False, True, [AYt, Vtok], [pX])
                S.copy("act", Xb.t[:], pX.t[:], [pX], [Xb])
                yield
                pU = bk.next()
                for h in range(8):
                    S.mm(pU.t[:, h * 64:(h + 1) * 64], TTf.t[:, h, :], Xb.t[:, h * 64:(h + 1) * 64], True, True, [TTf, Xb], [pU])
                S.copy("dve", Ub.t[:], pU.t[:], [pU], [Ub])
                yield
                pS = bk.next()
                for j in range(4):
                    js = slice(j * 128, (j + 1) * 128)
                    S.mm(pS.t[:, js], BKtok.t[:, 0, js], Ub.t[:, js], True, False, [BKtok, Ub], [pS])
                    S.mm(pS.t[:, js], BKtok.t[:, 1, js], Vtok.t[:, js], False, True, [BKtok, Vtok], [pS])
                pY = bk.next()
                for h in range(8):
                    j, pb = h // 2, 64 * (h % 2)
                    hs_ = slice(h * 64, (h + 1) * 64)
                    S.mm(pY.t[:, hs_], AR.t[pb:pb + 64, j, ch, 1, :], Sb.t[pb:pb + 64, j, (h % 2) * 64:(h % 2 + 1) * 64], True, False, [AR, Sb], [pY])
                    S.mm(pY.t[:, hs_], AXt.t[:, h, 128:256], Ub.t[:, hs_], False, False, [AXt, Ub], [pY])
                    S.mm(pY.t[:, hs_], AYt.t[:, h, 128:256], Vtok.t[:, hs_], False, True, [AYt, Vtok], [pY])
                for j in range(4):
                    S.stt("dve", St.t[:, j, :], pS.t[:, j * 128:(j + 1) * 128], GC.t[:, j, ch:ch + 1], Stmp.t[:, j, :], ALU.mult, ALU.add, [pS, GC, Stmp], [St])
                S.copy("act", Sb.t[:], St.t[:], [St], [Sb])
                sm, sm2 = smr
                S.copy("dve", y1.t[:], pY.t[:], [pY], [y1])
                yield
                S.act(ysq.t[:], y1.t[:], AF.Square, [y1], [ysq])
                S.op("dve", lambda e: e.tensor_reduce(out=sm.t[:, 0:8], in_=h8(y1.t[:]), axis=AX.X, op=ALU.add), [y1], [sm])
                yield
                S.op("dve", lambda e: e.tensor_reduce(out=sm2.t[:, 0:8], in_=h8(ysq.t[:]), axis=AX.X, op=ALU.add), [ysq], [sm2])
                yield
                S.ts("dve", sm.t[:, 0:8], sm.t[:, 0:8], 1.0 / 64, ALU.mult, [sm], [sm])
                yield
                S.tt("dve", sm.t[:, 8:16], sm.t[:, 0:8], sm.t[:, 0:8], ALU.mult, [sm], [sm])
                yield
                S.stt("dve", sm2.t[:, 8:16], sm2.t[:, 0:8], 1.0 / 64, sm.t[:, 8:16], ALU.mult, ALU.subtract, [sm, sm2], [sm2])
                yield
                S.act(sm2.t[:, 0:8], sm2.t[:, 8:16], AF.Sqrt, [sm2], [sm2], bias=GN_EPS)
                yield
                S.op("dve", lambda e: e.reciprocal(out=sm2.t[:, 8:16], in_=sm2.t[:, 0:8]), [sm2], [sm2])
                yield
                bc = lambda ap: ap.rearrange("p (h e) -> p h e", e=1).broadcast_to([128, 8, 64])
                S.tt("dve", h8(y1.t[:]), h8(y1.t[:]), bc(sm.t[:, 0:8]), ALU.subtract, [y1, sm], [y1])
                S.tt("dve", h8(ysq.t[:]), h8(Vtok.t[:]), bc(smb.t[:, 0:8]), ALU.mult, [Vtok, smb], [ysq])
                yield
                S.tt("dve", h8(y1.t[:]), h8(y1.t[:]), bc(sm2.t[:, 8:16]), ALU.mult, [y1, sm2], [y1])
                yield
                S.tt("dve", y1.t[:], y1.t[:], lnw.t[:], ALU.mult, [y1, lnw], [y1])
                yield
                S.tt("dve", y1.t[:], y1.t[:], lnb.t[:], ALU.add, [y1, lnb], [y1])
                yield
                S.tt("dve", y1.t[:], y1.t[:], ysq.t[:], ALU.add, [y1, ysq], [y1])
                yield
                yb_ = yo.next()
                S.tt("dve", yb_.t[:], y1.t[:], y2.t[:], ALU.mult, [y1, y2], [yb_])
                S.dma("sync", self.ya_d[ci * 128:(ci + 1) * 128, :], yb_.t[:], reads=[yb_], writes=[self.ya_b[ci]])
                yield

            NCH = NT
            interleave([prep(0)])
            interleave([neu(0)])
            for ci in range(NCH):
                st = ci // 2
                streams = [rec(ci)]
                if ci + 1 < NCH:
                    streams.append(neu(ci + 1))
                if ci % 2 == 0 and st + 1 < T // ST:
                    streams.append(prep(st + 1))
                interleave(streams)

    def retention(self, l, uT, cf):
        S = self.S
        with contextlib.ExitStack() as es:
            W = S.sbt(es, "Wret", [128, 8, 1536], BF16)
            self.load_w(W, self.W["w_in"][l][:, C_RET:C_RET + 1536], 8)
            gret = S.sbt(es, "gret", [128, 512], F32)
            self.load_row(gret, l, RP_RET, 512)
            Sr = S.sbt(es, "Sr", [128, 2, 256], F32)
            Srb = S.sbt(es, "Srb", [128, 2, 256], BF16)
            S.op("dve", lambda e: e.memset(Sr.t[:], 0.0), [], [Sr])
            S.op("dve", lambda e: e.memset(Srb.t[:], 0.0), [], [Srb])
            vbs = Rot([S.sbt(es, "vb%d" % i, [128, 512], BF16) for i in range(2)])
            sgs = Rot([S.sbt(es, "sg%d" % i, [128, 512], F32) for i in range(2)])
            qks = Rot([S.sbt(es, "qk%d" % i, [128, 2, 4, 64], F32) for i in range(2)])
            qkrs = Rot([S.sbt(es, "qkr%d" % i, [128, 2, 4, 64], F32) for i in range(2)])
            tmps = Rot([S.sbt(es, "rt%d" % i, [128, 4, 4, 32], F32) for i in range(2)])
            qts = Rot([S.sbt(es, "qt%d" % i, [128, 3, 256], BF16) for i in range(2)])
            qkTs = Rot([S.sbt(es, "qkT%d" % i, [128, 4, 128], BF16) for i in range(2)])
            sTms = Rot([S.sbt(es, "sTm%d" % i, [128, 4, 128], BF16) for i in range(2)])
            ysqs = Rot([S.sbt(es, "ysq%d" % i, [128, 512], F32) for i in range(2)])
            yns = Rot([S.sbt(es, "yn%d" % i, [128, 512], F32) for i in range(2)])
            ycs = Rot([S.sbt(es, "yc%d" % i, [128, 512], BF16) for i in range(2)])
            Dp = cf.t[:, CF_DP:CF_DP + 512]
            import os
            rc = int(os.environ.get('RET_CUT', '99'))
            for c in range(int(os.environ.get('RET_NCH', NT))):
                tsl = slice(c * 128, (c + 1) * 128)
                if os.environ.get("RET_VAR", "") == "only_tr":
                    tb = self.tbanks.next()
                    for a in range(4):
                        S.tr(tb.t[:, a * 128:(a + 1) * 128], self.junk.t[:, a * 128:(a + 1) * 128], self.cb, [self.junk], [tb])
                    continue
                if os.environ.get("RET_VAR", "") == "mm_tr":
                    pqk = self.banks.next()
                    for kc in range(8):
                        S.mm(pqk.t[:], uT.t[:, kc, tsl], W.t[:, kc, 0:512], kc == 0, kc == 7, [uT, W], [pqk])
                    tb = self.tbanks.next()
                    for a in range(4):
                        S.tr(tb.t[:, a * 128:(a + 1) * 128], self.junk.t[:, a * 128:(a + 1) * 128], self.cb, [self.junk], [tb])
                    continue
                pqk, pv, pg = self.banks.next(), self.banks.next(), self.banks.next()
                for pp, c0 in ((pqk, 0), (pv, 512), (pg, 1024)):
                    for kc in range(8):
                        S.mm(pp.t[:], uT.t[:, kc, tsl], W.t[:, kc, c0:c0 + 512], kc == 0, kc == 7, [uT, W], [pp])
                vb = vbs.next()
                S.copy("act", vb.t[:], pv.t[:], [pv], [vb])
                sg = sgs.next()
                S.act(sg.t[:], pg.t[:], AF.Silu, [pg], [sg])
                qk = qks.next()
                S.copy("dve", qk.t[:].rearrange("p a h e -> p (a h e)"), pqk.t[:], [pqk], [qk])
                if rc < 1:
                    continue
                qkr = qkrs.next()
                tm = tmps.next()
                cosb = cf.t[:, CF_RCOS + c * 32:CF_RCOS + (c + 1) * 32].rearrange("p (a e) -> p a e", a=1).broadcast_to([128, 4, 32])
                sinb = cf.t[:, CF_RSIN + c * 32:CF_RSIN + (c + 1) * 32].rearrange("p (a e) -> p a e", a=1).broadcast_to([128, 4, 32])
                for w in range(2):
                    x1 = qk.t[:, w, :, 0:32]
                    x2 = qk.t[:, w, :, 32:64]
                    eng = "dve"
                    S.tt(eng, tm.t[:, 0], x1, cosb, ALU.mult, [qk, cf], [tm])
                    S.tt(eng, tm.t[:, 1], x2, sinb, ALU.mult, [qk, cf], [tm])
                    S.tt(eng, tm.t[:, 2], x2, cosb, ALU.mult, [qk, cf], [tm])
                    S.tt(eng, tm.t[:, 3], x1, sinb, ALU.mult, [qk, cf], [tm])
                    S.tt(eng, qkr.t[:, w, :, 0:32], tm.t[:, 0], tm.t[:, 1], ALU.subtract, [tm], [qkr])
                    S.tt(eng, qkr.t[:, w, :, 32:64], tm.t[:, 2], tm.t[:, 3], ALU.add, [tm], [qkr])
                if rc < 2:
                    continue
                qt = qts.next()
                qdb = cf.t[:, CF_QD:CF_QD + 4].rearrange("p (h e) -> p h e", e=1).broadcast_to([128, 4, 64])
                kdb = cf.t[:, CF_KD:CF_KD + 4].rearrange("p (h e) -> p h e", e=1).broadcast_to([128, 4, 64])
                v3 = lambda ap: ap.rearrange("p (h e) -> p h e", h=4)
                S.tt("dve", v3(qt.t[:, 0, :]), qkr.t[:, 0], qdb, ALU.mult, [qkr, cf], [qt])
                S.copy("act", v3(qt.t[:, 1, :]), qkr.t[:, 1], [qkr], [qt])
                S.tt("dve", v3(qt.t[:, 2, :]), qkr.t[:, 1], kdb, ALU.mult, [qkr, cf], [qt])
                if rc < 3:
                    continue
                tb = self.tbanks.next()
                rv = os.environ.get("RET_VAR", "")
                for a in range(2):
                    for jj in range(2):
                        if rv == "notr" or (rv == "tr1" and (a or jj)):
                            continue
                        src_ = qt.t[:, a, jj * 128:(jj + 1) * 128]
                        if rv == "xnsrc":
                            src_ = self.junk.t[:, (a * 2 + jj) * 128:(a * 2 + jj + 1) * 128]
                        S.tr(tb.t[:, (a * 2 + jj) * 128:(a * 2 + jj + 1) * 128], src_, self.cb, [qt], [tb])
                qkT = qkTs.next()
                if rv != "nocopy":
                    S.copy("act", qkT.t[:, 0:4, :], tb.t[:, 0:512].rearrange("p (k t) -> p k t", k=4), [tb], [qkT])
                if rc < 4:
                    continue
                psr = (self.banks.next(), self.banks.next())
                for h in range(4):
                    jj, pb = h // 2, 64 * (h % 2)
                    S.mm(psr[h % 2].t[:, h * 128:(h + 1) * 128], qkT.t[pb:pb + 64, 2 + jj, :], qkT.t[pb:pb + 64, jj, :], True, True, [qkT], [psr[h % 2]])
                sTm = sTms.next()
                for h in range(4):
                    S.tt("dve", sTm.t[:, h, :], psr[h % 2].t[:, h * 128:(h + 1) * 128], Dp[:, h * 128:(h + 1) * 128], ALU.mult, [psr[h % 2], cf], [sTm])
                if rc < 5:
                    continue
                po = self.banks.next()
                for h in range(4):
                    jj, pb = h // 2, 64 * (h % 2)
                    S.mm(po.t[:, h * 128:(h + 1) * 128], sTm.t[:, h, :], vb.t[:, h * 128:(h + 1) * 128], True, False, [sTm, vb], [po])
                    S.mm(po.t[:, h * 128:(h + 1) * 128], qkT.t[pb:pb + 64, jj, :], Srb.t[pb:pb + 64, jj, (h % 2) * 128:(h % 2 + 1) * 128], False, True, [qkT, Srb], [po])
                if rc < 6:
                    continue
                pkv = self.banks.next()
                for jj in range(2):
                    S.mm(pkv.t[:, jj * 256:(jj + 1) * 256], qt.t[:, 2, jj * 128:(jj + 1) * 128], vb.t[:, jj * 256:(jj + 1) * 256], True, True, [qt, vb], [pkv])
                for jj in range(2):
                    S.stt("dve", Sr.t[:, jj, :], Sr.t[:, jj, :], cf.t[:, CF_GC + jj:CF_GC + jj + 1], pkv.t[:, jj * 256:(jj + 1) * 256], ALU.mult, ALU.add, [Sr, cf, pkv], [Sr])
                S.copy("act", Srb.t[:], Sr.t[:], [Sr], [Srb])
                if rc < 7:
                    continue
                ysq = ysqs.next()
                S.act(ysq.t[:], po.t[:], AF.Square, [po], [ysq])
                sm = self.smalls.next()
                S.op("dve", lambda e: e.tensor_reduce(out=sm.t[:, 0:4], in_=ysq.t[:].rearrange("p (h e) -> p h e", h=4), axis=AX.X, op=ALU.add), [ysq], [sm])
                S.act(sm.t[:, 4:8], sm.t[:, 0:4], AF.Sqrt, [sm], [sm], scale=1.0 / 128, bias=EPS)
                S.op("dve", lambda e: e.reciprocal(out=sm.t[:, 8:12], in_=sm.t[:, 4:8]), [sm], [sm])
                yn = yns.next()
                S.tt("dve", v3(yn.t[:]), v3(po.t[:]), sm.t[:, 8:12].rearrange("p (h e) -> p h e", e=1).broadcast_to([128, 4, 128]), ALU.mult, [po, sm], [yn])
                S.tt("dve", yn.t[:], yn.t[:], gret.t[:], ALU.mult, [yn, gret], [yn])
                yc = ycs.next()
                S.tt("dve", yc.t[:], yn.t[:], sg.t[:], ALU.mult, [yn, sg], [yc])
                S.dma("sync", self.yc_d[tsl, :], yc.t[:], reads=[yc], writes=[self.yc_b[c]])

    def merge(self, l, uT):
        S = self.S
        with contextlib.ExitStack() as es:
            Wg = S.sbt(es, "Wg", [128, 8, 3072], BF16)
            Wb = S.sbt(es, "Wb", [128, 12, D], BF16)
            Wo = S.sbt(es, "Wo", [128, 8, D], BF16)
            self.load_w(Wg, self.W["w_in"][l][:, C_GATE:C_GATE + 3072], 8)
            for b, nm in enumerate(("w_branch_rwkv", "w_branch_dil", "w_branch_ret")):
                for kc in range(4):
                    S.dma("pool", Wb.t[:, b * 4 + kc, :], self.W[nm][l][kc * 128:(kc + 1) * 128, :], writes=[Wb])
            self.load_w(Wo, self.W["w_out"][l], 8)
            yas = Rot([S.sbt(es, "ya%d" % i, [128, 512], BF16) for i in range(2)])
            ycs = Rot([S.sbt(es, "yc%d" % i, [128, 512], BF16) for i in range(2)])
            ybs = Rot([S.sbt(es, "yb%d" % i, [128, 512], BF16) for i in range(2)])
            oas = Rot([S.sbt(es, "oa%d" % i, [128, 3, 520], F32) for i in range(2)])
            hts = Rot([S.sbt(es, "ht%d" % i, [128, D], F32) for i in range(2)])
            yTs = Rot([S.sbt(es, "yT%d" % i, [128, 12, 128], BF16) for i in range(2)])
            mTs = Rot([S.sbt(es, "mT%d" % i, [128, 8, 128], BF16) for i in range(2)])
            sigs = Rot([S.sbt(es, "sig%d" % i, [128, 384], F32) for i in range(2)])
            prods = Rot([S.sbt(es, "prod%d" % i, [128, 384], F32) for i in range(2)])
            for ti in range(NT):
                tsl = slice(ti * 128, (ti + 1) * 128)
                ya, yc, yb, oa, ht = yas.next(), ycs.next(), ybs.next(), oas.next(), hts.next()
                S.dma("sync", ya.t[:], self.ya_d[tsl, :], reads=[self.ya_b[ti]], writes=[ya])
                S.dma("sync", yc.t[:], self.yc_d[tsl, :], reads=[self.yc_b[ti]], writes=[yc])
                for g in range(3):
                    S.dma("sync", oa.t[:, g, :], self.oa_d[g, tsl, :], reads=[self.oa_b[g]], writes=[oa])
                S.dma("sync", ht.t[:], self.out[tsl, :], reads=[self.hbuf[ti]], writes=[ht])
                S.tt("dve", oa.t[:, 0, :], oa.t[:, 0, :], oa.t[:, 1, :], ALU.add, [oa], [oa])
                S.tt("dve", oa.t[:, 0, :], oa.t[:, 0, :], oa.t[:, 2, :], ALU.add, [oa], [oa])
                o3 = oa.t[:, 0, :].rearrange("p (h e) -> p h e", h=8)
                sm = self.smalls.next()
                S.op("dve", lambda e: e.reciprocal(out=sm.t[:, 0:8].rearrange("p (h e) -> p h e", e=1), in_=o3[:, :, 64:65]), [oa], [sm])
                S.tt("dve", yb.t[:].rearrange("p (h e) -> p h e", h=8), o3[:, :, 0:64], sm.t[:, 0:8].rearrange("p (h e) -> p h e", e=1).broadcast_to([128, 8, 64]), ALU.mult, [oa, sm], [yb])
                yT = yTs.next()
                tb0, tb1 = self.tbanks.next(), self.tbanks.next()
                for kc in range(4):
                    S.tr(tb0.t[:, kc * 128:(kc + 1) * 128], ya.t[:, kc * 128:(kc + 1) * 128], self.cb, [ya], [tb0])
                for kc in range(4):
                    S.tr(tb0.t[:, 512 + kc * 128:512 + (kc + 1) * 128], yb.t[:, kc * 128:(kc + 1) * 128], self.cb, [yb], [tb0])
                for kc in range(4):
                    S.tr(tb1.t[:, kc * 128:(kc + 1) * 128], yc.t[:, kc * 128:(kc + 1) * 128], self.cb, [yc], [tb1])
                S.copy("act", yT.t[:, 0:8, :], tb0.t[:].rearrange("p (k t) -> p k t", k=8), [tb0], [yT])
                S.copy("act", yT.t[:, 8:12, :], tb1.t[:, 0:512].rearrange("p (k t) -> p k t", k=4), [tb1], [yT])
                mT = mTs.next()
                for oc in range(8):
                    pg_, pz = self.banks.next(), self.banks.next()
                    for b in range(3):
                        for kc in range(8):
                            S.mm(pg_.t[:, b * 128:(b + 1) * 128], Wg.t[:, kc, b * 1024 + oc * 128:b * 1024 + (oc + 1) * 128], uT.t[:, kc, tsl], kc == 0, kc == 7, [Wg, uT], [pg_])
                    for b in range(3):
                        for kc in range(4):
                            S.mm(pz.t[:, b * 128:(b + 1) * 128], Wb.t[:, b * 4 + kc, oc * 128:(oc + 1) * 128], yT.t[:, b * 4 + kc, :], kc == 0, kc == 3, [Wb, yT], [pz])
                    sig, prod = sigs.next(), prods.next()
                    S.act(sig.t[:], pg_.t[:, 0:384], AF.Sigmoid, [pg_], [sig])
                    S.tt("dve", prod.t[:], sig.t[:], pz.t[:, 0:384], ALU.mult, [sig, pz], [prod])
                    S.tt("dve", prod.t[:, 0:128], prod.t[:, 0:128], prod.t[:, 128:256], ALU.add, [prod], [prod])
                    S.tt("dve", mT.t[:, oc, :], prod.t[:, 0:128], prod.t[:, 256:384], ALU.add, [prod], [mT])
                for half in range(2):
                    po = self.banks.next()
                    for kc in range(8):
                        S.mm(po.t[:], mT.t[:, kc, :], Wo.t[:, kc, half * 512:(half + 1) * 512], kc == 0, kc == 7, [mT, Wo], [po])
                    hsl = ht.t[:, half * 512:(half + 1) * 512]
                    S.tt("dve", hsl, po.t[:], hsl, ALU.add, [po, ht], [ht])
                S.dma("sync", self.out[tsl, :], ht.t[:], reads=[ht], writes=[self.hbuf[ti]])


_CACHE = {}


def kernel(**inputs):
    stages = inputs.pop("_stages", None)
    debug = inputs.pop("_debug", False)
    ncores = inputs.pop("_ncores", 8)
    key = (None if stages is None else tuple(stages), debug)
    if key not in _CACHE:
        _CACHE[key] = Prog(stages, debug)
    prog = _CACHE[key]
    inp = {k: np.ascontiguousarray(np.asarray(v, dtype=np.float32)) for k, v in inputs.items()}
    colp, rowp = host_layout_params(inp)
    cb, cf = host_constants()
    in_maps = []
    for b in range(ncores):
        m = {"x": inp["x"][b], "mem": inp["mem"][b], "colp": colp, "rowp": rowp, "cb": cb, "cf": cf}
        for n, _ in WEIGHT_SPECS:
            m[n] = inp[n]
        in_maps.append(m)
    if debug == "time":
        res = run_bass_kernel_spmd(prog.nc, in_maps, core_ids=list(range(ncores)), trace=True)
        print("EXEC_TIME_NS", res.exec_time_ns)
    else:
        res = run_bass_kernel_spmd(prog.nc, in_maps, core_ids=list(range(ncores)))
    out = np.stack([np.asarray(r["out"], dtype=np.float32) for r in res.results], axis=0)
    if debug:
        return out, res.results
    return out
```
